# Optimizing a Trainium2 kernel written in Bass

```python
import math
import jax, jax.numpy as jnp
from jax import lax
import numpy as np


D_MODEL = 1024
BATCH = 2
SEQ = 16384
DEPTH = 4
DEC_BATCH = 16
DEC_SEQ = 2048
PAST_LEN = 128

MLA_HEADS = D_MODEL // 128
QK_NOPE_DIM = 64
QK_ROPE_DIM = 32
V_HEAD_DIM = 64
Q_LORA_RANK = D_MODEL // 4
KV_LORA_RANK = D_MODEL // 8
ROPE_THETA = 10000.0
Q_BLOCK = 128
MLA_OUT = MLA_HEADS * V_HEAD_DIM
POOL_GROUPS = 4
POOL_GROUP_DIM = D_MODEL // 16
POOL_WINDOWS = (2, 4, 8, 16)
POOL_WIDTH = POOL_GROUPS * POOL_GROUP_DIM
MLSTM_HEADS = 4
MLSTM_HEAD_DIM = D_MODEL // 16
MLSTM_WIDTH = MLSTM_HEADS * MLSTM_HEAD_DIM
MLSTM_CHUNK = 128
N_GATES = 4 * MLSTM_HEADS
NEG_BIG = -1e30
MIX_WIDTH = MLA_OUT + POOL_WIDTH + MLSTM_WIDTH
IN_WIDTHS = (Q_LORA_RANK, KV_LORA_RANK, QK_ROPE_DIM, POOL_WIDTH,
             MLSTM_WIDTH, MLSTM_WIDTH, MLSTM_WIDTH, MLSTM_WIDTH, N_GATES)
IN_WIDTH = sum(IN_WIDTHS)
IN_SPLITS = tuple(int(v) for v in np.cumsum(IN_WIDTHS)[:-1])
D_FF = -(-8 * D_MODEL // (3 * 256)) * 256
DEEPNORM_ALPHA = (2 * DEPTH) ** 0.25
DEEPNORM_BETA = (8 * DEPTH) ** -0.25
LN_EPS = 1e-5

kernel_name = 'hymba_mla_pool_mlstm_deepnorm_encoder'


def layer_norm(x, g, b):
    xf = x.astype(jnp.float32)
    mu = xf.mean(-1, keepdims=True)
    var = jnp.square(xf - mu).mean(-1, keepdims=True)
    return ((xf - mu) * lax.rsqrt(var + LN_EPS) * g + b).astype(x.dtype)


def rms_norm(x, g):
    xf = x.astype(jnp.float32)
    return (xf * lax.rsqrt(jnp.square(xf).mean(-1, keepdims=True) + LN_EPS) * g).astype(x.dtype)


def rope_tables(seq_len):
    inv = 1.0 / (ROPE_THETA ** (jnp.arange(0, QK_ROPE_DIM, 2, dtype=jnp.float32) / QK_ROPE_DIM))
    ang = jnp.arange(seq_len, dtype=jnp.float32)[:, None] * inv[None, :]
    return jnp.cos(ang), jnp.sin(ang)


def apply_rope(x, cos, sin):
    xf = x.astype(jnp.float32)
    half = QK_ROPE_DIM // 2
    x1, x2 = xf[..., :half], xf[..., half:]
    return jnp.concatenate([x1 * cos - x2 * sin, x2 * cos + x1 * sin], axis=-1).astype(x.dtype)


def mla_mix(c_q, c_kv, k_rope, q_norm_g, w_uq, kv_norm_g, w_ukv):
    B, S, _ = c_q.shape
    q = jnp.einsum('bsr,rn->bsn', rms_norm(c_q, q_norm_g), w_uq).reshape(
        B, S, MLA_HEADS, QK_NOPE_DIM + QK_ROPE_DIM)
    kv = jnp.einsum('bsr,rn->bsn', rms_norm(c_kv, kv_norm_g), w_ukv).reshape(
        B, S, MLA_HEADS, QK_NOPE_DIM + V_HEAD_DIM)
    k_nope, v = kv[..., :QK_NOPE_DIM], kv[..., QK_NOPE_DIM:]
    cos, sin = rope_tables(S)
    q_rope = apply_rope(q[..., QK_NOPE_DIM:], cos[:, None, :], sin[:, None, :])
    k_rope = apply_rope(k_rope, cos, sin)
    q = jnp.concatenate([q[..., :QK_NOPE_DIM], q_rope], axis=-1)
    k = jnp.concatenate(
        [k_nope, jnp.broadcast_to(k_rope[:, :, None, :], (B, S, MLA_HEADS, QK_ROPE_DIM))], axis=-1)
    scale = (QK_NOPE_DIM + QK_ROPE_DIM) ** -0.5
    q_blocks = q.reshape(B, S // Q_BLOCK, Q_BLOCK, MLA_HEADS, -1).transpose(1, 0, 2, 3, 4)

    def attend(q_blk):
        s = jnp.einsum('bqhd,bkhd->bhqk', q_blk, k, preferred_element_type=jnp.float32) * scale
        p = jax.nn.softmax(s, axis=-1).astype(v.dtype)
        return jnp.einsum('bhqk,bkhd->bqhd', p, v)

    o = lax.map(attend, q_blocks)
    return o.transpose(1, 0, 2, 3, 4).reshape(B, S, MLA_OUT)


def pool_mix(xp, w_pool, pool_scale):
    B, S, _ = xp.shape
    xf = xp.astype(jnp.float32)
    cs = jnp.concatenate([jnp.zeros((B, 1, POOL_WIDTH), jnp.float32), jnp.cumsum(xf, axis=1)], axis=1)
    t = jnp.arange(S)
    outs = []
    for g, w in enumerate(POOL_WINDOWS):
        lo = jnp.clip(t - w // 2, 0, S)
        hi = jnp.clip(t + w // 2, 0, S)
        sl = slice(g * POOL_GROUP_DIM, (g + 1) * POOL_GROUP_DIM)
        csg = cs[:, :, sl]
        win_sum = jnp.take(csg, hi, axis=1) - jnp.take(csg, lo, axis=1)
        cnt = (hi - lo).astype(jnp.float32)[None, :, None]
        outs.append(win_sum / cnt - xf[:, :, sl])
    y = jnp.stack(outs, axis=2).astype(xp.dtype)
    y = jnp.einsum('bsgc,gcd->bsgd', y, w_pool).reshape(B, S, POOL_WIDTH)
    return y * pool_scale


def mlstm_chunk_scan(q, k, v, i_pre, f_pre):
    B, H, S, D = q.shape
    L = MLSTM_CHUNK
    NC = S // L

    def to_chunks(a):
        return jnp.moveaxis(a.reshape(B, H, NC, L, *a.shape[3:]), 2, 0)

    logf = jax.nn.log_sigmoid(f_pre)
    lower = jnp.tril(jnp.ones((L, L), dtype=bool))

    def step(carry, inp):
        C, n, m = carry
        qc, kc, vc, ic, fc = inp
        b = jnp.cumsum(fc, axis=-1)
        d_mat = jnp.where(lower, b[..., :, None] - b[..., None, :] + ic[..., None, :], NEG_BIG)
        m_inter = b + m[..., None]
        m_j = jnp.maximum(m_inter, d_mat.max(-1))
        w_intra = jnp.exp(d_mat - m_j[..., None])
        w_inter = jnp.exp(m_inter - m_j)
        scores = jnp.einsum('bhjd,bhsd->bhjs', qc, kc) * w_intra
        num = (w_inter[..., None] * jnp.einsum('bhjk,bhkv->bhjv', qc, C)
               + jnp.einsum('bhjs,bhsv->bhjv', scores, vc))
        den = w_inter * jnp.einsum('bhjk,bhk->bhj', qc, n) + scores.sum(-1)
        h = num / jnp.maximum(jnp.abs(den), jnp.exp(-m_j))[..., None]
        b_last = b[..., -1]
        g = b_last[..., None] - b + ic
        m_new = jnp.maximum(b_last + m, g.max(-1))
        decay = jnp.exp(b_last + m - m_new)
        w_s = jnp.exp(g - m_new[..., None])
        C_new = decay[..., None, None] * C + jnp.einsum('bhs,bhsk,bhsv->bhkv', w_s, kc, vc)
        n_new = decay[..., None] * n + jnp.einsum('bhs,bhsk->bhk', w_s, kc)
        return (C_new, n_new, m_new), h

    init = (jnp.zeros((B, H, D, D), jnp.float32), jnp.zeros((B, H, D), jnp.float32),
            jnp.zeros((B, H), jnp.float32))
    _, h = lax.scan(step, init, (to_chunks(q), to_chunks(k), to_chunks(v),
                                 to_chunks(i_pre), to_chunks(logf)))
    return jnp.moveaxis(h, 0, 2).reshape(B, H, S, D)


def mlstm_mix(q, k, v, o_pre, gates, gate_bias, norm_g):
    B, S, _ = q.shape

    def heads(a):
        return a.astype(jnp.float32).reshape(B, S, MLSTM_HEADS, MLSTM_HEAD_DIM).transpose(0, 2, 1, 3)

    qh, kh, vh = heads(q), heads(k) * MLSTM_HEAD_DIM ** -0.5, heads(v)
    g = (gates.astype(jnp.float32) + gate_bias.astype(jnp.float32)).reshape(
        B, S, 4, MLSTM_HEADS).transpose(2, 0, 3, 1)
    h_fwd = mlstm_chunk_scan(qh, kh, vh, g[0], g[1])
    flip = lambda a: jnp.flip(a, axis=2)
    h_bwd = flip(mlstm_chunk_scan(flip(qh), flip(kh), flip(vh), flip(g[2]), flip(g[3])))
    h = h_fwd + h_bwd
    mu = h.mean(-1, keepdims=True)
    var = jnp.square(h - mu).mean(-1, keepdims=True)
    h = ((h - mu) * lax.rsqrt(var + LN_EPS)).transpose(0, 2, 1, 3).reshape(B, S, MLSTM_WIDTH) * norm_g
    return (jax.nn.sigmoid(o_pre.astype(jnp.float32)) * h).astype(q.dtype)


def encoder_layer(x, w_in, q_norm_g, w_uq, kv_norm_g, w_ukv, w_pool, pool_scale,
                  mlstm_gate_bias, mlstm_norm_g, w_out, ln1_g, ln1_b,
                  w_gate, w_up, w_down, ln2_g, ln2_b):
    proj = jnp.einsum('bsd,dn->bsn', x, w_in)
    c_q, c_kv, k_rope, x_pool, q_m, k_m, v_m, o_m, gates = jnp.split(proj, IN_SPLITS, axis=-1)
    y_mla = mla_mix(c_q, c_kv, k_rope, q_norm_g, w_uq, kv_norm_g, w_ukv)
    y_pool = pool_mix(x_pool, w_pool, pool_scale).astype(x.dtype)
    y_mlstm = mlstm_mix(q_m, k_m, v_m, o_m, gates, mlstm_gate_bias, mlstm_norm_g)
    mix = jnp.einsum('bsm,md->bsd', jnp.concatenate([y_mla, y_pool, y_mlstm], axis=-1), w_out)
    x = layer_norm(DEEPNORM_ALPHA * x + mix, ln1_g, ln1_b)
    hid = jax.nn.silu(jnp.einsum('bsd,df->bsf', x, w_gate)) * jnp.einsum('bsd,df->bsf', x, w_up)
    ffn = jnp.einsum('bsf,fd->bsd', hid, w_down)
    return layer_norm(DEEPNORM_ALPHA * x + ffn, ln2_g, ln2_b)


def setup_inputs(seed: int = 0) -> dict:
    key = jax.random.key(seed)
    ks = jax.random.split(key, 24)

    def nrm(k, shape, scale):
        return jax.random.normal(k, shape, jnp.float32) * scale

    x_prompt = nrm(ks[0], (BATCH, SEQ, D_MODEL), 1.0)
    x_sample = nrm(ks[1], (DEC_BATCH, DEC_SEQ, D_MODEL), 1.0)
    ln_in_g = 1.0 + nrm(ks[2], (D_MODEL,), 0.02)
    ln_in_b = nrm(ks[3], (D_MODEL,), 0.02)
    w_in = nrm(ks[4], (DEPTH, D_MODEL, IN_WIDTH), D_MODEL ** -0.5)
    q_norm_g = 1.0 + nrm(ks[5], (DEPTH, Q_LORA_RANK), 0.02)
    w_uq = nrm(ks[6], (DEPTH, Q_LORA_RANK, MLA_HEADS * (QK_NOPE_DIM + QK_ROPE_DIM)), Q_LORA_RANK ** -0.5)
    kv_norm_g = 1.0 + nrm(ks[7], (DEPTH, KV_LORA_RANK), 0.02)
    w_ukv = nrm(ks[8], (DEPTH, KV_LORA_RANK, MLA_HEADS * (QK_NOPE_DIM + V_HEAD_DIM)), KV_LORA_RANK ** -0.5)
    w_pool = nrm(ks[9], (DEPTH, POOL_GROUPS, POOL_GROUP_DIM, POOL_GROUP_DIM), POOL_GROUP_DIM ** -0.5)
    pool_scale = 1.0 + nrm(ks[10], (DEPTH, POOL_WIDTH), 0.02)
    i_bias = nrm(ks[11], (DEPTH, 2, MLSTM_HEADS), 0.1)
    f_bias = jnp.linspace(3.0, 6.0, MLSTM_HEADS, dtype=jnp.float32)[None, None, :] + nrm(
        ks[12], (DEPTH, 2, MLSTM_HEADS), 0.1)
    mlstm_gate_bias = jnp.stack([i_bias[:, 0], f_bias[:, 0], i_bias[:, 1], f_bias[:, 1]],
                                axis=1).reshape(DEPTH, N_GATES)
    mlstm_norm_g = 1.0 + nrm(ks[13], (DEPTH, MLSTM_WIDTH), 0.02)
    w_out = nrm(ks[14], (DEPTH, MIX_WIDTH, D_MODEL), MIX_WIDTH ** -0.5 * DEEPNORM_BETA)
    ln1_g = 1.0 + nrm(ks[15], (DEPTH, D_MODEL), 0.02)
    ln1_b = nrm(ks[16], (DEPTH, D_MODEL), 0.02)
    w_gate = nrm(ks[17], (DEPTH, D_MODEL, D_FF), D_MODEL ** -0.5)
    w_up = nrm(ks[18], (DEPTH, D_MODEL, D_FF), D_MODEL ** -0.5)
    w_down = nrm(ks[19], (DEPTH, D_FF, D_MODEL), D_FF ** -0.5 * DEEPNORM_BETA)
    ln2_g = 1.0 + nrm(ks[20], (DEPTH, D_MODEL), 0.02)
    ln2_b = nrm(ks[21], (DEPTH, D_MODEL), 0.02)
    return {'x_prompt': x_prompt, 'x_sample': x_sample, 'ln_in_g': ln_in_g, 'ln_in_b': ln_in_b,
            'w_in': w_in, 'q_norm_g': q_norm_g, 'w_uq': w_uq, 'kv_norm_g': kv_norm_g, 'w_ukv': w_ukv,
            'w_pool': w_pool, 'pool_scale': pool_scale, 'mlstm_gate_bias': mlstm_gate_bias,
            'mlstm_norm_g': mlstm_norm_g, 'w_out': w_out, 'ln1_g': ln1_g, 'ln1_b': ln1_b,
            'w_gate': w_gate, 'w_up': w_up, 'w_down': w_down, 'ln2_g': ln2_g, 'ln2_b': ln2_b}


def reference(x_prompt, x_sample, ln_in_g, ln_in_b, w_in, q_norm_g, w_uq, kv_norm_g, w_ukv,
              w_pool, pool_scale, mlstm_gate_bias, mlstm_norm_g, w_out, ln1_g, ln1_b,
              w_gate, w_up, w_down, ln2_g, ln2_b):
    def trunk(x):
        x = layer_norm(x, ln_in_g, ln_in_b)
        for l in range(DEPTH):
            x = encoder_layer(x, w_in[l], q_norm_g[l], w_uq[l], kv_norm_g[l], w_ukv[l],
                              w_pool[l], pool_scale[l], mlstm_gate_bias[l], mlstm_norm_g[l],
                              w_out[l], ln1_g[l], ln1_b[l], w_gate[l], w_up[l], w_down[l],
                              ln2_g[l], ln2_b[l])
        return x

    y_prompt = trunk(x_prompt)
    y_sample = trunk(x_sample)
    return (y_prompt, y_sample)
```

```python
import numpy as np
from contextlib import ExitStack
import concourse.bass as bass
import concourse.mybir as mybir
from concourse.bass_utils import run_bass_kernel_spmd

F32 = mybir.dt.float32
BF16 = mybir.dt.bfloat16
AF = mybir.ActivationFunctionType
ALU = mybir.AluOpType

D = 1024
NQ, NKV, NR, NP_, NM, NG = 256, 128, 32, 256, 256, 16
IN_W = 1712
DFF = 2816
NFC = DFF // 128
EPS = 1e-5
BIG = 1e30
SAME_ENG_SYNC = True


class Buf:
    __slots__ = ("w", "r")

    def __init__(self):
        self.w = None
        self.r = {}


class T:
    def __init__(self, h):
        self.h = h
        self.b = Buf()

    def __getitem__(self, idx):
        return self.h[idx]


class KB:
    ND = 8

    def __init__(self, nc):
        self.nc = nc
        self.E = {"pe": nc.tensor, "act": nc.scalar, "dve": nc.vector, "pool": nc.gpsimd, "sp": nc.sync}
        self.sem = {}
        self.val = {}
        for e in self.E:
            self.sem[e] = nc.alloc_semaphore(name="c_" + e)
            self.val[e] = 0
        for q in ("sp", "pool"):
            for i in range(self.ND):
                self.sem[(q, i)] = nc.alloc_semaphore(name="d_%s%d" % (q, i))
                self.val[(q, i)] = 0
        self.dslot = {"sp": 0, "pool": 0}
        self.waited = {e: {} for e in self.E}

    def _wait(self, e, sk, v):
        if v <= 0:
            return
        if self.waited[e].get(sk, 0) < v:
            self.E[e].wait_ge(self.sem[sk], v)
            self.waited[e][sk] = v

    def _deps(self, e, r, w):
        deps = {}
        for t in list(r) + list(w):
            ev = t.b.w
            if ev is not None:
                deps[ev[0]] = max(deps.get(ev[0], 0), ev[1])
        for t in w:
            for sk, v in t.b.r.items():
                deps[sk] = max(deps.get(sk, 0), v)
        for sk, v in deps.items():
            if sk == e and (e == "pe" or not SAME_ENG_SYNC):
                continue
            self._wait(e, sk, v)

    def _mark(self, ev, r, w):
        for t in r:
            t.b.r[ev[0]] = max(t.b.r.get(ev[0], 0), ev[1])
        for t in w:
            t.b.w = ev
            t.b.r = {}

    def op(self, e, fn, r=(), w=()):
        self._deps(e, r, w)
        ins = fn(self.E[e])
        self.val[e] += 1
        ins.then_inc(self.sem[e], 1)
        self._mark((e, self.val[e]), r, w)

    def dma(self, out, in_, r=(), w=(), q="sp"):
        self._deps(q, r, w)
        slot = self.dslot[q]
        self.dslot[q] = (slot + 1) % self.ND
        sk = (q, slot)
        self._wait(q, sk, self.val[sk])
        ins = self.E[q].dma_start(out=out, in_=in_)
        self.val[sk] += 16
        ins.then_inc(self.sem[sk], 16)
        self._mark((sk, self.val[sk]), r, w)

    def barrier(self):
        for e in self.E:
            for sk in self.sem:
                if sk != e:
                    self._wait(e, sk, self.val[sk])


def build_program(L, jobs, smax, dbg=False):
    nc = bass.Bass("TRN2", target_bir_lowering=False)
    kb = KB(nc)
    NJ = len(jobs)

    def din(name, shape, dt=F32):
        return nc.dram_tensor(name, list(shape), dt, kind="ExternalInput").ap()

    def dscr(name, shape, dt):
        return T(nc.dram_tensor(name, list(shape), dt).ap())

    xin = [din("xin%d" % j, [S, D]) for j, S in enumerate(jobs)]
    yout = [T(nc.dram_tensor("yout%d" % j, [S, D], F32, kind="ExternalOutput").ap()) for j, S in enumerate(jobs)]
    w_in = din("w_in", [L, D, IN_W])
    w_uq = din("w_uq", [L, NQ, 768])
    w_ukv = din("w_ukv", [L, NKV, 1024])
    w_pool = din("w_pool", [L, 4, 64, 64])
    w_out = din("w_out", [L, D, D])
    w_gate = din("w_gate", [L, D, DFF])
    w_up = din("w_up", [L, D, DFF])
    w_down = din("w_down", [L, DFF, D])
    rep_in = din("rep_in", [128, 2 * D])
    rep_l = din("rep_l", [L, 128, 4 * D + 256])
    qg = din("qg", [L, 128, 2])
    kvg = din("kvg", [L, 128, 1])
    psc = din("psc", [L, 64, 4])
    gbias = din("gbias", [L, 4, 4])
    cos2 = din("cos2", [32, smax])
    sin2 = din("sin2", [32, smax])
    pedge = din("pedge", [NJ, 128, 2, 16])
    maskf_d = din("maskf", [128, 128])
    maskb_d = din("maskb", [128, 128])
    sel_d = din("sel", [4, 4, 128])
    ident_d = din("ident", [128, 128])

    SC = []
    for j, S in enumerate(jobs):
        s = {}
        s["XN"] = [dscr("XN%d_%d" % (i, j), [S, D], F32) for i in range(2)]
        s["XT"] = dscr("XT%d" % j, [D, S], BF16)
        s["CQT"] = dscr("CQT%d" % j, [NQ, S], BF16)
        s["CKVT"] = dscr("CKVT%d" % j, [NKV, S], BF16)
        s["KRT"] = dscr("KRT%d" % j, [32, S], BF16)
        s["POOLT"] = dscr("POOLT%d" % j, [256, S], F32)
        s["QMT"] = dscr("QMT%d" % j, [256, S], BF16)
        s["KMT"] = dscr("KMT%d" % j, [256, S], BF16)
        s["G4"] = dscr("G4%d" % j, [4, 4, S], F32)
        s["VM"] = dscr("VM%d" % j, [S, 256], BF16)
        s["KM"] = dscr("KM%d" % j, [S, 256], BF16)
        s["OM"] = dscr("OM%d" % j, [S, 256], F32)
        s["MIXT"] = dscr("MIXT%d" % j, [D, S], BF16)
        s["HF"] = dscr("HF%d" % j, [S, 256], F32)
        s["HIDT"] = dscr("HIDT%d" % j, [DFF, S], BF16)
        SC.append(s)

    es_glob = ExitStack()

    uniq = [0]

    def sb(es, name, shape, dt):
        uniq[0] += 1
        return T(es.enter_context(nc.sbuf_tensor("%s_%d" % (name, uniq[0]), list(shape), dt)))

    PS = [T(es_glob.enter_context(nc.psum_tensor("ps%d" % i, [128, 512], F32))) for i in range(7)]
    PSB = T(es_glob.enter_context(nc.psum_tensor("psb", [128, 1024], BF16)))

    identf = sb(es_glob, "identf", [128, 128], F32)
    identb = sb(es_glob, "identb", [128, 128], BF16)
    onesf = sb(es_glob, "onesf", [128, 128], F32)
    onesb = sb(es_glob, "onesb", [128, 128], BF16)
    kb.dma(identf[:], ident_d[:, :], w=[identf])
    kb.op("dve", lambda e: e.tensor_copy(out=identb[:], in_=identf[:]), r=[identf], w=[identb])
    kb.op("dve", lambda e: e.memset(onesf[:], 1.0), w=[onesf])
    kb.op("dve", lambda e: e.memset(onesb[:], 1.0), w=[onesb])

    stg_ctr = [0]

    def load_w_bf16(es_stage, dst, dst_ap_fn, src_ap, nk, ncols, stg):
        CW = 2048
        for c in range(nk):
            for n0 in range(0, ncols, CW):
                n1 = min(ncols, n0 + CW)
                st = stg[stg_ctr[0] % len(stg)]
                stg_ctr[0] += 1
                kb.dma(st[:, 0:n1 - n0], src_ap[c * 128:(c + 1) * 128, n0:n1], w=[st])
                eng = "pool" if (stg_ctr[0] % 2) else "dve"
                kb.op(eng, lambda e, st=st, c=c, n0=n0, n1=n1: e.tensor_copy(out=dst_ap_fn(c, n0, n1), in_=st[:, 0:n1 - n0]),
                      r=[st], w=[dst])

    def rsqrt_eps(t, out_ap, in_ap, rd, scale=1.0):
        kb.op("dve", lambda e: e.tensor_scalar(out=out_ap, in0=in_ap, scalar1=scale, scalar2=EPS, op0=ALU.mult, op1=ALU.add),
              r=rd, w=[t])
        kb.op("act", lambda e: e.activation(out=out_ap, in_=out_ap, func=AF.Sqrt), r=[], w=[t])
        kb.op("dve", lambda e: e.reciprocal(out=out_ap, in_=out_ap), r=[], w=[t])

    def layer_norm_tile(z, gt, bt, g_off, b_off, outf, outb, tmp):
        st6, mv, rs = tmp["st6"], tmp["mv"], tmp["rs"]
        for hh in range(2):
            kb.op("dve", lambda e, hh=hh: e.bn_stats(out=st6[:, hh, :], in_=z[:, hh * 512:(hh + 1) * 512]), r=[z], w=[st6])
        kb.op("dve", lambda e: e.bn_aggr(out=mv[:, :], in_=st6[:, :, :]), r=[st6], w=[mv])
        rsqrt_eps(rs, rs[:, :], mv[:, 1:2], [mv])
        kb.op("dve", lambda e: e.tensor_scalar(out=outf[:, :], in0=z[:, :], scalar1=mv[:, 0:1], scalar2=rs[:, 0:1],
                                               op0=ALU.subtract, op1=ALU.mult), r=[z, mv, rs], w=[outf])
        kb.op("pool", lambda e: e.tensor_tensor(out=outf[:, :], in0=outf[:, :], in1=gt[:, g_off:g_off + D], op=ALU.mult),
              r=[gt], w=[outf])
        kb.op("dve", lambda e: e.tensor_tensor(out=outf[:, :], in0=outf[:, :], in1=bt[:, b_off:b_off + D], op=ALU.add),
              r=[bt], w=[outf])
        kb.op("act", lambda e: e.activation(out=outb[:, :], in_=outf[:, :], func=AF.Copy), r=[outf], w=[outb])

    def transpose_to_xt(outb, xts, XT_T, tok0):
        for c in range(8):
            kb.op("pe", lambda e, c=c: e.transpose(out=PSB[:, c * 128:(c + 1) * 128], in_=outb[:, c * 128:(c + 1) * 128],
                                                   identity=identb[:]), r=[outb, identb], w=[PSB])
        kb.op("act", lambda e: e.activation(out=xts[:, :], in_=PSB[:, :], func=AF.Copy), r=[PSB], w=[xts])
        kb.dma(XT_T.h.rearrange("(c p) s -> p c s", p=128)[:, :, tok0:tok0 + 128],
               xts.h.rearrange("p (c t) -> p c t", c=8), r=[xts], w=[XT_T])

    with ExitStack() as es:
        rin = sb(es, "rin", [128, 2 * D], F32)
        kb.dma(rin[:], rep_in[:, :], w=[rin])
        tmp = {"st6": sb(es, "st6", [128, 2, 6], F32), "mv": sb(es, "mv", [128, 2], F32), "rs": sb(es, "rs", [128, 1], F32)}
        zs = [sb(es, "pz%d" % i, [128, D], F32) for i in range(2)]
        ofs = [sb(es, "pof%d" % i, [128, D], F32) for i in range(2)]
        obs = [sb(es, "pob%d" % i, [128, D], BF16) for i in range(2)]
        xtss = [sb(es, "pxt%d" % i, [128, D], BF16) for i in range(2)]
        it = 0
        for j, S in enumerate(jobs):
            for t in range(S // 128):
                z, of, ob, xts = zs[it % 2], ofs[it % 2], obs[it % 2], xtss[it % 2]
                it += 1
                kb.dma(z[:], xin[j][t * 128:(t + 1) * 128, :], w=[z])
                layer_norm_tile(z, rin, rin, 0, D, of, ob, tmp)
                kb.dma(SC[j]["XN"][0].h[t * 128:(t + 1) * 128, :], of[:], r=[of], w=[SC[j]["XN"][0]])
                transpose_to_xt(ob, xts, SC[j]["XT"], t * 128)
        kb.barrier()

    for l in range(L):
        last = l == L - 1
        with ExitStack() as es:
            stg = [sb(es, "stg%d" % i, [128, 2048], F32) for i in range(2)]
            win = sb(es, "win", [128, 8, IN_W], BF16)
            load_w_bf16(es, win, lambda c, n0, n1: win[:, c, n0:n1], w_in[l], 8, IN_W, stg)
            wkr_sw = sb(es, "wkrsw", [128, 8, 96], BF16)
            kb.op("dve", lambda e: e.tensor_copy(out=wkr_sw[:, :, 0:64], in_=win[:, :, 320:384]), r=[win], w=[wkr_sw])
            kb.op("dve", lambda e: e.tensor_copy(out=wkr_sw[:, :, 64:80], in_=win[:, :, 400:416]), r=[win], w=[wkr_sw])
            kb.op("dve", lambda e: e.tensor_copy(out=wkr_sw[:, :, 80:96], in_=win[:, :, 384:400]), r=[win], w=[wkr_sw])
            qgt = sb(es, "qgt", [128, 2], F32)
            kvgt = sb(es, "kvgt", [128, 1], F32)
            kb.dma(qgt[:], qg[l], w=[qgt])
            kb.dma(kvgt[:], kvg[l], w=[kvgt])
            xTs = [sb(es, "xT%d" % i, [128, 8, 512], BF16) for i in range(2)]
            cst = [sb(es, "cs%d" % i, [96, 512], F32) for i in range(2)]
            snt = [sb(es, "sn%d" % i, [96, 512], F32) for i in range(2)]
            sq = sb(es, "sq", [128, 512], BF16)
            rstd = sb(es, "rstd", [128, 512], F32)
            ev = [sb(es, "ev%d" % i, [128, 512], BF16) for i in range(2)]
            evf = [sb(es, "evf%d" % i, [128, 512], F32) for i in range(2)]
            kr1 = sb(es, "kr1", [96, 512], F32)
            kr2 = sb(es, "kr2", [96, 512], F32)
            krb = sb(es, "krb", [96, 512], BF16)
            tmb = [sb(es, "tmb%d" % i, [128, 512], BF16) for i in range(2)]
            tmf = [sb(es, "tmf%d" % i, [128, 256], F32) for i in range(2)]
            evc = 0
            for j, S in enumerate(jobs):
                sc = SC[j]
                for blk in range(S // 512):
                    t0 = blk * 512
                    xT = xTs[blk % 2]
                    kb.dma(xT[:], sc["XT"].h.rearrange("(c p) s -> p c s", p=128)[:, :, t0:t0 + 512], r=[sc["XT"]], w=[xT])
                    cs, sn = cst[blk % 2], snt[blk % 2]
                    kb.dma(cs[64:96, :], cos2[:, t0:t0 + 512], w=[cs])
                    kb.dma(sn[64:96, :], sin2[:, t0:t0 + 512], w=[sn])

                    def fm(col0, m, bank, lhs=None):
                        for c in range(8):
                            lt = (win[:, c, col0:col0 + m] if lhs is None else lhs[:, c, 0:m])
                            kb.op("pe", lambda e, c=c, lt=lt: e.matmul(PS[bank][0:m, :], lhsT=lt, rhs=xT[:, c, :],
                                                                        start=(c == 0), stop=(c == 7)),
                                  r=[win if lhs is None else lhs, xT], w=[PS[bank]])

                    for (col0, nchunk, gtile, dst, key) in ((0, 2, qgt, "CQT", "q"), (256, 1, kvgt, "CKVT", "kv")):
                        for cc in range(nchunk):
                            fm(col0 + cc * 128, 128, cc)
                        for cc in range(nchunk):
                            kb.op("act", lambda e, cc=cc: e.activation(out=sq[:, :], in_=PS[cc][:, :], func=AF.Square),
                                  r=[PS[cc]], w=[sq])
                            kb.op("pe", lambda e, cc=cc: e.matmul(PS[2][:, :], lhsT=onesb[:, :], rhs=sq[:, :],
                                                                  start=(cc == 0), stop=(cc == nchunk - 1)),
                                  r=[onesb, sq], w=[PS[2]])
                        nfeat = 128.0 * nchunk
                        rsqrt_eps(rstd, rstd[:, :], PS[2][:, :], [PS[2]], scale=1.0 / nfeat)
                        for cc in range(nchunk):
                            o = ev[evc % 2]
                            evc += 1
                            kb.op("dve", lambda e, cc=cc, o=o, gtile=gtile: e.scalar_tensor_tensor(
                                out=o[:, :], in0=PS[cc][:, :], scalar=gtile[:, cc:cc + 1], in1=rstd[:, :],
                                op0=ALU.mult, op1=ALU.mult), r=[PS[cc], gtile, rstd], w=[o])
                            kb.dma(sc[dst].h[cc * 128:(cc + 1) * 128, t0:t0 + 512], o[:, :], r=[o], w=[sc[dst]])
                    fm(320, 96, 3)
                    fm(0, 96, 4, lhs=wkr_sw)
                    kb.op("dve", lambda e: e.tensor_tensor(out=kr1[64:96, :], in0=PS[3][64:96, :], in1=cs[64:96, :], op=ALU.mult),
                          r=[PS[3], cs], w=[kr1])
                    kb.op("dve", lambda e: e.tensor_tensor(out=kr2[64:96, :], in0=PS[4][64:96, :], in1=sn[64:96, :], op=ALU.mult),
                          r=[PS[4], sn], w=[kr2])
                    kb.op("pool", lambda e: e.tensor_tensor(out=krb[64:96, :], in0=kr1[64:96, :], in1=kr2[64:96, :], op=ALU.add),
                          r=[kr1, kr2], w=[krb])
                    kb.dma(sc["KRT"].h[:, t0:t0 + 512], krb[64:96, :], r=[krb], w=[sc["KRT"]])
                    bi = 0
                    for (col0, dst, kind) in ((416, "POOLT", "f"), (544, "POOLT", "f"), (672, "QMT", "b"), (800, "QMT", "b"),
                                              (928, "KMT", "k"), (1056, "KMT", "k")):
                        bank = 5 + (bi % 2)
                        bi += 1
                        fm(col0, 128, bank)
                        r0 = ((col0 - 416) % 256) if dst == "POOLT" else ((col0 - 672) % 256)
                        if kind == "f":
                            o = evf[evc % 2]
                            evc += 1
                            kb.op("act", lambda e, o=o, bank=bank: e.activation(out=o[:, :], in_=PS[bank][:, :], func=AF.Copy),
                                  r=[PS[bank]], w=[o])
                        else:
                            o = ev[evc % 2]
                            evc += 1
                            scl = 0.125 if kind == "k" else 1.0
                            kb.op("act", lambda e, o=o, bank=bank, scl=scl: e.activation(out=o[:, :], in_=PS[bank][:, :],
                                                                                           func=AF.Copy, scale=scl),
                                  r=[PS[bank]], w=[o])
                        kb.dma(sc[dst].h[r0:r0 + 128, t0:t0 + 512], o[:, :], r=[o], w=[sc[dst]])
                    for ty in range(4):
                        fm(1696 + 4 * ty, 4, 3 + (ty % 2))
                        o = evf[evc % 2]
                        evc += 1
                        bank = 3 + (ty % 2)
                        kb.op("act", lambda e, o=o, bank=bank: e.activation(out=o[0:4, :], in_=PS[bank][0:4, :], func=AF.Copy),
                              r=[PS[bank]], w=[o])
                        kb.dma(sc["G4"].h[ty, :, t0:t0 + 512], o[0:4, :], r=[o], w=[sc["G4"]])
                    for tt in range(4):
                        tk0 = t0 + tt * 128
                        for c in range(8):
                            kb.op("pe", lambda e, c=c, tt=tt: e.matmul(PS[0][:, :], lhsT=xT[:, c, tt * 128:(tt + 1) * 128],
                                                                        rhs=win[:, c, 928:1440], start=(c == 0), stop=(c == 7)),
                                  r=[xT, win], w=[PS[0]])
                        for c in range(8):
                            kb.op("pe", lambda e, c=c, tt=tt: e.matmul(PS[1][:, 0:256], lhsT=xT[:, c, tt * 128:(tt + 1) * 128],
                                                                        rhs=win[:, c, 1440:1696], start=(c == 0), stop=(c == 7)),
                                  r=[xT, win], w=[PS[1]])
                        ob = tmb[tt % 2]
                        of = tmf[tt % 2]
                        kb.op("act", lambda e, ob=ob: e.activation(out=ob[:, 0:256], in_=PS[0][:, 0:256], func=AF.Copy, scale=0.125),
                              r=[PS[0]], w=[ob])
                        kb.op("dve", lambda e, ob=ob: e.tensor_copy(out=ob[:, 256:512], in_=PS[0][:, 256:512]), r=[PS[0]], w=[ob])
                        kb.op("act", lambda e, of=of: e.activation(out=of[:, :], in_=PS[1][:, 0:256], func=AF.Copy), r=[PS[1]], w=[of])
                        kb.dma(sc["KM"].h[tk0:tk0 + 128, :], ob[:, 0:256], r=[ob], w=[sc["KM"]])
                        kb.dma(sc["VM"].h[tk0:tk0 + 128, :], ob[:, 256:512], r=[ob], w=[sc["VM"]])
                        kb.dma(sc["OM"].h[tk0:tk0 + 128, :], of[:, :], r=[of], w=[sc["OM"]])
            kb.barrier()

        for j, S in enumerate(jobs):
            sc = SC[j]
            NKC = S // 128
            NQB = S // 512
            with ExitStack() as es:
                stg = [sb(es, "stg%d" % i, [128, 2048], F32) for i in range(2)]
                wq = sb(es, "wq", [128, 2, 768], BF16)
                wqs = sb(es, "wqs", [128, 2, 768], BF16)
                wkv = sb(es, "wkv", [128, 1, 1024], BF16)
                load_w_bf16(es, wq, lambda c, n0, n1: wq[:, c, n0:n1], w_uq[l], 2, 768, stg)
                load_w_bf16(es, wkv, lambda c, n0, n1: wkv[:, c, n0:n1], w_ukv[l], 1, 1024, stg)
                wq4 = wq.h.rearrange("p c (h d) -> p c h d", h=8)
                wqs4 = wqs.h.rearrange("p c (h d) -> p c h d", h=8)
                kb.op("dve", lambda e: e.tensor_copy(out=wqs[:, :, :], in_=wq[:, :, :]), r=[wq], w=[wqs])
                kb.op("dve", lambda e: e.tensor_copy(out=wqs4[:, :, :, 64:80], in_=wq4[:, :, :, 80:96]), r=[wq], w=[wqs])
                kb.op("dve", lambda e: e.tensor_copy(out=wqs4[:, :, :, 80:96], in_=wq4[:, :, :, 64:80]), r=[wq], w=[wqs])
                ckv = sb(es, "ckv", [128, S], BF16)
                KT = sb(es, "KT", [96, S], BF16)
                VA = sb(es, "VA", [128, NKC, 65], BF16)
                kb.dma(ckv[:], sc["CKVT"].h[:, :], r=[sc["CKVT"]], w=[ckv])
                kb.dma(KT[64:96, :], sc["KRT"].h[:, :], r=[sc["KRT"]], w=[KT])
                kb.op("dve", lambda e: e.memset(VA[:, :, 64:65], 1.0), w=[VA])
                cqs = [sb(es, "cq%d" % i, [128, 2, 512], BF16) for i in range(2)]
                cst = [sb(es, "acs%d" % i, [96, 512], F32) for i in range(2)]
                snt = [sb(es, "asn%d" % i, [96, 512], F32) for i in range(2)]
                QTs = [sb(es, "QT%d" % i, [96, 512], BF16) for i in range(2)]
                q1 = sb(es, "q1", [96, 512], F32)
                q2 = sb(es, "q2", [96, 512], F32)
                PTs = [sb(es, "PT%d" % i, [128, 2, 512], BF16) for i in range(2)]
                den = sb(es, "den", [65, 512], F32)
                bcs = sb(es, "bcs", [64, 512], F32)
                ots = [sb(es, "ot%d" % i, [64, 512], BF16) for i in range(2)]
                STB = [(PS[0], PS[1]), (PS[2], PS[3])]
                scale = 96.0 ** -0.5
                qbc = 0
                for h in range(8):
                    for blk in range(NQB):
                        kb.op("pe", lambda e, blk=blk: e.matmul(PS[6][0:64, :], lhsT=wkv[:, 0, h * 128:h * 128 + 64],
                                                                 rhs=ckv[:, blk * 512:(blk + 1) * 512], start=True, stop=True),
                              r=[wkv, ckv], w=[PS[6]])
                        kb.op("dve", lambda e, blk=blk: e.tensor_copy(out=KT[0:64, blk * 512:(blk + 1) * 512], in_=PS[6][0:64, :]),
                              r=[PS[6]], w=[KT])
                    for g in range(NKC // 8):
                        for i in range(8):
                            kc = g * 8 + i
                            kb.op("pe", lambda e, kc=kc, i=i: e.matmul(PS[6][:, i * 64:(i + 1) * 64], lhsT=ckv[:, kc * 128:(kc + 1) * 128],
                                                                        rhs=wkv[:, 0, h * 128 + 64:h * 128 + 128], start=True, stop=True),
                                  r=[wkv, ckv], w=[PS[6]])
                        kb.op("act", lambda e, g=g: e.activation(out=VA[:, g * 8:(g + 1) * 8, 0:64],
                                                                 in_=PS[6].h.rearrange("p (i d) -> p i d", i=8), func=AF.Copy),
                              r=[PS[6]], w=[VA])
                    for qb in range(NQB):
                        q0 = qb * 512
                        cq, cs, sn, QT = cqs[qbc % 2], cst[qbc % 2], snt[qbc % 2], QTs[qbc % 2]
                        OT = PS[4 + (qbc % 2)]
                        ot = ots[qbc % 2]
                        qbc += 1
                        kb.dma(cq[:], sc["CQT"].h.rearrange("(c p) s -> p c s", p=128)[:, :, q0:q0 + 512], r=[sc["CQT"]], w=[cq])
                        kb.dma(cs[64:96, :], cos2[:, q0:q0 + 512], w=[cs])
                        kb.dma(sn[64:96, :], sin2[:, q0:q0 + 512], w=[sn])
                        for c in range(2):
                            kb.op("pe", lambda e, c=c: e.matmul(PS[6][0:96, :], lhsT=wq[:, c, h * 96:(h + 1) * 96], rhs=cq[:, c, :],
                                                                start=(c == 0), stop=(c == 1)), r=[wq, cq], w=[PS[6]])
                        kb.op("act", lambda e: e.activation(out=QT[0:64, :], in_=PS[6][0:64, :], func=AF.Copy), r=[PS[6]], w=[QT])
                        kb.op("dve", lambda e: e.tensor_tensor(out=q1[64:96, :], in0=PS[6][64:96, :], in1=cs[64:96, :], op=ALU.mult),
                              r=[PS[6], cs], w=[q1])
                        for c in range(2):
                            kb.op("pe", lambda e, c=c: e.matmul(PS[6][0:96, :], lhsT=wqs[:, c, h * 96:(h + 1) * 96], rhs=cq[:, c, :],
                                                                start=(c == 0), stop=(c == 1)), r=[wqs, cq], w=[PS[6]])
                        kb.op("dve", lambda e: e.tensor_tensor(out=q2[64:96, :], in0=PS[6][64:96, :], in1=sn[64:96, :], op=ALU.mult),
                              r=[PS[6], sn], w=[q2])
                        kb.op("pool", lambda e: e.tensor_tensor(out=QT[64:96, :], in0=q1[64:96, :], in1=q2[64:96, :], op=ALU.add),
                              r=[q1, q2], w=[QT])
                        npair = NKC // 2

                        def mm1(p):
                            st = STB[p % 2]
                            for i in range(2):
                                kc = 2 * p + i
                                kb.op("pe", lambda e, kc=kc, i=i: e.matmul(st[i][:, :], lhsT=KT[0:96, kc * 128:(kc + 1) * 128],
                                                                            rhs=QT[0:96, :], start=True, stop=True),
                                      r=[KT, QT], w=[st[i]])

                        def ex(p):
                            st = STB[p % 2]
                            pt = PTs[p % 2]
                            for i in range(2):
                                kb.op("act", lambda e, i=i: e.activation(out=pt[:, i, :], in_=st[i][:, :], func=AF.Exp, scale=scale),
                                      r=[st[i]], w=[pt])

                        def mm2(p):
                            pt = PTs[p % 2]
                            for i in range(2):
                                kc = 2 * p + i
                                kb.op("pe", lambda e, kc=kc, i=i: e.matmul(OT[0:65, :], lhsT=VA[:, kc, :], rhs=pt[:, i, :],
                                                                            start=(kc == 0), stop=(kc == NKC - 1)),
                                      r=[VA, pt], w=[OT])

                        mm1(0)
                        for p in range(npair):
                            ex(p)
                            if p + 1 < npair:
                                mm1(p + 1)
                            mm2(p)
                        kb.op("dve", lambda e: e.reciprocal(out=den[64:65, :], in_=OT[64:65, :]), r=[OT], w=[den])
                        kb.op("pe", lambda e: e.matmul(PS[6][0:64, :], lhsT=onesf[64:65, 0:64], rhs=den[64:65, :], start=True, stop=True),
                              r=[onesf, den], w=[PS[6]])
                        kb.op("act", lambda e: e.activation(out=bcs[:, :], in_=PS[6][0:64, :], func=AF.Copy), r=[PS[6]], w=[bcs])
                        kb.op("dve", lambda e: e.tensor_tensor(out=ot[:, :], in0=OT[0:64, :], in1=bcs[:, :], op=ALU.mult),
                              r=[OT, bcs], w=[ot])
                        kb.dma(sc["MIXT"].h[h * 64:(h + 1) * 64, q0:q0 + 512], ot[:, :], r=[ot], w=[sc["MIXT"]])
                kb.barrier()

        with ExitStack() as es:
            stg = [sb(es, "stg%d" % i, [128, 2048], F32) for i in range(2)]
            wp = sb(es, "wp", [128, 2, 64], BF16)
            for g in range(4):
                st = stg[g % 2]
                p0 = (g % 2) * 64
                kb.dma(st[p0:p0 + 64, 0:64], w_pool[l, g], w=[st])
                kb.op("dve", lambda e, st=st, p0=p0, g=g: e.tensor_copy(out=wp[p0:p0 + 64, g // 2, :], in_=st[p0:p0 + 64, 0:64]),
                      r=[st], w=[wp])
            psct = sb(es, "psct", [64, 4], F32)
            kb.dma(psct[:], psc[l], w=[psct])
            PB = 2048
            xps = [sb(es, "xp%d" % i, [128, 2, PB + 16], F32) for i in range(2)]
            a2 = sb(es, "a2", [128, PB + 16], F32)
            a4 = sb(es, "a4", [128, PB + 16], F32)
            yb = [sb(es, "yb%d" % i, [128, PB], BF16) for i in range(2)]
            yf = sb(es, "yf", [128, PB], F32)
            ped = sb(es, "ped", [128, 2, 16], F32)
            po = [sb(es, "po%d" % i, [64, 512], BF16) for i in range(2)]
            WINS = (2, 4, 8, 16)
            bc = 0
            for j, S in enumerate(jobs):
                sc = SC[j]
                kb.dma(ped[:], pedge[j], w=[ped])
                pb = min(PB, S)
                for blk in range(S // pb):
                    t0 = blk * pb
                    xp = xps[bc % 2]
                    bc += 1
                    lo = max(0, t0 - 8)
                    hi = min(S, t0 + pb + 8)
                    if lo > t0 - 8:
                        kb.op("pool", lambda e: e.memset(xp[:, :, 0:8], 0.0), w=[xp])
                    if hi < t0 + pb + 8:
                        kb.op("pool", lambda e: e.memset(xp[:, :, pb + 8:pb + 16], 0.0), w=[xp])
                    kb.dma(xp[:, :, 8 + (lo - t0):8 + (hi - t0)],
                           sc["POOLT"].h.rearrange("(c p) s -> p c s", p=128)[:, :, lo:hi], r=[sc["POOLT"]], w=[xp])
                    for c in range(2):
                        n = pb + 16
                        x = xp.h[:, c, :]
                        kb.op("dve", lambda e, x=x: e.tensor_tensor(out=a2[:, 0:n - 1], in0=x[:, 0:n - 1], in1=x[:, 1:n], op=ALU.add),
                              r=[xp], w=[a2])
                        kb.op("pool", lambda e: e.tensor_tensor(out=a4[:, 0:n - 3], in0=a2[:, 0:n - 3], in1=a2[:, 2:n - 1], op=ALU.add),
                              r=[a2], w=[a4])
                        if c == 1:
                            kb.op("dve", lambda e: e.tensor_tensor(out=a2[:, 0:n - 7], in0=a4[:, 0:n - 7], in1=a4[:, 4:n - 3], op=ALU.add),
                                  r=[a4], w=[a2])
                            kb.op("pool", lambda e: e.tensor_tensor(out=a4[64:128, 0:n - 15], in0=a2[64:128, 0:n - 15],
                                                                    in1=a2[64:128, 8:n - 7], op=ALU.add), r=[a2], w=[a4])
                        for half in range(2):
                            w = WINS[2 * c + half]
                            src = a2 if half == 0 else a4
                            p0 = half * 64
                            o0 = 8 - w // 2
                            kb.op("dve", lambda e, src=src, p0=p0, w=w, o0=o0, x=x: e.scalar_tensor_tensor(
                                out=yf[p0:p0 + 64, 0:pb], in0=src[p0:p0 + 64, o0:o0 + pb], scalar=1.0 / w,
                                in1=x[p0:p0 + 64, 8:8 + pb], op0=ALU.mult, op1=ALU.subtract), r=[src, xp], w=[yf])
                            if t0 == 0:
                                kb.op("dve", lambda e, src=src, p0=p0, o0=o0, x=x, c=c: e.tensor_tensor(
                                    out=yf[p0:p0 + 64, 0:8], in0=src[p0:p0 + 64, o0:o0 + 8], in1=ped[p0:p0 + 64, c, 0:8], op=ALU.mult),
                                    r=[src, ped], w=[yf])
                                kb.op("dve", lambda e, p0=p0, x=x: e.tensor_tensor(
                                    out=yf[p0:p0 + 64, 0:8], in0=yf[p0:p0 + 64, 0:8], in1=x[p0:p0 + 64, 8:16], op=ALU.subtract),
                                    r=[xp], w=[yf])
                            if t0 + pb == S:
                                kb.op("dve", lambda e, src=src, p0=p0, o0=o0, x=x, c=c: e.tensor_tensor(
                                    out=yf[p0:p0 + 64, pb - 8:pb], in0=src[p0:p0 + 64, o0 + pb - 8:o0 + pb], in1=ped[p0:p0 + 64, c, 8:16],
                                    op=ALU.mult), r=[src, ped], w=[yf])
                                kb.op("dve", lambda e, p0=p0, x=x: e.tensor_tensor(
                                    out=yf[p0:p0 + 64, pb - 8:pb], in0=yf[p0:p0 + 64, pb - 8:pb], in1=x[p0:p0 + 64, pb:pb + 8],
                                    op=ALU.subtract), r=[xp], w=[yf])
                        ybt = yb[c]
                        kb.op("act", lambda e, ybt=ybt: e.activation(out=ybt[:, 0:pb], in_=yf[:, 0:pb], func=AF.Copy), r=[yf], w=[ybt])
                        for half in range(2):
                            g = 2 * c + half
                            p0 = half * 64
                            for sb_ in range(pb // 512):
                                bank = 5 + (sb_ % 2)
                                kb.op("pe", lambda e, p0=p0, c=c, sb_=sb_, ybt=ybt, bank=bank: e.matmul(
                                    PS[bank][0:64, :], lhsT=wp[p0:p0 + 64, c, :], rhs=ybt[p0:p0 + 64, sb_ * 512:(sb_ + 1) * 512],
                                    start=True, stop=True), r=[wp, ybt], w=[PS[bank]])
                                o = po[sb_ % 2]
                                kb.op("act", lambda e, o=o, bank=bank, g=g: e.activation(out=o[:, :], in_=PS[bank][0:64, :], func=AF.Copy,
                                                                                         scale=psct[:, g:g + 1]),
                                      r=[PS[bank], psct], w=[o])
                                kb.dma(sc["MIXT"].h[512 + g * 64:512 + (g + 1) * 64, t0 + sb_ * 512:t0 + (sb_ + 1) * 512], o[:, :],
                                       r=[o], w=[sc["MIXT"]])
            kb.barrier()

        for j, S in enumerate(jobs):
            sc = SC[j]
            SCH = min(S, 2048)
            NSC = S // SCH
            NCH = SCH // 128
            NCT = S // 128
            with ExitStack() as es:
                gb = sb(es, "gb", [4, 4], F32)
                ngb = sb(es, "ngb", [4, 4], F32)
                kb.dma(gb[:], gbias[l], w=[gb])
                kb.op("dve", lambda e: e.tensor_scalar(out=ngb[:, :], in0=gb[:, :], scalar1=-1.0, scalar2=None, op0=ALU.mult),
                      r=[gb], w=[ngb])
                selt = sb(es, "selt", [4, 4, 128], F32)
                kb.dma(selt[:], sel_d[:, :, :], w=[selt])
                masks = [sb(es, "mkf", [128, 128], F32), sb(es, "mkb", [128, 128], F32)]
                kb.dma(masks[0][:], maskf_d[:, :], w=[masks[0]])
                kb.dma(masks[1][:], maskb_d[:, :], w=[masks[1]])
                repn = sb(es, "repn", [128, 256], F32)
                kb.dma(repn[:], rep_l[l, :, 4 * D:4 * D + 256], w=[repn])
                ones4 = sb(es, "ones4", [4, SCH], F32)
                kb.op("pool", lambda e: e.memset(ones4[:], 1.0), w=[ones4])
                gi = sb(es, "gi", [4, SCH], F32)
                gf = sb(es, "gf", [4, SCH], F32)
                t1 = sb(es, "gt1", [4, SCH], F32)
                t2 = sb(es, "gt2", [4, SCH], F32)
                Mo = sb(es, "Mo", [4, SCH], F32)
                WIo = sb(es, "WIo", [4, SCH], F32)
                A_ = sb(es, "A_", [4, SCH], F32)
                NMt = sb(es, "NMt", [4, SCH], F32)
                WS = sb(es, "WS", [4, SCH], F32)
                EM = sb(es, "EM", [4, SCH], F32)
                STK = sb(es, "STK", [128, SCH], F32)
                carNB = sb(es, "carNB", [4, 1], F32)
                carM = sb(es, "carM", [4, 1], F32)
                kb.op("pool", lambda e: e.memset(STK[:], 0.0), w=[STK])
                qT = sb(es, "mqT", [64, 4, SCH], BF16)
                kT = sb(es, "mkT", [64, 4, SCH], BF16)
                CN = [sb(es, "CN%d" % h, [64, 65], F32) for h in range(4)]
                CNb = [sb(es, "CNb%d" % h, [64, 65], BF16) for h in range(4)]
                tq = [sb(es, "tq%d" % i, [128, 16], F32) for i in range(2)]
                vas = [sb(es, "va%d" % i, [128, 4, 65], BF16) for i in range(2)]
                vms = [sb(es, "vm%d" % i, [128, 256], BF16) for i in range(2)]
                kms = [sb(es, "km%d" % i, [128, 256], BF16) for i in range(2)]
                Dm = [sb(es, "Dm%d" % i, [128, 128], F32) for i in range(2)]
                Em = [sb(es, "Em%d" % i, [128, 128], F32) for i in range(2)]
                PTm = [sb(es, "PTm%d" % i, [128, 128], BF16) for i in range(2)]
                intra = [sb(es, "intra%d" % i, [128, 65], F32) for i in range(2)]
                HN = [sb(es, "HN%d" % i, [128, 65], F32) for i in range(2)]
                dn = [sb(es, "dn%d" % i, [128, 2], F32) for i in range(2)]
                dcs = [sb(es, "dcs%d" % i, [64, 1], F32) for i in range(2)]
                KW = [sb(es, "KW%d" % i, [128, 64], BF16) for i in range(2)]
                HT = [sb(es, "HT%d" % i, [128, 256], F32) for i in range(2)]
                hfl = [sb(es, "hfl%d" % i, [128, 256], F32) for i in range(2)]
                oml = [sb(es, "oml%d" % i, [128, 256], F32) for i in range(2)]
                gst = sb(es, "gst", [128, 4, 6], F32)
                gmv = sb(es, "gmv", [128, 4, 2], F32)
                grs = sb(es, "grs", [128, 4], F32)
                yb_ = [sb(es, "myb%d" % i, [128, 256], BF16) for i in range(2)]
                yT = [sb(es, "myT%d" % i, [128, 256], BF16) for i in range(2)]
                it = 0
                un = 0
                for d in range(2):
                    kb.op("dve", lambda e: e.memset(carNB[:], 0.0), w=[carNB])
                    kb.op("dve", lambda e: e.memset(carM[:], 0.0), w=[carM])
                    gci = 0
                    for k in range(NSC):
                        o0 = k * SCH if d == 0 else (NSC - 1 - k) * SCH
                        kb.dma(gi[:], sc["G4"].h[2 * d, :, o0:o0 + SCH], r=[sc["G4"]], w=[gi])
                        kb.dma(gf[:], sc["G4"].h[2 * d + 1, :, o0:o0 + SCH], r=[sc["G4"]], w=[gf])
                        kb.dma(qT[:], sc["QMT"].h.rearrange("(h p) s -> p h s", p=64)[:, :, o0:o0 + SCH], r=[sc["QMT"]], w=[qT])
                        kb.dma(kT[:], sc["KMT"].h.rearrange("(h p) s -> p h s", p=64)[:, :, o0:o0 + SCH], r=[sc["KMT"]], w=[kT])
                        kb.op("dve", lambda e, d=d: e.tensor_scalar(out=gi[:, :], in0=gi[:, :], scalar1=gb[:, 2 * d:2 * d + 1], scalar2=None,
                                                                    op0=ALU.add), r=[gb], w=[gi])
                        kb.op("act", lambda e, d=d: e.activation(out=t1[:, :], in_=gf[:, :], func=AF.Exp, scale=-1.0,
                                                                 bias=ngb[:, 2 * d + 1:2 * d + 2]), r=[gf, ngb], w=[t1])
                        kb.op("act", lambda e: e.activation(out=t1[:, :], in_=t1[:, :], func=AF.Ln, scale=1.0, bias=onesf[0:4, 0:1]),
                              r=[onesf], w=[t1])
                        if d == 0:
                            isrc, fsrc = gi, t1
                        else:
                            kb.op("dve", lambda e: e.tensor_copy(out=t2[:, :], in_=gi[:, ::-1]), r=[gi], w=[t2])
                            kb.op("dve", lambda e: e.tensor_copy(out=gf[:, :], in_=t1[:, ::-1]), r=[t1], w=[gf])
                            isrc, fsrc = t2, gf
                        kb.op("dve", lambda e, fsrc=fsrc: e.tensor_tensor_scan(out=NMt[:, :], data0=ones4[:, :], data1=fsrc[:, :],
                                                                               initial=carNB[:, 0:1], op0=ALU.mult, op1=ALU.add),
                              r=[ones4, fsrc, carNB], w=[NMt])
                        kb.op("dve", lambda e: e.tensor_copy(out=carNB[:, 0:1], in_=NMt[:, SCH - 1:SCH]), r=[NMt], w=[carNB])
                        kb.op("dve", lambda e, isrc=isrc: e.tensor_tensor(out=A_[:, :], in0=isrc[:, :], in1=NMt[:, :], op=ALU.add),
                              r=[isrc, NMt], w=[A_])
                        Mf = t1 if d == 0 else gi
                        kb.op("dve", lambda e, Mf=Mf: e.tensor_tensor_scan(out=Mf[:, :], data0=ones4[:, :], data1=A_[:, :],
                                                                           initial=carM[:, 0:1], op0=ALU.mult, op1=ALU.max),
                              r=[ones4, A_, carM], w=[Mf])
                        kb.op("dve", lambda e, Mf=Mf: e.tensor_tensor(out=EM[:, :], in0=NMt[:, :], in1=Mf[:, :], op=ALU.subtract),
                              r=[NMt, Mf], w=[EM])
                        kb.op("act", lambda e: e.activation(out=EM[:, :], in_=EM[:, :], func=AF.Exp), r=[], w=[EM])
                        kb.op("dve", lambda e, Mf=Mf: e.tensor_scalar(out=NMt[:, :], in0=Mf[:, :], scalar1=-1.0, scalar2=None, op0=ALU.mult),
                              r=[Mf], w=[NMt])
                        WIf = gf if d == 0 else t1
                        for c in range(NCH):
                            c0 = c * 128
                            bprev = carM[:, 0:1] if c == 0 else Mf[:, c0 - 1:c0]
                            kb.op("act", lambda e, c0=c0, bprev=bprev, WIf=WIf: e.activation(out=WIf[:, c0:c0 + 128], in_=NMt[:, c0:c0 + 128],
                                                                                             func=AF.Exp, bias=bprev),
                                  r=[NMt, Mf, carM], w=[WIf])
                            kb.op("act", lambda e, c0=c0: e.activation(out=WS[:, c0:c0 + 128], in_=A_[:, c0:c0 + 128], func=AF.Exp,
                                                                       bias=NMt[:, c0 + 127:c0 + 128]), r=[A_, NMt], w=[WS])
                        kb.op("dve", lambda e, Mf=Mf: e.tensor_copy(out=carM[:, 0:1], in_=Mf[:, SCH - 1:SCH]), r=[Mf], w=[carM])
                        if d == 0:
                            kb.op("pool", lambda e, Mf=Mf: e.tensor_copy(out=Mo[:, :], in_=Mf[:, :]), r=[Mf], w=[Mo])
                            kb.op("pool", lambda e, WIf=WIf: e.tensor_copy(out=WIo[:, :], in_=WIf[:, :]), r=[WIf], w=[WIo])
                            srcs = (A_, WIo, EM, WS)
                        else:
                            kb.op("dve", lambda e, Mf=Mf: e.tensor_copy(out=Mo[:, :], in_=Mf[:, ::-1]), r=[Mf], w=[Mo])
                            kb.op("dve", lambda e, WIf=WIf: e.tensor_copy(out=WIo[:, :], in_=WIf[:, ::-1]), r=[WIf], w=[WIo])
                            kb.op("dve", lambda e: e.tensor_copy(out=t2[:, :], in_=A_[:, ::-1]), r=[A_], w=[t2])
                            kb.op("dve", lambda e: e.tensor_copy(out=A_[:, :], in_=EM[:, ::-1]), r=[EM], w=[A_])
                            kb.op("dve", lambda e: e.tensor_copy(out=EM[:, :], in_=WS[:, ::-1]), r=[WS], w=[EM])
                            srcs = (t2, WIo, A_, EM)
                        for kk, s_ in enumerate(srcs):
                            kb.dma(STK[32 * kk:32 * kk + 4, :], s_[:, :], r=[s_], w=[STK])
                        order = range(NCH) if d == 0 else range(NCH - 1, -1, -1)
                        for c in order:
                            c0 = c * 128
                            g0 = o0 + c0
                            ci = gci
                            gci += 1
                            sl = it % 2
                            it += 1
                            tqs, va, vm, km, ht = tq[sl], vas[sl], vms[sl], kms[sl], HT[sl]
                            kb.op("pe", lambda e, c0=c0: e.transpose(out=PS[0][:, 0:128], in_=STK[:, c0:c0 + 128], identity=identf[:]),
                                  r=[STK, identf], w=[PS[0]])
                            kb.op("act", lambda e, tqs=tqs: e.activation(
                                out=tqs.h.rearrange("p (k h) -> p k h", k=4),
                                in_=PS[0].h[:, 0:128].rearrange("p (k x) -> p k x", k=4)[:, :, 0:4], func=AF.Copy), r=[PS[0]], w=[tqs])
                            kb.dma(vm[:], sc["VM"].h[g0:g0 + 128, :], r=[sc["VM"]], w=[vm])
                            kb.dma(km[:], sc["KM"].h[g0:g0 + 128, :], r=[sc["KM"]], w=[km])
                            kb.op("pool", lambda e, va=va, vm=vm: e.tensor_copy(out=va[:, :, 0:64], in_=vm.h.rearrange("p (h x) -> p h x", h=4)),
                                  r=[vm], w=[va])
                            kb.op("pool", lambda e, va=va: e.memset(va[:, :, 64:65], 1.0), w=[va])
                            for h in range(4):
                                u = un % 2
                                un += 1
                                b_st, b_mb, b_ia = PS[1 + 3 * u], PS[2 + 3 * u], PS[3 + 3 * u]
                                b_ie, b_dc = b_ia, b_mb
                                kb.op("pe", lambda e, h=h, c0=c0, b_st=b_st: e.matmul(b_st[:, 0:128], lhsT=kT[:, h, c0:c0 + 128],
                                                                                       rhs=qT[:, h, c0:c0 + 128], start=True, stop=True),
                                      r=[kT, qT], w=[b_st])
                                kb.op("pe", lambda e, h=h, c0=c0, b_mb=b_mb: e.matmul(b_mb[:, 0:128], lhsT=selt[:, h, :],
                                                                                       rhs=Mo[:, c0:c0 + 128], start=True, stop=True),
                                      r=[selt, Mo], w=[b_mb])
                                Dt, Et, Pt = Dm[u], Em[u], PTm[u]
                                kb.op("dve", lambda e, h=h, d=d, Dt=Dt, tqs=tqs, b_mb=b_mb: e.scalar_tensor_tensor(
                                    out=Dt[:, :], in0=b_mb[:, 0:128], scalar=tqs[:, h:h + 1], in1=masks[d][:, :],
                                    op0=ALU.subtract, op1=ALU.max), r=[b_mb, tqs, masks[d]], w=[Dt])
                                kb.op("act", lambda e, Dt=Dt, Et=Et: e.activation(out=Et[:, :], in_=Dt[:, :], func=AF.Exp, scale=-1.0),
                                      r=[Dt], w=[Et])
                                kb.op("dve", lambda e, Et=Et, Pt=Pt, b_st=b_st: e.tensor_tensor(out=Pt[:, :], in0=b_st[:, 0:128], in1=Et[:, :],
                                                                                               op=ALU.mult), r=[b_st, Et], w=[Pt])
                                kb.op("pe", lambda e, h=h, Pt=Pt, va=va, b_ia=b_ia: e.matmul(b_ia[:, 0:65], lhsT=Pt[:, :], rhs=va[:, h, :],
                                                                                             start=True, stop=True), r=[Pt, va], w=[b_ia])
                                hn = HN[u]
                                if ci == 0:
                                    kb.op("act", lambda e, hn=hn, b_ia=b_ia: e.activation(out=hn[:, :], in_=b_ia[:, 0:65], func=AF.Copy),
                                          r=[b_ia], w=[hn])
                                else:
                                    ia = intra[u]
                                    kb.op("act", lambda e, ia=ia, b_ia=b_ia: e.activation(out=ia[:, :], in_=b_ia[:, 0:65], func=AF.Copy),
                                          r=[b_ia], w=[ia])
                                    kb.op("pe", lambda e, h=h, c0=c0, b_ie=b_ie: e.matmul(b_ie[:, 128:193], lhsT=qT[:, h, c0:c0 + 128],
                                                                                           rhs=CNb[h][:, :], start=True, stop=True),
                                          r=[qT, CNb[h]], w=[b_ie])
                                    kb.op("dve", lambda e, h=h, hn=hn, ia=ia, tqs=tqs, b_ie=b_ie: e.scalar_tensor_tensor(
                                        out=hn[:, :], in0=b_ie[:, 128:193], scalar=tqs[:, 4 + h:5 + h], in1=ia[:, :],
                                        op0=ALU.mult, op1=ALU.add), r=[b_ie, tqs, ia], w=[hn])
                                dnt = dn[u]
                                kb.op("dve", lambda e, hn=hn, dnt=dnt: e.scalar_tensor_tensor(
                                    out=dnt[:, 0:1], in0=hn[:, 64:65], scalar=-1.0, in1=hn[:, 64:65],
                                    op0=ALU.mult, op1=ALU.max), r=[hn], w=[dnt])
                                kb.op("dve", lambda e, h=h, dnt=dnt, tqs=tqs: e.tensor_scalar(
                                    out=dnt[:, 0:1], in0=dnt[:, 0:1], scalar1=tqs[:, 8 + h:9 + h], scalar2=None,
                                    op0=ALU.max), r=[tqs], w=[dnt])
                                kb.op("dve", lambda e, dnt=dnt: e.reciprocal(out=dnt[:, 1:2], in_=dnt[:, 0:1]), r=[], w=[dnt])
                                kb.op("dve", lambda e, h=h, hn=hn, dnt=dnt, ht=ht: e.tensor_scalar(
                                    out=ht[:, h * 64:(h + 1) * 64], in0=hn[:, 0:64], scalar1=dnt[:, 1:2], scalar2=None, op0=ALU.mult),
                                    r=[hn, dnt], w=[ht])
                                if ci < NCT - 1:
                                    kw = KW[u]
                                    kb.op("pool", lambda e, h=h, kw=kw, km=km, tqs=tqs: e.tensor_scalar(
                                        out=kw[:, :], in0=km[:, h * 64:(h + 1) * 64], scalar1=tqs[:, 12 + h:13 + h], scalar2=None,
                                        op0=ALU.mult), r=[km, tqs], w=[kw])
                                    kb.op("pe", lambda e, h=h, kw=kw, va=va, b_dc=b_dc: e.matmul(b_dc[0:64, 128:193], lhsT=kw[:, :],
                                                                                                 rhs=va[:, h, :], start=True, stop=True),
                                          r=[kw, va], w=[b_dc])
                                    if ci == 0:
                                        kb.op("act", lambda e, h=h, b_dc=b_dc: e.activation(out=CN[h][:, :], in_=b_dc[0:64, 128:193],
                                                                                            func=AF.Copy), r=[b_dc], w=[CN[h]])
                                    else:
                                        col = c0 + 127 if d == 0 else c0
                                        dct = dcs[u]
                                        kb.op("pe", lambda e, h=h, col=col, b_dc=b_dc: e.matmul(
                                            b_dc[0:64, 200:201], lhsT=selt[:, h, 0:64], rhs=WIo[:, col:col + 1], start=True, stop=True),
                                            r=[selt, WIo], w=[b_dc])
                                        kb.op("dve", lambda e, dct=dct, b_dc=b_dc: e.tensor_copy(out=dct[:, 0:1], in_=b_dc[0:64, 200:201]),
                                              r=[b_dc], w=[dct])
                                        kb.op("dve", lambda e, h=h, dct=dct, b_dc=b_dc: e.scalar_tensor_tensor(
                                            out=CN[h][:, :], in0=CN[h][:, :], scalar=dct[:, 0:1], in1=b_dc[0:64, 128:193],
                                            op0=ALU.mult, op1=ALU.add), r=[b_dc, dct], w=[CN[h]])
                                    kb.op("act", lambda e, h=h: e.activation(out=CNb[h][:, :], in_=CN[h][:, :], func=AF.Copy),
                                          r=[CN[h]], w=[CNb[h]])
                            if d == 0:
                                kb.dma(sc["HF"].h[g0:g0 + 128, :], ht[:, :], r=[ht], w=[sc["HF"]])
                            else:
                                hf, om, ybt, yTt = hfl[sl], oml[sl], yb_[sl], yT[sl]
                                kb.dma(hf[:], sc["HF"].h[g0:g0 + 128, :], r=[sc["HF"]], w=[hf])
                                kb.dma(om[:], sc["OM"].h[g0:g0 + 128, :], r=[sc["OM"]], w=[om])
                                kb.op("dve", lambda e, hf=hf, ht=ht: e.tensor_tensor(out=hf[:, :], in0=hf[:, :], in1=ht[:, :], op=ALU.add),
                                      r=[ht], w=[hf])
                                for h in range(4):
                                    kb.op("dve", lambda e, h=h, hf=hf: e.bn_stats(out=gst[:, h, :], in_=hf[:, h * 64:(h + 1) * 64]),
                                          r=[hf], w=[gst])
                                    kb.op("dve", lambda e, h=h: e.bn_aggr(out=gmv[:, h, :], in_=gst[:, h, :]), r=[gst], w=[gmv])
                                rsqrt_eps(grs, grs[:, :], gmv[:, :, 1], [gmv])
                                for h in range(4):
                                    kb.op("dve", lambda e, h=h, hf=hf: e.tensor_scalar(
                                        out=hf[:, h * 64:(h + 1) * 64], in0=hf[:, h * 64:(h + 1) * 64], scalar1=gmv[:, h, 0:1],
                                        scalar2=grs[:, h:h + 1], op0=ALU.subtract, op1=ALU.mult), r=[gmv, grs], w=[hf])
                                kb.op("act", lambda e, om=om: e.activation(out=om[:, :], in_=om[:, :], func=AF.Sigmoid), r=[], w=[om])
                                kb.op("pool", lambda e, hf=hf: e.tensor_tensor(out=hf[:, :], in0=hf[:, :], in1=repn[:, :], op=ALU.mult),
                                      r=[repn], w=[hf])
                                kb.op("dve", lambda e, hf=hf, om=om, ybt=ybt: e.tensor_tensor(out=ybt[:, :], in0=hf[:, :], in1=om[:, :],
                                                                                             op=ALU.mult), r=[hf, om], w=[ybt])
                                for cc in range(2):
                                    kb.op("pe", lambda e, cc=cc, ybt=ybt: e.transpose(out=PSB[:, cc * 128:(cc + 1) * 128],
                                                                                      in_=ybt[:, cc * 128:(cc + 1) * 128], identity=identb[:]),
                                          r=[ybt, identb], w=[PSB])
                                kb.op("act", lambda e, yTt=yTt: e.activation(out=yTt[:, :], in_=PSB[:, 0:256], func=AF.Copy), r=[PSB], w=[yTt])
                                kb.dma(sc["MIXT"].h[768:1024, :].rearrange("(c p) s -> p c s", p=128)[:, :, g0:g0 + 128],
                                       yTt.h.rearrange("p (c t) -> p c t", c=2), r=[yTt], w=[sc["MIXT"]])
                kb.barrier()

        if dbg and l == 0:
            dbg_out = T(nc.dram_tensor("dbg_mixt", [D, jobs[0]], BF16, kind="ExternalOutput").ap())
            with ExitStack() as es:
                dt_ = sb(es, "dbgt", [128, 8, jobs[0]], BF16)
                kb.dma(dt_[:], SC[0]["MIXT"].h.rearrange("(c p) s -> p c s", p=128), r=[SC[0]["MIXT"]], w=[dt_])
                kb.dma(dbg_out.h.rearrange("(c p) s -> p c s", p=128), dt_[:], r=[dt_], w=[dbg_out])
                kb.barrier()
        alpha = float(8.0 ** 0.25)
        with ExitStack() as es:
            stg = [sb(es, "stg%d" % i, [128, 2048], F32) for i in range(2)]
            wo = sb(es, "wo", [128, 8, D], BF16)
            load_w_bf16(es, wo, lambda c, n0, n1: wo[:, c, n0:n1], w_out[l], 8, D, stg)
            rl = sb(es, "rl", [128, 2 * D], F32)
            kb.dma(rl[:], rep_l[l, :, 0:2 * D], w=[rl])
            tmp = {"st6": sb(es, "st6", [128, 2, 6], F32), "mv": sb(es, "mv", [128, 2], F32), "rs": sb(es, "rs", [128, 1], F32)}
            mts = [sb(es, "mt%d" % i, [128, 8, 512], BF16) for i in range(2)]
            xres = [sb(es, "xres%d" % i, [128, D], F32) for i in range(2)]
            z = [sb(es, "z%d" % i, [128, D], F32) for i in range(2)]
            x1f = [sb(es, "x1f%d" % i, [128, D], F32) for i in range(2)]
            ob = [sb(es, "ob%d" % i, [128, D], BF16) for i in range(2)]
            xts = [sb(es, "xts%d" % i, [128, D], BF16) for i in range(2)]
            bi = 0
            for j, S in enumerate(jobs):
                sc = SC[j]
                XNi, XNo = sc["XN"][0], sc["XN"][1]
                for blk in range(S // 512):
                    t0 = blk * 512
                    mt = mts[bi % 2]
                    bi += 1
                    kb.dma(mt[:], sc["MIXT"].h.rearrange("(c p) s -> p c s", p=128)[:, :, t0:t0 + 512], r=[sc["MIXT"]], w=[mt])
                    for tt in range(4):
                        tk0 = t0 + tt * 128
                        xr, zt, obt, x1, xt_ = xres[tt % 2], z[tt % 2], ob[tt % 2], x1f[tt % 2], xts[tt % 2]
                        kb.dma(xr[:], XNi.h[tk0:tk0 + 128, :], r=[XNi], w=[xr])
                        for nb in range(2):
                            for c in range(8):
                                kb.op("pe", lambda e, c=c, nb=nb, tt=tt: e.matmul(PS[nb][:, :], lhsT=mt[:, c, tt * 128:(tt + 1) * 128],
                                                                                   rhs=wo[:, c, nb * 512:(nb + 1) * 512],
                                                                                   start=(c == 0), stop=(c == 7)), r=[mt, wo], w=[PS[nb]])
                            kb.op("dve", lambda e, nb=nb, xr=xr, zt=zt: e.scalar_tensor_tensor(
                                out=zt[:, nb * 512:(nb + 1) * 512], in0=xr[:, nb * 512:(nb + 1) * 512], scalar=alpha,
                                in1=PS[nb][:, :], op0=ALU.mult, op1=ALU.add), r=[xr, PS[nb]], w=[zt])
                        layer_norm_tile(zt, rl, rl, 0, D, x1, obt, tmp)
                        kb.dma(XNo.h[tk0:tk0 + 128, :], x1[:], r=[x1], w=[XNo])
                        transpose_to_xt(obt, xt_, sc["XT"], tk0)
            kb.barrier()
        with ExitStack() as es:
            stg = [sb(es, "stg%d" % i, [128, 2048], F32) for i in range(2)]
            wg = sb(es, "wg", [128, 8, DFF], BF16)
            wu = sb(es, "wu", [128, 8, DFF], BF16)
            load_w_bf16(es, wg, lambda c, n0, n1: wg[:, c, n0:n1], w_gate[l], 8, DFF, stg)
            load_w_bf16(es, wu, lambda c, n0, n1: wu[:, c, n0:n1], w_up[l], 8, DFF, stg)
            x1Ts = [sb(es, "x1T%d" % i, [128, 8, 512], BF16) for i in range(2)]
            sg = [sb(es, "sg%d" % i, [128, 512], F32) for i in range(2)]
            hd = [sb(es, "hd%d" % i, [128, 512], BF16) for i in range(2)]
            bi = 0
            for j, S in enumerate(jobs):
                sc = SC[j]
                for blk in range(S // 512):
                    t0 = blk * 512
                    x1T = x1Ts[bi % 2]
                    bi += 1
                    kb.dma(x1T[:], sc["XT"].h.rearrange("(c p) s -> p c s", p=128)[:, :, t0:t0 + 512], r=[sc["XT"]], w=[x1T])
                    for f in range(NFC):
                        bg, bu = PS[2 * (f % 2)], PS[1 + 2 * (f % 2)]
                        for c in range(8):
                            kb.op("pe", lambda e, c=c, f=f, bg=bg: e.matmul(bg[:, :], lhsT=wg[:, c, f * 128:(f + 1) * 128], rhs=x1T[:, c, :],
                                                                            start=(c == 0), stop=(c == 7)), r=[wg, x1T], w=[bg])
                        for c in range(8):
                            kb.op("pe", lambda e, c=c, f=f, bu=bu: e.matmul(bu[:, :], lhsT=wu[:, c, f * 128:(f + 1) * 128], rhs=x1T[:, c, :],
                                                                            start=(c == 0), stop=(c == 7)), r=[wu, x1T], w=[bu])
                        sgt, hdt = sg[f % 2], hd[f % 2]
                        kb.op("act", lambda e, sgt=sgt, bg=bg: e.activation(out=sgt[:, :], in_=bg[:, :], func=AF.Silu), r=[bg], w=[sgt])
                        kb.op("dve", lambda e, hdt=hdt, sgt=sgt, bu=bu: e.tensor_tensor(out=hdt[:, :], in0=bu[:, :], in1=sgt[:, :], op=ALU.mult),
                              r=[bu, sgt], w=[hdt])
                        kb.dma(sc["HIDT"].h[f * 128:(f + 1) * 128, t0:t0 + 512], hdt[:, :], r=[hdt], w=[sc["HIDT"]])
            kb.barrier()
        with ExitStack() as es:
            stg = [sb(es, "stg%d" % i, [128, 2048], F32) for i in range(2)]
            wd = sb(es, "wd", [128, NFC, D], BF16)
            load_w_bf16(es, wd, lambda c, n0, n1: wd[:, c, n0:n1], w_down[l], NFC, D, stg)
            rl = sb(es, "rl", [128, 2 * D], F32)
            kb.dma(rl[:], rep_l[l, :, 2 * D:4 * D], w=[rl])
            tmp = {"st6": sb(es, "st6", [128, 2, 6], F32), "mv": sb(es, "mv", [128, 2], F32), "rs": sb(es, "rs", [128, 1], F32)}
            HIDs = [sb(es, "HID%d" % i, [128, NFC, 512], BF16) for i in range(2)]
            x1f = [sb(es, "x1f%d" % i, [128, D], F32) for i in range(2)]
            z = [sb(es, "z%d" % i, [128, D], F32) for i in range(2)]
            of2 = [sb(es, "of2%d" % i, [128, D], F32) for i in range(2)]
            ob = [sb(es, "ob%d" % i, [128, D], BF16) for i in range(2)]
            xts = [sb(es, "xts%d" % i, [128, D], BF16) for i in range(2)]
            bi = 0
            for j, S in enumerate(jobs):
                sc = SC[j]
                X1, XNo = sc["XN"][1], sc["XN"][0]
                for blk in range(S // 512):
                    t0 = blk * 512
                    HID = HIDs[bi % 2]
                    bi += 1
                    kb.dma(HID[:], sc["HIDT"].h.rearrange("(f p) s -> p f s", p=128)[:, :, t0:t0 + 512], r=[sc["HIDT"]], w=[HID])
                    for tt in range(4):
                        tk0 = t0 + tt * 128
                        zt, obt, x1, o2, xt_ = z[tt % 2], ob[tt % 2], x1f[tt % 2], of2[tt % 2], xts[tt % 2]
                        kb.dma(x1[:], X1.h[tk0:tk0 + 128, :], r=[X1], w=[x1])
                        for nb in range(2):
                            for f in range(NFC):
                                kb.op("pe", lambda e, f=f, nb=nb, tt=tt: e.matmul(PS[nb][:, :], lhsT=HID[:, f, tt * 128:(tt + 1) * 128],
                                                                                   rhs=wd[:, f, nb * 512:(nb + 1) * 512],
                                                                                   start=(f == 0), stop=(f == NFC - 1)), r=[HID, wd], w=[PS[nb]])
                            kb.op("dve", lambda e, nb=nb, x1=x1, zt=zt: e.scalar_tensor_tensor(
                                out=zt[:, nb * 512:(nb + 1) * 512], in0=x1[:, nb * 512:(nb + 1) * 512], scalar=alpha,
                                in1=PS[nb][:, :], op0=ALU.mult, op1=ALU.add), r=[x1, PS[nb]], w=[zt])
                        layer_norm_tile(zt, rl, rl, 0, D, o2, obt, tmp)
                        if last:
                            kb.dma(yout[j].h[tk0:tk0 + 128, :], o2[:], r=[o2], w=[yout[j]])
                        else:
                            kb.dma(XNo.h[tk0:tk0 + 128, :], o2[:], r=[o2], w=[XNo])
                            transpose_to_xt(obt, xt_, sc["XT"], tk0)
            kb.barrier()
    kb.barrier()
    es_glob.close()
    return nc


def _tables(jobs, smax):
    half = 16
    inv = 1.0 / (10000.0 ** (np.arange(0, 32, 2, dtype=np.float32) / 32.0))
    ang = np.arange(smax, dtype=np.float32)[:, None] * inv[None, :].astype(np.float32)
    cos = np.cos(ang).astype(np.float32).T
    sin = np.sin(ang).astype(np.float32).T
    cos2 = np.concatenate([cos, cos], 0)
    sin2 = np.concatenate([-sin, sin], 0)
    pedge = np.zeros((len(jobs), 128, 2, 16), np.float32)
    for j, S in enumerate(jobs):
        for g, w in enumerate((2, 4, 8, 16)):
            c, p0 = g // 2, (g % 2) * 64
            for i in range(8):
                t = i
                cnt = min(t + w // 2, S) - max(t - w // 2, 0)
                pedge[j, p0:p0 + 64, c, i] = 1.0 / cnt
                t = S - 8 + i
                cnt = min(t + w // 2, S) - max(t - w // 2, 0)
                pedge[j, p0:p0 + 64, c, 8 + i] = 1.0 / cnt
    s_ = np.arange(128)[:, None]
    j_ = np.arange(128)[None, :]
    maskf = np.where(s_ <= j_, 0.0, BIG).astype(np.float32)
    maskb = np.where(s_ >= j_, 0.0, BIG).astype(np.float32)
    sel = np.zeros((4, 4, 128), np.float32)
    for h in range(4):
        sel[h, h, :] = 1.0
    return dict(cos2=np.ascontiguousarray(cos2), sin2=np.ascontiguousarray(sin2), pedge=pedge, maskf=maskf, maskb=maskb,
                sel=sel, ident=np.eye(128, dtype=np.float32))


def _common_inputs(inp, L, jobs, smax):
    f = lambda a: np.ascontiguousarray(np.asarray(a, dtype=np.float32))
    rep = lambda v: np.broadcast_to(f(v)[None, :], (128, f(v).shape[0]))
    m = {}
    for k in ("w_in", "w_uq", "w_ukv", "w_pool", "w_out", "w_gate", "w_up", "w_down"):
        m[k] = f(inp[k])[:L]
    m["rep_in"] = np.ascontiguousarray(np.concatenate([rep(inp["ln_in_g"]), rep(inp["ln_in_b"])], 1))
    m["rep_l"] = np.ascontiguousarray(np.stack([
        np.concatenate([rep(inp["ln1_g"][l]), rep(inp["ln1_b"][l]), rep(inp["ln2_g"][l]), rep(inp["ln2_b"][l]),
                        rep(inp["mlstm_norm_g"][l])], 1) for l in range(L)]))
    m["qg"] = np.ascontiguousarray(f(inp["q_norm_g"])[:L].reshape(L, 2, 128).transpose(0, 2, 1))
    m["kvg"] = np.ascontiguousarray(f(inp["kv_norm_g"])[:L].reshape(L, 128, 1))
    m["psc"] = np.ascontiguousarray(f(inp["pool_scale"])[:L].reshape(L, 4, 64).transpose(0, 2, 1))
    m["gbias"] = np.ascontiguousarray(f(inp["mlstm_gate_bias"])[:L].reshape(L, 4, 4).transpose(0, 2, 1))
    m.update(_tables(jobs, smax))
    return m


_CACHE = {}


def run(inp, L, job_inputs_per_core, jobs, dbg=False):
    smax = max(jobs)
    key = (L, tuple(jobs))
    if key not in _CACHE:
        _CACHE[key] = build_program(L, jobs, smax, dbg)
    nc = _CACHE[key]
    common = _common_inputs(inp, L, jobs, smax)
    in_maps = []
    for xs in job_inputs_per_core:
        m = dict(common)
        for j, x in enumerate(xs):
            m["xin%d" % j] = np.ascontiguousarray(x, dtype=np.float32)
        in_maps.append(m)
    res = run_bass_kernel_spmd(nc, in_maps, core_ids=list(range(len(in_maps))))
    if dbg:
        return [[r["yout%d" % j] for j in range(len(jobs))] + [r["dbg_mixt"]] for r in res.results]
    return [[r["yout%d" % j] for j in range(len(jobs))] for r in res.results]


def kernel(**inputs):
    xp = np.asarray(inputs["x_prompt"], dtype=np.float32)
    xs = np.asarray(inputs["x_sample"], dtype=np.float32)
    L = int(np.asarray(inputs["w_in"]).shape[0])
    jobs = [xp.shape[1], xs.shape[1], xs.shape[1]]
    per_core = [[xp[c % 2], xs[2 * c], xs[2 * c + 1]] for c in range(8)]
    outs = run(inputs, L, per_core, jobs)
    y_prompt = np.stack([outs[0][0], outs[1][0]], 0).astype(np.float32)
    y_sample = np.stack([outs[c][1 + i] for c in range(8) for i in range(2)], 0).astype(np.float32)
    return (y_prompt, y_sample)
```

```python
import numpy as np
from contextlib import ExitStack
import concourse.bass as bass
import concourse.mybir as mybir
from concourse.bass_utils import run_bass_kernel_spmd

F32 = mybir.dt.float32
BF16 = mybir.dt.bfloat16
AF = mybir.ActivationFunctionType
ALU = mybir.AluOpType

D = 1024
NQ, NKV, NR, NP_, NM, NG = 256, 128, 32, 256, 256, 16
IN_W = 1712
DFF = 2816
NFC = DFF // 128
EPS = 1e-5
BIG = 1e30
SAME_ENG_SYNC = True


class Buf:
    __slots__ = ("w", "r")

    def __init__(self):
        self.w = None
        self.r = {}


class T:
    def __init__(self, h):
        self.h = h
        self.b = Buf()

    def __getitem__(self, idx):
        return self.h[idx]


class KB:
    ND = 8

    def __init__(self, nc):
        self.nc = nc
        self.E = {"pe": nc.tensor, "act": nc.scalar, "dve": nc.vector, "pool": nc.gpsimd, "sp": nc.sync}
        self.sem = {}
        self.val = {}
        for e in self.E:
            self.sem[e] = nc.alloc_semaphore(name="c_" + e)
            self.val[e] = 0
        for q in ("sp", "pool"):
            for i in range(self.ND):
                self.sem[(q, i)] = nc.alloc_semaphore(name="d_%s%d" % (q, i))
                self.val[(q, i)] = 0
        self.dslot = {"sp": 0, "pool": 0}
        self.waited = {e: {} for e in self.E}

    def _wait(self, e, sk, v):
        if v <= 0:
            return
        if self.waited[e].get(sk, 0) < v:
            self.E[e].wait_ge(self.sem[sk], v)
            self.waited[e][sk] = v

    def _deps(self, e, r, w):
        deps = {}
        for t in list(r) + list(w):
            ev = t.b.w
            if ev is not None:
                deps[ev[0]] = max(deps.get(ev[0], 0), ev[1])
        for t in w:
            for sk, v in t.b.r.items():
                deps[sk] = max(deps.get(sk, 0), v)
        for sk, v in deps.items():
            if sk == e and (e == "pe" or not SAME_ENG_SYNC):
                continue
            self._wait(e, sk, v)

    def _mark(self, ev, r, w):
        for t in r:
            t.b.r[ev[0]] = max(t.b.r.get(ev[0], 0), ev[1])
        for t in w:
            t.b.w = ev
            t.b.r = {}

    def op(self, e, fn, r=(), w=()):
        self._deps(e, r, w)
        ins = fn(self.E[e])
        self.val[e] += 1
        ins.then_inc(self.sem[e], 1)
        self._mark((e, self.val[e]), r, w)

    def dma(self, out, in_, r=(), w=(), q="sp"):
        self._deps(q, r, w)
        slot = self.dslot[q]
        self.dslot[q] = (slot + 1) % self.ND
        sk = (q, slot)
        self._wait(q, sk, self.val[sk])
        ins = self.E[q].dma_start(out=out, in_=in_)
        self.val[sk] += 16
        ins.then_inc(self.sem[sk], 16)
        self._mark((sk, self.val[sk]), r, w)

    def barrier(self):
        for e in self.E:
            for sk in self.sem:
                if sk != e:
                    self._wait(e, sk, self.val[sk])


def build_program(L, jobs, smax, dbg=False):
    nc = bass.Bass("TRN2", target_bir_lowering=False)
    kb = KB(nc)
    NJ = len(jobs)

    def din(name, shape, dt=F32):
        return nc.dram_tensor(name, list(shape), dt, kind="ExternalInput").ap()

    def dscr(name, shape, dt):
        return T(nc.dram_tensor(name, list(shape), dt).ap())

    xin = [din("xin%d" % j, [S, D]) for j, S in enumerate(jobs)]
    yout = [T(nc.dram_tensor("yout%d" % j, [S, D], F32, kind="ExternalOutput").ap()) for j, S in enumerate(jobs)]
    w_in = din("w_in", [L, D, IN_W])
    w_uq = din("w_uq", [L, NQ, 768])
    w_ukv = din("w_ukv", [L, NKV, 1024])
    w_pool = din("w_pool", [L, 4, 64, 64])
    w_out = din("w_out", [L, D, D])
    w_gate = din("w_gate", [L, D, DFF])
    w_up = din("w_up", [L, D, DFF])
    w_down = din("w_down", [L, DFF, D])
    rep_in = din("rep_in", [128, 2 * D])
    rep_l = din("rep_l", [L, 128, 4 * D + 256])
    qg = din("qg", [L, 128, 2])
    kvg = din("kvg", [L, 128, 1])
    psc = din("psc", [L, 64, 4])
    gbias = din("gbias", [L, 4, 4])
    cos2 = din("cos2", [32, smax])
    sin2 = din("sin2", [32, smax])
    pedge = din("pedge", [NJ, 128, 2, 16])
    maskf_d = din("maskf", [128, 128])
    maskb_d = din("maskb", [128, 128])
    sel_d = din("sel", [4, 4, 128])
    ident_d = din("ident", [128, 128])

    SC = []
    for j, S in enumerate(jobs):
        s = {}
        s["XN"] = [dscr("XN%d_%d" % (i, j), [S, D], F32) for i in range(2)]
        s["XT"] = dscr("XT%d" % j, [D, S], BF16)
        s["CQT"] = dscr("CQT%d" % j, [NQ, S], BF16)
        s["CKVT"] = dscr("CKVT%d" % j, [NKV, S], BF16)
        s["KRT"] = dscr("KRT%d" % j, [32, S], BF16)
        s["POOLT"] = dscr("POOLT%d" % j, [256, S], F32)
        s["QMT"] = dscr("QMT%d" % j, [256, S], BF16)
        s["KMT"] = dscr("KMT%d" % j, [256, S], BF16)
        s["G4"] = dscr("G4%d" % j, [4, 4, S], F32)
        s["VM"] = dscr("VM%d" % j, [S, 256], BF16)
        s["KM"] = dscr("KM%d" % j, [S, 256], BF16)
        s["OM"] = dscr("OM%d" % j, [S, 256], F32)
        s["MIXT"] = dscr("MIXT%d" % j, [D, S], BF16)
        s["HF"] = dscr("HF%d" % j, [S, 256], F32)
        s["HIDT"] = dscr("HIDT%d" % j, [DFF, S], BF16)
        SC.append(s)

    es_glob = ExitStack()
    uniq = [0]

    def sb(es, name, shape, dt):
        uniq[0] += 1
        return T(es.enter_context(nc.sbuf_tensor("%s_%d" % (name, uniq[0]), list(shape), dt)))

    PS = [None] * 7
    PSBh = [None]

    class _PSB:
        @property
        def h(self):
            return PSBh[0].h

        @property
        def b(self):
            return PSBh[0].b

        def __getitem__(self, idx):
            return PSBh[0].h[idx]

    PSB = _PSB()

    def std_psum(es):
        for i in range(7):
            uniq[0] += 1
            PS[i] = T(es.enter_context(nc.psum_tensor("ps%d_%d" % (i, uniq[0]), [128, 512], F32)))
        uniq[0] += 1
        PSBh[0] = T(es.enter_context(nc.psum_tensor("psb_%d" % uniq[0], [128, 1024], BF16)))

    identf = sb(es_glob, "identf", [128, 128], F32)
    identb = sb(es_glob, "identb", [128, 128], BF16)
    onesf = sb(es_glob, "onesf", [128, 128], F32)
    onesb = sb(es_glob, "onesb", [128, 128], BF16)
    kb.dma(identf[:], ident_d[:, :], w=[identf])
    kb.op("dve", lambda e: e.tensor_copy(out=identb[:], in_=identf[:]), r=[identf], w=[identb])
    kb.op("dve", lambda e: e.memset(onesf[:], 1.0), w=[onesf])
    kb.op("dve", lambda e: e.memset(onesb[:], 1.0), w=[onesb])

    stg_ctr = [0]

    def load_w_bf16(es_stage, dst, dst_ap_fn, src_ap, nk, ncols, stg):
        CW = 2048
        for c in range(nk):
            for n0 in range(0, ncols, CW):
                n1 = min(ncols, n0 + CW)
                st = stg[stg_ctr[0] % len(stg)]
                stg_ctr[0] += 1
                kb.dma(st[:, 0:n1 - n0], src_ap[c * 128:(c + 1) * 128, n0:n1], w=[st])
                eng = "pool" if (stg_ctr[0] % 2) else "dve"
                kb.op(eng, lambda e, st=st, c=c, n0=n0, n1=n1: e.tensor_copy(out=dst_ap_fn(c, n0, n1), in_=st[:, 0:n1 - n0]),
                      r=[st], w=[dst])

    def rsqrt_eps(t, out_ap, in_ap, rd, scale=1.0):
        kb.op("dve", lambda e: e.tensor_scalar(out=out_ap, in0=in_ap, scalar1=scale, scalar2=EPS, op0=ALU.mult, op1=ALU.add),
              r=rd, w=[t])
        kb.op("act", lambda e: e.activation(out=out_ap, in_=out_ap, func=AF.Sqrt), r=[], w=[t])
        kb.op("dve", lambda e: e.reciprocal(out=out_ap, in_=out_ap), r=[], w=[t])

    def layer_norm_tile(z, gt, bt, g_off, b_off, outf, outb, tmp):
        st6, mv, rs = tmp["st6"], tmp["mv"], tmp["rs"]
        for hh in range(2):
            kb.op("dve", lambda e, hh=hh: e.bn_stats(out=st6[:, hh, :], in_=z[:, hh * 512:(hh + 1) * 512]), r=[z], w=[st6])
        kb.op("dve", lambda e: e.bn_aggr(out=mv[:, :], in_=st6[:, :, :]), r=[st6], w=[mv])
        rsqrt_eps(rs, rs[:, :], mv[:, 1:2], [mv])
        kb.op("dve", lambda e: e.tensor_scalar(out=outf[:, :], in0=z[:, :], scalar1=mv[:, 0:1], scalar2=rs[:, 0:1],
                                               op0=ALU.subtract, op1=ALU.mult), r=[z, mv, rs], w=[outf])
        kb.op("pool", lambda e: e.tensor_tensor(out=outf[:, :], in0=outf[:, :], in1=gt[:, g_off:g_off + D], op=ALU.mult),
              r=[gt], w=[outf])
        kb.op("dve", lambda e: e.tensor_tensor(out=outf[:, :], in0=outf[:, :], in1=bt[:, b_off:b_off + D], op=ALU.add),
              r=[bt], w=[outf])
        kb.op("act", lambda e: e.activation(out=outb[:, :], in_=outf[:, :], func=AF.Copy), r=[outf], w=[outb])

    def transpose_to_xt(outb, xts, XT_T, tok0):
        for c in range(8):
            kb.op("pe", lambda e, c=c: e.transpose(out=PSB[:, c * 128:(c + 1) * 128], in_=outb[:, c * 128:(c + 1) * 128],
                                                   identity=identb[:]), r=[outb, identb], w=[PSB])
        kb.op("act", lambda e: e.activation(out=xts[:, :], in_=PSB[:, :], func=AF.Copy), r=[PSB], w=[xts])
        kb.dma(XT_T.h.rearrange("(c p) s -> p c s", p=128)[:, :, tok0:tok0 + 128],
               xts.h.rearrange("p (c t) -> p c t", c=8), r=[xts], w=[XT_T])

    with ExitStack() as es:
        std_psum(es)
        rin = sb(es, "rin", [128, 2 * D], F32)
        kb.dma(rin[:], rep_in[:, :], w=[rin])
        tmp = {"st6": sb(es, "st6", [128, 2, 6], F32), "mv": sb(es, "mv", [128, 2], F32), "rs": sb(es, "rs", [128, 1], F32)}
        zs = [sb(es, "pz%d" % i, [128, D], F32) for i in range(2)]
        ofs = [sb(es, "pof%d" % i, [128, D], F32) for i in range(2)]
        obs = [sb(es, "pob%d" % i, [128, D], BF16) for i in range(2)]
        xtss = [sb(es, "pxt%d" % i, [128, D], BF16) for i in range(2)]
        it = 0
        for j, S in enumerate(jobs):
            for t in range(S // 128):
                z, of, ob, xts = zs[it % 2], ofs[it % 2], obs[it % 2], xtss[it % 2]
                it += 1
                kb.dma(z[:], xin[j][t * 128:(t + 1) * 128, :], w=[z])
                layer_norm_tile(z, rin, rin, 0, D, of, ob, tmp)
                kb.dma(SC[j]["XN"][0].h[t * 128:(t + 1) * 128, :], of[:], r=[of], w=[SC[j]["XN"][0]])
                transpose_to_xt(ob, xts, SC[j]["XT"], t * 128)
        kb.barrier()

    for l in range(L):
        last = l == L - 1
        with ExitStack() as es:
            std_psum(es)
            stg = [sb(es, "stg%d" % i, [128, 2048], F32) for i in range(2)]
            win = sb(es, "win", [128, 8, IN_W], BF16)
            load_w_bf16(es, win, lambda c, n0, n1: win[:, c, n0:n1], w_in[l], 8, IN_W, stg)
            wkr_sw = sb(es, "wkrsw", [128, 8, 96], BF16)
            kb.op("dve", lambda e: e.tensor_copy(out=wkr_sw[:, :, 0:64], in_=win[:, :, 320:384]), r=[win], w=[wkr_sw])
            kb.op("dve", lambda e: e.tensor_copy(out=wkr_sw[:, :, 64:80], in_=win[:, :, 400:416]), r=[win], w=[wkr_sw])
            kb.op("dve", lambda e: e.tensor_copy(out=wkr_sw[:, :, 80:96], in_=win[:, :, 384:400]), r=[win], w=[wkr_sw])
            qgt = sb(es, "qgt", [128, 2], F32)
            kvgt = sb(es, "kvgt", [128, 1], F32)
            kb.dma(qgt[:], qg[l], w=[qgt])
            kb.dma(kvgt[:], kvg[l], w=[kvgt])
            xTs = [sb(es, "xT%d" % i, [128, 8, 512], BF16) for i in range(2)]
            cst = [sb(es, "cs%d" % i, [96, 512], F32) for i in range(2)]
            snt = [sb(es, "sn%d" % i, [96, 512], F32) for i in range(2)]
            sq = sb(es, "sq", [128, 512], BF16)
            rstd = sb(es, "rstd", [128, 512], F32)
            ev = [sb(es, "ev%d" % i, [128, 512], BF16) for i in range(2)]
            evf = [sb(es, "evf%d" % i, [128, 512], F32) for i in range(2)]
            kr1 = sb(es, "kr1", [96, 512], F32)
            kr2 = sb(es, "kr2", [96, 512], F32)
            krb = sb(es, "krb", [96, 512], BF16)
            tmb = [sb(es, "tmb%d" % i, [128, 512], BF16) for i in range(2)]
            tmf = [sb(es, "tmf%d" % i, [128, 256], F32) for i in range(2)]
            evc = 0
            for j, S in enumerate(jobs):
                sc = SC[j]
                for blk in range(S // 512):
                    t0 = blk * 512
                    xT = xTs[blk % 2]
                    kb.dma(xT[:], sc["XT"].h.rearrange("(c p) s -> p c s", p=128)[:, :, t0:t0 + 512], r=[sc["XT"]], w=[xT])
                    cs, sn = cst[blk % 2], snt[blk % 2]
                    kb.dma(cs[64:96, :], cos2[:, t0:t0 + 512], w=[cs])
                    kb.dma(sn[64:96, :], sin2[:, t0:t0 + 512], w=[sn])

                    def fm(col0, m, bank, lhs=None):
                        for c in range(8):
                            lt = (win[:, c, col0:col0 + m] if lhs is None else lhs[:, c, 0:m])
                            kb.op("pe", lambda e, c=c, lt=lt: e.matmul(PS[bank][0:m, :], lhsT=lt, rhs=xT[:, c, :],
                                                                        start=(c == 0), stop=(c == 7)),
                                  r=[win if lhs is None else lhs, xT], w=[PS[bank]])

                    for (col0, nchunk, gtile, dst, key) in ((0, 2, qgt, "CQT", "q"), (256, 1, kvgt, "CKVT", "kv")):
                        for cc in range(nchunk):
                            fm(col0 + cc * 128, 128, cc)
                        for cc in range(nchunk):
                            kb.op("act", lambda e, cc=cc: e.activation(out=sq[:, :], in_=PS[cc][:, :], func=AF.Square),
                                  r=[PS[cc]], w=[sq])
                            kb.op("pe", lambda e, cc=cc: e.matmul(PS[2][:, :], lhsT=onesb[:, :], rhs=sq[:, :],
                                                                  start=(cc == 0), stop=(cc == nchunk - 1)),
                                  r=[onesb, sq], w=[PS[2]])
                        nfeat = 128.0 * nchunk
                        rsqrt_eps(rstd, rstd[:, :], PS[2][:, :], [PS[2]], scale=1.0 / nfeat)
                        for cc in range(nchunk):
                            o = ev[evc % 2]
                            evc += 1
                            kb.op("dve", lambda e, cc=cc, o=o, gtile=gtile: e.scalar_tensor_tensor(
                                out=o[:, :], in0=PS[cc][:, :], scalar=gtile[:, cc:cc + 1], in1=rstd[:, :],
                                op0=ALU.mult, op1=ALU.mult), r=[PS[cc], gtile, rstd], w=[o])
                            kb.dma(sc[dst].h[cc * 128:(cc + 1) * 128, t0:t0 + 512], o[:, :], r=[o], w=[sc[dst]])
                    fm(320, 96, 3)
                    fm(0, 96, 4, lhs=wkr_sw)
                    kb.op("dve", lambda e: e.tensor_tensor(out=kr1[64:96, :], in0=PS[3][64:96, :], in1=cs[64:96, :], op=ALU.mult),
                          r=[PS[3], cs], w=[kr1])
                    kb.op("dve", lambda e: e.tensor_tensor(out=kr2[64:96, :], in0=PS[4][64:96, :], in1=sn[64:96, :], op=ALU.mult),
                          r=[PS[4], sn], w=[kr2])
                    kb.op("pool", lambda e: e.tensor_tensor(out=krb[64:96, :], in0=kr1[64:96, :], in1=kr2[64:96, :], op=ALU.add),
                          r=[kr1, kr2], w=[krb])
                    kb.dma(sc["KRT"].h[:, t0:t0 + 512], krb[64:96, :], r=[krb], w=[sc["KRT"]])
                    bi = 0
                    for (col0, dst, kind) in ((416, "POOLT", "f"), (544, "POOLT", "f"), (672, "QMT", "b"), (800, "QMT", "b"),
                                              (928, "KMT", "k"), (1056, "KMT", "k")):
                        bank = 5 + (bi % 2)
                        bi += 1
                        fm(col0, 128, bank)
                        r0 = ((col0 - 416) % 256) if dst == "POOLT" else ((col0 - 672) % 256)
                        if kind == "f":
                            o = evf[evc % 2]
                            evc += 1
                            kb.op("act", lambda e, o=o, bank=bank: e.activation(out=o[:, :], in_=PS[bank][:, :], func=AF.Copy),
                                  r=[PS[bank]], w=[o])
                        else:
                            o = ev[evc % 2]
                            evc += 1
                            scl = 0.125 if kind == "k" else 1.0
                            kb.op("act", lambda e, o=o, bank=bank, scl=scl: e.activation(out=o[:, :], in_=PS[bank][:, :],
                                                                                           func=AF.Copy, scale=scl),
                                  r=[PS[bank]], w=[o])
                        kb.dma(sc[dst].h[r0:r0 + 128, t0:t0 + 512], o[:, :], r=[o], w=[sc[dst]])
                    for ty in range(4):
                        fm(1696 + 4 * ty, 4, 3 + (ty % 2))
                        o = evf[evc % 2]
                        evc += 1
                        bank = 3 + (ty % 2)
                        kb.op("act", lambda e, o=o, bank=bank: e.activation(out=o[0:4, :], in_=PS[bank][0:4, :], func=AF.Copy),
                              r=[PS[bank]], w=[o])
                        kb.dma(sc["G4"].h[ty, :, t0:t0 + 512], o[0:4, :], r=[o], w=[sc["G4"]])
                    for tt in range(4):
                        tk0 = t0 + tt * 128
                        for c in range(8):
                            kb.op("pe", lambda e, c=c, tt=tt: e.matmul(PS[0][:, :], lhsT=xT[:, c, tt * 128:(tt + 1) * 128],
                                                                        rhs=win[:, c, 928:1440], start=(c == 0), stop=(c == 7)),
                                  r=[xT, win], w=[PS[0]])
                        for c in range(8):
                            kb.op("pe", lambda e, c=c, tt=tt: e.matmul(PS[1][:, 0:256], lhsT=xT[:, c, tt * 128:(tt + 1) * 128],
                                                                        rhs=win[:, c, 1440:1696], start=(c == 0), stop=(c == 7)),
                                  r=[xT, win], w=[PS[1]])
                        ob = tmb[tt % 2]
                        of = tmf[tt % 2]
                        kb.op("act", lambda e, ob=ob: e.activation(out=ob[:, 0:256], in_=PS[0][:, 0:256], func=AF.Copy, scale=0.125),
                              r=[PS[0]], w=[ob])
                        kb.op("dve", lambda e, ob=ob: e.tensor_copy(out=ob[:, 256:512], in_=PS[0][:, 256:512]), r=[PS[0]], w=[ob])
                        kb.op("act", lambda e, of=of: e.activation(out=of[:, :], in_=PS[1][:, 0:256], func=AF.Copy), r=[PS[1]], w=[of])
                        kb.dma(sc["KM"].h[tk0:tk0 + 128, :], ob[:, 0:256], r=[ob], w=[sc["KM"]])
                        kb.dma(sc["VM"].h[tk0:tk0 + 128, :], ob[:, 256:512], r=[ob], w=[sc["VM"]])
                        kb.dma(sc["OM"].h[tk0:tk0 + 128, :], of[:, :], r=[of], w=[sc["OM"]])
            kb.barrier()

        for j, S in enumerate(jobs):
            sc = SC[j]
            NKC = S // 128
            NQB = S // 512
            with ExitStack() as es:
                uniq[0] += 1
                STT = [T(es.enter_context(nc.psum_tensor("st%d_%d" % (i_, uniq[0]), [128, 1024], F32))) for i_ in range(3)]
                OTT = T(es.enter_context(nc.psum_tensor("ot_%d" % uniq[0], [128, 512], F32)))
                MSC = T(es.enter_context(nc.psum_tensor("msc_%d" % uniq[0], [128, 512], F32)))
                stg = [sb(es, "stg%d" % i, [128, 2048], F32) for i in range(2)]
                wq = sb(es, "wq", [128, 2, 768], BF16)
                wqs = sb(es, "wqs", [128, 2, 768], BF16)
                wkv = sb(es, "wkv", [128, 1, 1024], BF16)
                load_w_bf16(es, wq, lambda c, n0, n1: wq[:, c, n0:n1], w_uq[l], 2, 768, stg)
                load_w_bf16(es, wkv, lambda c, n0, n1: wkv[:, c, n0:n1], w_ukv[l], 1, 1024, stg)
                wq4 = wq.h.rearrange("p c (h d) -> p c h d", h=8)
                wqs4 = wqs.h.rearrange("p c (h d) -> p c h d", h=8)
                kb.op("dve", lambda e: e.tensor_copy(out=wqs[:, :, :], in_=wq[:, :, :]), r=[wq], w=[wqs])
                kb.op("dve", lambda e: e.tensor_copy(out=wqs4[:, :, :, 64:80], in_=wq4[:, :, :, 80:96]), r=[wq], w=[wqs])
                kb.op("dve", lambda e: e.tensor_copy(out=wqs4[:, :, :, 80:96], in_=wq4[:, :, :, 64:80]), r=[wq], w=[wqs])
                ckv = sb(es, "ckv", [128, S], BF16)
                KT = sb(es, "KT", [96, S], BF16)
                VA = sb(es, "VA", [128, NKC, 65], BF16)
                kb.dma(ckv[:], sc["CKVT"].h[:, :], r=[sc["CKVT"]], w=[ckv])
                kb.dma(KT[64:96, :], sc["KRT"].h[:, :], r=[sc["KRT"]], w=[KT])
                kb.op("dve", lambda e: e.memset(VA[:, :, 64:65], 1.0), w=[VA])
                cqs = [sb(es, "cq%d" % i, [128, 2, 512], BF16) for i in range(2)]
                cst = [sb(es, "acs%d" % i, [96, 512], F32) for i in range(2)]
                snt = [sb(es, "asn%d" % i, [96, 512], F32) for i in range(2)]
                QTs = [sb(es, "QT%d" % i, [96, 512], BF16) for i in range(2)]
                q1 = sb(es, "q1", [96, 512], F32)
                q2 = sb(es, "q2", [96, 512], F32)
                PTs = [sb(es, "PT%d" % i, [128, 1024], BF16) for i in range(3)]
                den = sb(es, "den", [65, 512], F32)
                bcs = sb(es, "bcs", [64, 512], F32)
                ots = [sb(es, "ot%d" % i, [64, 512], BF16) for i in range(2)]
                scale = 96.0 ** -0.5
                qbc = 0
                for h in range(8):
                    for blk in range(NQB):
                        kb.op("pe", lambda e, blk=blk: e.matmul(MSC[0:64, :], lhsT=wkv[:, 0, h * 128:h * 128 + 64],
                                                                 rhs=ckv[:, blk * 512:(blk + 1) * 512], start=True, stop=True),
                              r=[wkv, ckv], w=[MSC])
                        kb.op("dve", lambda e, blk=blk: e.tensor_copy(out=KT[0:64, blk * 512:(blk + 1) * 512], in_=MSC[0:64, :]),
                              r=[MSC], w=[KT])
                    for g in range(NKC // 8):
                        for i in range(8):
                            kc = g * 8 + i
                            kb.op("pe", lambda e, kc=kc, i=i: e.matmul(MSC[:, i * 64:(i + 1) * 64], lhsT=ckv[:, kc * 128:(kc + 1) * 128],
                                                                        rhs=wkv[:, 0, h * 128 + 64:h * 128 + 128], start=True, stop=True),
                                  r=[wkv, ckv], w=[MSC])
                        kb.op("act", lambda e, g=g: e.activation(out=VA[:, g * 8:(g + 1) * 8, 0:64],
                                                                 in_=MSC.h.rearrange("p (i d) -> p i d", i=8), func=AF.Copy),
                              r=[MSC], w=[VA])
                    for qb in range(NQB):
                        q0 = qb * 512
                        cq, cs, sn, QT = cqs[qbc % 2], cst[qbc % 2], snt[qbc % 2], QTs[qbc % 2]
                        OT = OTT
                        ot = ots[qbc % 2]
                        qbc += 1
                        kb.dma(cq[:], sc["CQT"].h.rearrange("(c p) s -> p c s", p=128)[:, :, q0:q0 + 512], r=[sc["CQT"]], w=[cq])
                        kb.dma(cs[64:96, :], cos2[:, q0:q0 + 512], w=[cs])
                        kb.dma(sn[64:96, :], sin2[:, q0:q0 + 512], w=[sn])
                        for c in range(2):
                            kb.op("pe", lambda e, c=c: e.matmul(MSC[0:96, :], lhsT=wq[:, c, h * 96:(h + 1) * 96], rhs=cq[:, c, :],
                                                                start=(c == 0), stop=(c == 1)), r=[wq, cq], w=[MSC])
                        kb.op("act", lambda e: e.activation(out=QT[0:64, :], in_=MSC[0:64, :], func=AF.Copy), r=[MSC], w=[QT])
                        kb.op("dve", lambda e: e.tensor_tensor(out=q1[64:96, :], in0=MSC[64:96, :], in1=cs[64:96, :], op=ALU.mult),
                              r=[MSC, cs], w=[q1])
                        for c in range(2):
                            kb.op("pe", lambda e, c=c: e.matmul(MSC[0:96, :], lhsT=wqs[:, c, h * 96:(h + 1) * 96], rhs=cq[:, c, :],
                                                                start=(c == 0), stop=(c == 1)), r=[wqs, cq], w=[MSC])
                        kb.op("dve", lambda e: e.tensor_tensor(out=q2[64:96, :], in0=MSC[64:96, :], in1=sn[64:96, :], op=ALU.mult),
                              r=[MSC, sn], w=[q2])
                        kb.op("pool", lambda e: e.tensor_tensor(out=QT[64:96, :], in0=q1[64:96, :], in1=q2[64:96, :], op=ALU.add),
                              r=[q1, q2], w=[QT])
                        npair = NKC // 2

                        def mm1(p):
                            st = STT[p % 3]
                            for i in range(2):
                                kc = 2 * p + i
                                kb.op("pe", lambda e, kc=kc, i=i: e.matmul(st[:, i * 512:(i + 1) * 512], lhsT=KT[0:96, kc * 128:(kc + 1) * 128],
                                                                            rhs=QT[0:96, :], start=True, stop=True),
                                      r=[KT, QT], w=[st])

                        def ex(p):
                            st = STT[p % 3]
                            pt = PTs[p % 3]
                            kb.op("act", lambda e: e.activation(out=pt[:, :], in_=st[:, :], func=AF.Exp, scale=scale),
                                  r=[st], w=[pt])

                        def mm2(p):
                            pt = PTs[p % 3]
                            for i in range(2):
                                kc = 2 * p + i
                                kb.op("pe", lambda e, kc=kc, i=i: e.matmul(OT[0:65, :], lhsT=VA[:, kc, :], rhs=pt[:, i * 512:(i + 1) * 512],
                                                                            start=(kc == 0), stop=(kc == NKC - 1)),
                                      r=[VA, pt], w=[OT])

                        mm1(0)
                        if npair > 1:
                            mm1(1)
                        for p in range(npair):
                            ex(p)
                            if p + 2 < npair:
                                mm1(p + 2)
                            mm2(p)
                        kb.op("dve", lambda e: e.reciprocal(out=den[64:65, :], in_=OT[64:65, :]), r=[OT], w=[den])
                        kb.op("pe", lambda e: e.matmul(MSC[0:64, :], lhsT=onesf[64:65, 0:64], rhs=den[64:65, :], start=True, stop=True),
                              r=[onesf, den], w=[MSC])
                        kb.op("act", lambda e: e.activation(out=bcs[:, :], in_=MSC[0:64, :], func=AF.Copy), r=[MSC], w=[bcs])
                        kb.op("dve", lambda e: e.tensor_tensor(out=ot[:, :], in0=OT[0:64, :], in1=bcs[:, :], op=ALU.mult),
                              r=[OT, bcs], w=[ot])
                        kb.dma(sc["MIXT"].h[h * 64:(h + 1) * 64, q0:q0 + 512], ot[:, :], r=[ot], w=[sc["MIXT"]])
                kb.barrier()

        with ExitStack() as es:
            std_psum(es)
            stg = [sb(es, "stg%d" % i, [128, 2048], F32) for i in range(2)]
            wp = sb(es, "wp", [128, 2, 64], BF16)
            for g in range(4):
                st = stg[g % 2]
                p0 = (g % 2) * 64
                kb.dma(st[p0:p0 + 64, 0:64], w_pool[l, g], w=[st])
                kb.op("dve", lambda e, st=st, p0=p0, g=g: e.tensor_copy(out=wp[p0:p0 + 64, g // 2, :], in_=st[p0:p0 + 64, 0:64]),
                      r=[st], w=[wp])
            psct = sb(es, "psct", [64, 4], F32)
            kb.dma(psct[:], psc[l], w=[psct])
            PB = 2048
            xps = [sb(es, "xp%d" % i, [128, 2, PB + 16], F32) for i in range(2)]
            a2 = sb(es, "a2", [128, PB + 16], F32)
            a4 = sb(es, "a4", [128, PB + 16], F32)
            yb = [sb(es, "yb%d" % i, [128, PB], BF16) for i in range(2)]
            yf = sb(es, "yf", [128, PB], F32)
            ped = sb(es, "ped", [128, 2, 16], F32)
            po = [sb(es, "po%d" % i, [64, 512], BF16) for i in range(2)]
            WINS = (2, 4, 8, 16)
            bc = 0
            for j, S in enumerate(jobs):
                sc = SC[j]
                kb.dma(ped[:], pedge[j], w=[ped])
                pb = min(PB, S)
                for blk in range(S // pb):
                    t0 = blk * pb
                    xp = xps[bc % 2]
                    bc += 1
                    lo = max(0, t0 - 8)
                    hi = min(S, t0 + pb + 8)
                    if lo > t0 - 8:
                        kb.op("pool", lambda e: e.memset(xp[:, :, 0:8], 0.0), w=[xp])
                    if hi < t0 + pb + 8:
                        kb.op("pool", lambda e: e.memset(xp[:, :, pb + 8:pb + 16], 0.0), w=[xp])
                    kb.dma(xp[:, :, 8 + (lo - t0):8 + (hi - t0)],
                           sc["POOLT"].h.rearrange("(c p) s -> p c s", p=128)[:, :, lo:hi], r=[sc["POOLT"]], w=[xp])
                    for c in range(2):
                        n = pb + 16
                        x = xp.h[:, c, :]
                        kb.op("dve", lambda e, x=x: e.tensor_tensor(out=a2[:, 0:n - 1], in0=x[:, 0:n - 1], in1=x[:, 1:n], op=ALU.add),
                              r=[xp], w=[a2])
                        kb.op("pool", lambda e: e.tensor_tensor(out=a4[:, 0:n - 3], in0=a2[:, 0:n - 3], in1=a2[:, 2:n - 1], op=ALU.add),
                              r=[a2], w=[a4])
                        if c == 1:
                            kb.op("dve", lambda e: e.tensor_tensor(out=a2[:, 0:n - 7], in0=a4[:, 0:n - 7], in1=a4[:, 4:n - 3], op=ALU.add),
                                  r=[a4], w=[a2])
                            kb.op("pool", lambda e: e.tensor_tensor(out=a4[64:128, 0:n - 15], in0=a2[64:128, 0:n - 15],
                                                                    in1=a2[64:128, 8:n - 7], op=ALU.add), r=[a2], w=[a4])
                        for half in range(2):
                            w = WINS[2 * c + half]
                            src = a2 if half == 0 else a4
                            p0 = half * 64
                            o0 = 8 - w // 2
                            kb.op("dve", lambda e, src=src, p0=p0, w=w, o0=o0, x=x: e.scalar_tensor_tensor(
                                out=yf[p0:p0 + 64, 0:pb], in0=src[p0:p0 + 64, o0:o0 + pb], scalar=1.0 / w,
                                in1=x[p0:p0 + 64, 8:8 + pb], op0=ALU.mult, op1=ALU.subtract), r=[src, xp], w=[yf])
                            if t0 == 0:
                                kb.op("dve", lambda e, src=src, p0=p0, o0=o0, x=x, c=c: e.tensor_tensor(
                                    out=yf[p0:p0 + 64, 0:8], in0=src[p0:p0 + 64, o0:o0 + 8], in1=ped[p0:p0 + 64, c, 0:8], op=ALU.mult),
                                    r=[src, ped], w=[yf])
                                kb.op("dve", lambda e, p0=p0, x=x: e.tensor_tensor(
                                    out=yf[p0:p0 + 64, 0:8], in0=yf[p0:p0 + 64, 0:8], in1=x[p0:p0 + 64, 8:16], op=ALU.subtract),
                                    r=[xp], w=[yf])
                            if t0 + pb == S:
                                kb.op("dve", lambda e, src=src, p0=p0, o0=o0, x=x, c=c: e.tensor_tensor(
                                    out=yf[p0:p0 + 64, pb - 8:pb], in0=src[p0:p0 + 64, o0 + pb - 8:o0 + pb], in1=ped[p0:p0 + 64, c, 8:16],
                                    op=ALU.mult), r=[src, ped], w=[yf])
                                kb.op("dve", lambda e, p0=p0, x=x: e.tensor_tensor(
                                    out=yf[p0:p0 + 64, pb - 8:pb], in0=yf[p0:p0 + 64, pb - 8:pb], in1=x[p0:p0 + 64, pb:pb + 8],
                                    op=ALU.subtract), r=[xp], w=[yf])
                        ybt = yb[c]
                        kb.op("act", lambda e, ybt=ybt: e.activation(out=ybt[:, 0:pb], in_=yf[:, 0:pb], func=AF.Copy), r=[yf], w=[ybt])
                        for half in range(2):
                            g = 2 * c + half
                            p0 = half * 64
                            for sb_ in range(pb // 512):
                                bank = 5 + (sb_ % 2)
                                kb.op("pe", lambda e, p0=p0, c=c, sb_=sb_, ybt=ybt, bank=bank: e.matmul(
                                    PS[bank][0:64, :], lhsT=wp[p0:p0 + 64, c, :], rhs=ybt[p0:p0 + 64, sb_ * 512:(sb_ + 1) * 512],
                                    start=True, stop=True), r=[wp, ybt], w=[PS[bank]])
                                o = po[sb_ % 2]
                                kb.op("act", lambda e, o=o, bank=bank, g=g: e.activation(out=o[:, :], in_=PS[bank][0:64, :], func=AF.Copy,
                                                                                         scale=psct[:, g:g + 1]),
                                      r=[PS[bank], psct], w=[o])
                                kb.dma(sc["MIXT"].h[512 + g * 64:512 + (g + 1) * 64, t0 + sb_ * 512:t0 + (sb_ + 1) * 512], o[:, :],
                                       r=[o], w=[sc["MIXT"]])
            kb.barrier()

        for j, S in enumerate(jobs):
            sc = SC[j]
            SCH = min(S, 2048)
            NSC = S // SCH
            NCH = SCH // 128
            NCT = S // 128
            with ExitStack() as es:
                std_psum(es)
                gb = sb(es, "gb", [4, 4], F32)
                ngb = sb(es, "ngb", [4, 4], F32)
                kb.dma(gb[:], gbias[l], w=[gb])
                kb.op("dve", lambda e: e.tensor_scalar(out=ngb[:, :], in0=gb[:, :], scalar1=-1.0, scalar2=None, op0=ALU.mult),
                      r=[gb], w=[ngb])
                selt = sb(es, "selt", [4, 4, 128], F32)
                kb.dma(selt[:], sel_d[:, :, :], w=[selt])
                masks = [sb(es, "mkf", [128, 128], F32), sb(es, "mkb", [128, 128], F32)]
                kb.dma(masks[0][:], maskf_d[:, :], w=[masks[0]])
                kb.dma(masks[1][:], maskb_d[:, :], w=[masks[1]])
                repn = sb(es, "repn", [128, 256], F32)
                kb.dma(repn[:], rep_l[l, :, 4 * D:4 * D + 256], w=[repn])
                ones4 = sb(es, "ones4", [4, SCH], F32)
                kb.op("pool", lambda e: e.memset(ones4[:], 1.0), w=[ones4])
                gi = sb(es, "gi", [4, SCH], F32)
                gf = sb(es, "gf", [4, SCH], F32)
                t1 = sb(es, "gt1", [4, SCH], F32)
                t2 = sb(es, "gt2", [4, SCH], F32)
                Mo = sb(es, "Mo", [4, SCH], F32)
                WIo = sb(es, "WIo", [4, SCH], F32)
                A_ = sb(es, "A_", [4, SCH], F32)
                NMt = sb(es, "NMt", [4, SCH], F32)
                WS = sb(es, "WS", [4, SCH], F32)
                EM = sb(es, "EM", [4, SCH], F32)
                STK = sb(es, "STK", [128, SCH], F32)
                carNB = sb(es, "carNB", [4, 1], F32)
                carM = sb(es, "carM", [4, 1], F32)
                kb.op("pool", lambda e: e.memset(STK[:], 0.0), w=[STK])
                qT = sb(es, "mqT", [64, 4, SCH], BF16)
                kT = sb(es, "mkT", [64, 4, SCH], BF16)
                CN = [sb(es, "CN%d" % h, [64, 65], F32) for h in range(4)]
                CNb = [sb(es, "CNb%d" % h, [64, 65], BF16) for h in range(4)]
                tq = [sb(es, "tq%d" % i, [128, 16], F32) for i in range(2)]
                vas = [sb(es, "va%d" % i, [128, 4, 65], BF16) for i in range(2)]
                vms = [sb(es, "vm%d" % i, [128, 256], BF16) for i in range(2)]
                kms = [sb(es, "km%d" % i, [128, 256], BF16) for i in range(2)]
                Dm = [sb(es, "Dm%d" % i, [128, 128], F32) for i in range(2)]
                Em = [sb(es, "Em%d" % i, [128, 128], F32) for i in range(2)]
                PTm = [sb(es, "PTm%d" % i, [128, 128], BF16) for i in range(2)]
                intra = [sb(es, "intra%d" % i, [128, 65], F32) for i in range(2)]
                HN = [sb(es, "HN%d" % i, [128, 65], F32) for i in range(2)]
                dn = [sb(es, "dn%d" % i, [128, 2], F32) for i in range(2)]
                dcs = [sb(es, "dcs%d" % i, [64, 1], F32) for i in range(2)]
                KW = [sb(es, "KW%d" % i, [128, 64], BF16) for i in range(2)]
                HT = [sb(es, "HT%d" % i, [128, 256], F32) for i in range(2)]
                hfl = [sb(es, "hfl%d" % i, [128, 256], F32) for i in range(2)]
                oml = [sb(es, "oml%d" % i, [128, 256], F32) for i in range(2)]
                gst = sb(es, "gst", [128, 4, 6], F32)
                gmv = sb(es, "gmv", [128, 4, 2], F32)
                grs = sb(es, "grs", [128, 4], F32)
                yb_ = [sb(es, "myb%d" % i, [128, 256], BF16) for i in range(2)]
                yT = [sb(es, "myT%d" % i, [128, 256], BF16) for i in range(2)]
                it = 0
                un = 0
                for d in range(2):
                    kb.op("dve", lambda e: e.memset(carNB[:], 0.0), w=[carNB])
                    kb.op("dve", lambda e: e.memset(carM[:], 0.0), w=[carM])
                    gci = 0
                    for k in range(NSC):
                        o0 = k * SCH if d == 0 else (NSC - 1 - k) * SCH
                        kb.dma(gi[:], sc["G4"].h[2 * d, :, o0:o0 + SCH], r=[sc["G4"]], w=[gi])
                        kb.dma(gf[:], sc["G4"].h[2 * d + 1, :, o0:o0 + SCH], r=[sc["G4"]], w=[gf])
                        kb.dma(qT[:], sc["QMT"].h.rearrange("(h p) s -> p h s", p=64)[:, :, o0:o0 + SCH], r=[sc["QMT"]], w=[qT])
                        kb.dma(kT[:], sc["KMT"].h.rearrange("(h p) s -> p h s", p=64)[:, :, o0:o0 + SCH], r=[sc["KMT"]], w=[kT])
                        kb.op("dve", lambda e, d=d: e.tensor_scalar(out=gi[:, :], in0=gi[:, :], scalar1=gb[:, 2 * d:2 * d + 1], scalar2=None,
                                                                    op0=ALU.add), r=[gb], w=[gi])
                        kb.op("act", lambda e, d=d: e.activation(out=t1[:, :], in_=gf[:, :], func=AF.Exp, scale=-1.0,
                                                                 bias=ngb[:, 2 * d + 1:2 * d + 2]), r=[gf, ngb], w=[t1])
                        kb.op("act", lambda e: e.activation(out=t1[:, :], in_=t1[:, :], func=AF.Ln, scale=1.0, bias=onesf[0:4, 0:1]),
                              r=[onesf], w=[t1])
                        if d == 0:
                            isrc, fsrc = gi, t1
                        else:
                            kb.op("dve", lambda e: e.tensor_copy(out=t2[:, :], in_=gi[:, ::-1]), r=[gi], w=[t2])
                            kb.op("dve", lambda e: e.tensor_copy(out=gf[:, :], in_=t1[:, ::-1]), r=[t1], w=[gf])
                            isrc, fsrc = t2, gf
                        kb.op("dve", lambda e, fsrc=fsrc: e.tensor_tensor_scan(out=NMt[:, :], data0=ones4[:, :], data1=fsrc[:, :],
                                                                               initial=carNB[:, 0:1], op0=ALU.mult, op1=ALU.add),
                              r=[ones4, fsrc, carNB], w=[NMt])
                        kb.op("dve", lambda e: e.tensor_copy(out=carNB[:, 0:1], in_=NMt[:, SCH - 1:SCH]), r=[NMt], w=[carNB])
                        kb.op("dve", lambda e, isrc=isrc: e.tensor_tensor(out=A_[:, :], in0=isrc[:, :], in1=NMt[:, :], op=ALU.add),
                              r=[isrc, NMt], w=[A_])
                        Mf = t1 if d == 0 else gi
                        kb.op("dve", lambda e, Mf=Mf: e.tensor_tensor_scan(out=Mf[:, :], data0=ones4[:, :], data1=A_[:, :],
                                                                           initial=carM[:, 0:1], op0=ALU.mult, op1=ALU.max),
                              r=[ones4, A_, carM], w=[Mf])
                        kb.op("dve", lambda e, Mf=Mf: e.tensor_tensor(out=EM[:, :], in0=NMt[:, :], in1=Mf[:, :], op=ALU.subtract),
                              r=[NMt, Mf], w=[EM])
                        kb.op("act", lambda e: e.activation(out=EM[:, :], in_=EM[:, :], func=AF.Exp), r=[], w=[EM])
                        kb.op("dve", lambda e, Mf=Mf: e.tensor_scalar(out=NMt[:, :], in0=Mf[:, :], scalar1=-1.0, scalar2=None, op0=ALU.mult),
                              r=[Mf], w=[NMt])
                        WIf = gf if d == 0 else t1
                        for c in range(NCH):
                            c0 = c * 128
                            bprev = carM[:, 0:1] if c == 0 else Mf[:, c0 - 1:c0]
                            kb.op("act", lambda e, c0=c0, bprev=bprev, WIf=WIf: e.activation(out=WIf[:, c0:c0 + 128], in_=NMt[:, c0:c0 + 128],
                                                                                             func=AF.Exp, bias=bprev),
                                  r=[NMt, Mf, carM], w=[WIf])
                            kb.op("act", lambda e, c0=c0: e.activation(out=WS[:, c0:c0 + 128], in_=A_[:, c0:c0 + 128], func=AF.Exp,
                                                                       bias=NMt[:, c0 + 127:c0 + 128]), r=[A_, NMt], w=[WS])
                        kb.op("dve", lambda e, Mf=Mf: e.tensor_copy(out=carM[:, 0:1], in_=Mf[:, SCH - 1:SCH]), r=[Mf], w=[carM])
                        if d == 0:
                            kb.op("pool", lambda e, Mf=Mf: e.tensor_copy(out=Mo[:, :], in_=Mf[:, :]), r=[Mf], w=[Mo])
                            kb.op("pool", lambda e, WIf=WIf: e.tensor_copy(out=WIo[:, :], in_=WIf[:, :]), r=[WIf], w=[WIo])
                            srcs = (A_, WIo, EM, WS)
                        else:
                            kb.op("dve", lambda e, Mf=Mf: e.tensor_copy(out=Mo[:, :], in_=Mf[:, ::-1]), r=[Mf], w=[Mo])
                            kb.op("dve", lambda e, WIf=WIf: e.tensor_copy(out=WIo[:, :], in_=WIf[:, ::-1]), r=[WIf], w=[WIo])
                            kb.op("dve", lambda e: e.tensor_copy(out=t2[:, :], in_=A_[:, ::-1]), r=[A_], w=[t2])
                            kb.op("dve", lambda e: e.tensor_copy(out=A_[:, :], in_=EM[:, ::-1]), r=[EM], w=[A_])
                            kb.op("dve", lambda e: e.tensor_copy(out=EM[:, :], in_=WS[:, ::-1]), r=[WS], w=[EM])
                            srcs = (t2, WIo, A_, EM)
                        for kk, s_ in enumerate(srcs):
                            kb.dma(STK[32 * kk:32 * kk + 4, :], s_[:, :], r=[s_], w=[STK])
                        order = range(NCH) if d == 0 else range(NCH - 1, -1, -1)
                        for c in order:
                            c0 = c * 128
                            g0 = o0 + c0
                            ci = gci
                            gci += 1
                            sl = it % 2
                            it += 1
                            tqs, va, vm, km, ht = tq[sl], vas[sl], vms[sl], kms[sl], HT[sl]
                            kb.op("pe", lambda e, c0=c0: e.transpose(out=PS[0][:, 0:128], in_=STK[:, c0:c0 + 128], identity=identf[:]),
                                  r=[STK, identf], w=[PS[0]])
                            kb.op("act", lambda e, tqs=tqs: e.activation(
                                out=tqs.h.rearrange("p (k h) -> p k h", k=4),
                                in_=PS[0].h[:, 0:128].rearrange("p (k x) -> p k x", k=4)[:, :, 0:4], func=AF.Copy), r=[PS[0]], w=[tqs])
                            kb.dma(vm[:], sc["VM"].h[g0:g0 + 128, :], r=[sc["VM"]], w=[vm])
                            kb.dma(km[:], sc["KM"].h[g0:g0 + 128, :], r=[sc["KM"]], w=[km])
                            kb.op("pool", lambda e, va=va, vm=vm: e.tensor_copy(out=va[:, :, 0:64], in_=vm.h.rearrange("p (h x) -> p h x", h=4)),
                                  r=[vm], w=[va])
                            kb.op("pool", lambda e, va=va: e.memset(va[:, :, 64:65], 1.0), w=[va])
                            for h in range(4):
                                u = un % 2
                                un += 1
                                b_st, b_mb, b_ia = PS[1 + 3 * u], PS[2 + 3 * u], PS[3 + 3 * u]
                                b_ie, b_dc = b_ia, b_mb
                                kb.op("pe", lambda e, h=h, c0=c0, b_st=b_st: e.matmul(b_st[:, 0:128], lhsT=kT[:, h, c0:c0 + 128],
                                                                                       rhs=qT[:, h, c0:c0 + 128], start=True, stop=True),
                                      r=[kT, qT], w=[b_st])
                                kb.op("pe", lambda e, h=h, c0=c0, b_mb=b_mb: e.matmul(b_mb[:, 0:128], lhsT=selt[:, h, :],
                                                                                       rhs=Mo[:, c0:c0 + 128], start=True, stop=True),
                                      r=[selt, Mo], w=[b_mb])
                                Dt, Et, Pt = Dm[u], Em[u], PTm[u]
                                kb.op("dve", lambda e, h=h, d=d, Dt=Dt, tqs=tqs, b_mb=b_mb: e.scalar_tensor_tensor(
                                    out=Dt[:, :], in0=b_mb[:, 0:128], scalar=tqs[:, h:h + 1], in1=masks[d][:, :],
                                    op0=ALU.subtract, op1=ALU.max), r=[b_mb, tqs, masks[d]], w=[Dt])
                                kb.op("act", lambda e, Dt=Dt, Et=Et: e.activation(out=Et[:, :], in_=Dt[:, :], func=AF.Exp, scale=-1.0),
                                      r=[Dt], w=[Et])
                                kb.op("dve", lambda e, Et=Et, Pt=Pt, b_st=b_st: e.tensor_tensor(out=Pt[:, :], in0=b_st[:, 0:128], in1=Et[:, :],
                                                                                               op=ALU.mult), r=[b_st, Et], w=[Pt])
                                kb.op("pe", lambda e, h=h, Pt=Pt, va=va, b_ia=b_ia: e.matmul(b_ia[:, 0:65], lhsT=Pt[:, :], rhs=va[:, h, :],
                                                                                             start=True, stop=True), r=[Pt, va], w=[b_ia])
                                hn = HN[u]
                                if ci == 0:
                                    kb.op("act", lambda e, hn=hn, b_ia=b_ia: e.activation(out=hn[:, :], in_=b_ia[:, 0:65], func=AF.Copy),
                                          r=[b_ia], w=[hn])
                                else:
                                    ia = intra[u]
                                    kb.op("act", lambda e, ia=ia, b_ia=b_ia: e.activation(out=ia[:, :], in_=b_ia[:, 0:65], func=AF.Copy),
                                          r=[b_ia], w=[ia])
                                    kb.op("pe", lambda e, h=h, c0=c0, b_ie=b_ie: e.matmul(b_ie[:, 128:193], lhsT=qT[:, h, c0:c0 + 128],
                                                                                           rhs=CNb[h][:, :], start=True, stop=True),
                                          r=[qT, CNb[h]], w=[b_ie])
                                    kb.op("dve", lambda e, h=h, hn=hn, ia=ia, tqs=tqs, b_ie=b_ie: e.scalar_tensor_tensor(
                                        out=hn[:, :], in0=b_ie[:, 128:193], scalar=tqs[:, 4 + h:5 + h], in1=ia[:, :],
                                        op0=ALU.mult, op1=ALU.add), r=[b_ie, tqs, ia], w=[hn])
                                dnt = dn[u]
                                kb.op("dve", lambda e, hn=hn, dnt=dnt: e.scalar_tensor_tensor(
                                    out=dnt[:, 0:1], in0=hn[:, 64:65], scalar=-1.0, in1=hn[:, 64:65],
                                    op0=ALU.mult, op1=ALU.max), r=[hn], w=[dnt])
                                kb.op("dve", lambda e, h=h, dnt=dnt, tqs=tqs: e.tensor_scalar(
                                    out=dnt[:, 0:1], in0=dnt[:, 0:1], scalar1=tqs[:, 8 + h:9 + h], scalar2=None,
                                    op0=ALU.max), r=[tqs], w=[dnt])
                                kb.op("dve", lambda e, dnt=dnt: e.reciprocal(out=dnt[:, 1:2], in_=dnt[:, 0:1]), r=[], w=[dnt])
                                kb.op("dve", lambda e, h=h, hn=hn, dnt=dnt, ht=ht: e.tensor_scalar(
                                    out=ht[:, h * 64:(h + 1) * 64], in0=hn[:, 0:64], scalar1=dnt[:, 1:2], scalar2=None, op0=ALU.mult),
                                    r=[hn, dnt], w=[ht])
                                if ci < NCT - 1:
                                    kw = KW[u]
                                    kb.op("pool", lambda e, h=h, kw=kw, km=km, tqs=tqs: e.tensor_scalar(
                                        out=kw[:, :], in0=km[:, h * 64:(h + 1) * 64], scalar1=tqs[:, 12 + h:13 + h], scalar2=None,
                                        op0=ALU.mult), r=[km, tqs], w=[kw])
                                    kb.op("pe", lambda e, h=h, kw=kw, va=va, b_dc=b_dc: e.matmul(b_dc[0:64, 128:193], lhsT=kw[:, :],
                                                                                                 rhs=va[:, h, :], start=True, stop=True),
                                          r=[kw, va], w=[b_dc])
                                    if ci == 0:
                                        kb.op("act", lambda e, h=h, b_dc=b_dc: e.activation(out=CN[h][:, :], in_=b_dc[0:64, 128:193],
                                                                                            func=AF.Copy), r=[b_dc], w=[CN[h]])
                                    else:
                                        col = c0 + 127 if d == 0 else c0
                                        dct = dcs[u]
                                        kb.op("pe", lambda e, h=h, col=col, b_dc=b_dc: e.matmul(
                                            b_dc[0:64, 200:201], lhsT=selt[:, h, 0:64], rhs=WIo[:, col:col + 1], start=True, stop=True),
                                            r=[selt, WIo], w=[b_dc])
                                        kb.op("dve", lambda e, dct=dct, b_dc=b_dc: e.tensor_copy(out=dct[:, 0:1], in_=b_dc[0:64, 200:201]),
                                              r=[b_dc], w=[dct])
                                        kb.op("dve", lambda e, h=h, dct=dct, b_dc=b_dc: e.scalar_tensor_tensor(
                                            out=CN[h][:, :], in0=CN[h][:, :], scalar=dct[:, 0:1], in1=b_dc[0:64, 128:193],
                                            op0=ALU.mult, op1=ALU.add), r=[b_dc, dct], w=[CN[h]])
                                    kb.op("act", lambda e, h=h: e.activation(out=CNb[h][:, :], in_=CN[h][:, :], func=AF.Copy),
                                          r=[CN[h]], w=[CNb[h]])
                            if d == 0:
                                kb.dma(sc["HF"].h[g0:g0 + 128, :], ht[:, :], r=[ht], w=[sc["HF"]])
                            else:
                                hf, om, ybt, yTt = hfl[sl], oml[sl], yb_[sl], yT[sl]
                                kb.dma(hf[:], sc["HF"].h[g0:g0 + 128, :], r=[sc["HF"]], w=[hf])
                                kb.dma(om[:], sc["OM"].h[g0:g0 + 128, :], r=[sc["OM"]], w=[om])
                                kb.op("dve", lambda e, hf=hf, ht=ht: e.tensor_tensor(out=hf[:, :], in0=hf[:, :], in1=ht[:, :], op=ALU.add),
                                      r=[ht], w=[hf])
                                for h in range(4):
                                    kb.op("dve", lambda e, h=h, hf=hf: e.bn_stats(out=gst[:, h, :], in_=hf[:, h * 64:(h + 1) * 64]),
                                          r=[hf], w=[gst])
                                    kb.op("dve", lambda e, h=h: e.bn_aggr(out=gmv[:, h, :], in_=gst[:, h, :]), r=[gst], w=[gmv])
                                rsqrt_eps(grs, grs[:, :], gmv[:, :, 1], [gmv])
                                for h in range(4):
                                    kb.op("dve", lambda e, h=h, hf=hf: e.tensor_scalar(
                                        out=hf[:, h * 64:(h + 1) * 64], in0=hf[:, h * 64:(h + 1) * 64], scalar1=gmv[:, h, 0:1],
                                        scalar2=grs[:, h:h + 1], op0=ALU.subtract, op1=ALU.mult), r=[gmv, grs], w=[hf])
                                kb.op("act", lambda e, om=om: e.activation(out=om[:, :], in_=om[:, :], func=AF.Sigmoid), r=[], w=[om])
                                kb.op("pool", lambda e, hf=hf: e.tensor_tensor(out=hf[:, :], in0=hf[:, :], in1=repn[:, :], op=ALU.mult),
                                      r=[repn], w=[hf])
                                kb.op("dve", lambda e, hf=hf, om=om, ybt=ybt: e.tensor_tensor(out=ybt[:, :], in0=hf[:, :], in1=om[:, :],
                                                                                             op=ALU.mult), r=[hf, om], w=[ybt])
                                for cc in range(2):
                                    kb.op("pe", lambda e, cc=cc, ybt=ybt: e.transpose(out=PSB[:, cc * 128:(cc + 1) * 128],
                                                                                      in_=ybt[:, cc * 128:(cc + 1) * 128], identity=identb[:]),
                                          r=[ybt, identb], w=[PSB])
                                kb.op("act", lambda e, yTt=yTt: e.activation(out=yTt[:, :], in_=PSB[:, 0:256], func=AF.Copy), r=[PSB], w=[yTt])
                                kb.dma(sc["MIXT"].h[768:1024, :].rearrange("(c p) s -> p c s", p=128)[:, :, g0:g0 + 128],
                                       yTt.h.rearrange("p (c t) -> p c t", c=2), r=[yTt], w=[sc["MIXT"]])
                kb.barrier()

        if dbg and l == 0:
            dbg_out = T(nc.dram_tensor("dbg_mixt", [D, jobs[0]], BF16, kind="ExternalOutput").ap())
            with ExitStack() as es:
                dt_ = sb(es, "dbgt", [128, 8, jobs[0]], BF16)
                kb.dma(dt_[:], SC[0]["MIXT"].h.rearrange("(c p) s -> p c s", p=128), r=[SC[0]["MIXT"]], w=[dt_])
                kb.dma(dbg_out.h.rearrange("(c p) s -> p c s", p=128), dt_[:], r=[dt_], w=[dbg_out])
                kb.barrier()
        alpha = float(8.0 ** 0.25)
        with ExitStack() as es:
            std_psum(es)
            stg = [sb(es, "stg%d" % i, [128, 2048], F32) for i in range(2)]
            wo = sb(es, "wo", [128, 8, D], BF16)
            load_w_bf16(es, wo, lambda c, n0, n1: wo[:, c, n0:n1], w_out[l], 8, D, stg)
            rl = sb(es, "rl", [128, 2 * D], F32)
            kb.dma(rl[:], rep_l[l, :, 0:2 * D], w=[rl])
            tmp = {"st6": sb(es, "st6", [128, 2, 6], F32), "mv": sb(es, "mv", [128, 2], F32), "rs": sb(es, "rs", [128, 1], F32)}
            mts = [sb(es, "mt%d" % i, [128, 8, 512], BF16) for i in range(2)]
            xres = [sb(es, "xres%d" % i, [128, D], F32) for i in range(2)]
            z = [sb(es, "z%d" % i, [128, D], F32) for i in range(2)]
            x1f = [sb(es, "x1f%d" % i, [128, D], F32) for i in range(2)]
            ob = [sb(es, "ob%d" % i, [128, D], BF16) for i in range(2)]
            xts = [sb(es, "xts%d" % i, [128, D], BF16) for i in range(2)]
            bi = 0
            for j, S in enumerate(jobs):
                sc = SC[j]
                XNi, XNo = sc["XN"][0], sc["XN"][1]
                for blk in range(S // 512):
                    t0 = blk * 512
                    mt = mts[bi % 2]
                    bi += 1
                    kb.dma(mt[:], sc["MIXT"].h.rearrange("(c p) s -> p c s", p=128)[:, :, t0:t0 + 512], r=[sc["MIXT"]], w=[mt])
                    for tt in range(4):
                        tk0 = t0 + tt * 128
                        xr, zt, obt, x1, xt_ = xres[tt % 2], z[tt % 2], ob[tt % 2], x1f[tt % 2], xts[tt % 2]
                        kb.dma(xr[:], XNi.h[tk0:tk0 + 128, :], r=[XNi], w=[xr])
                        for nb in range(2):
                            for c in range(8):
                                kb.op("pe", lambda e, c=c, nb=nb, tt=tt: e.matmul(PS[nb + 2 * (tt % 2)][:, :], lhsT=mt[:, c, tt * 128:(tt + 1) * 128],
                                                                                   rhs=wo[:, c, nb * 512:(nb + 1) * 512],
                                                                                   start=(c == 0), stop=(c == 7)), r=[mt, wo], w=[PS[nb + 2 * (tt % 2)]])
                            kb.op("dve", lambda e, nb=nb, xr=xr, zt=zt, tt=tt: e.scalar_tensor_tensor(
                                out=zt[:, nb * 512:(nb + 1) * 512], in0=xr[:, nb * 512:(nb + 1) * 512], scalar=alpha,
                                in1=PS[nb + 2 * (tt % 2)][:, :], op0=ALU.mult, op1=ALU.add), r=[xr, PS[nb + 2 * (tt % 2)]], w=[zt])
                        layer_norm_tile(zt, rl, rl, 0, D, x1, obt, tmp)
                        kb.dma(XNo.h[tk0:tk0 + 128, :], x1[:], r=[x1], w=[XNo])
                        transpose_to_xt(obt, xt_, sc["XT"], tk0)
            kb.barrier()
        with ExitStack() as es:
            std_psum(es)
            stg = [sb(es, "stg%d" % i, [128, 2048], F32) for i in range(2)]
            wg = sb(es, "wg", [128, 8, DFF], BF16)
            wu = sb(es, "wu", [128, 8, DFF], BF16)
            load_w_bf16(es, wg, lambda c, n0, n1: wg[:, c, n0:n1], w_gate[l], 8, DFF, stg)
            load_w_bf16(es, wu, lambda c, n0, n1: wu[:, c, n0:n1], w_up[l], 8, DFF, stg)
            x1Ts = [sb(es, "x1T%d" % i, [128, 8, 512], BF16) for i in range(2)]
            sg = [sb(es, "sg%d" % i, [128, 512], F32) for i in range(2)]
            hd = [sb(es, "hd%d" % i, [128, 512], BF16) for i in range(2)]
            bi = 0
            for j, S in enumerate(jobs):
                sc = SC[j]
                for blk in range(S // 512):
                    t0 = blk * 512
                    x1T = x1Ts[bi % 2]
                    bi += 1
                    kb.dma(x1T[:], sc["XT"].h.rearrange("(c p) s -> p c s", p=128)[:, :, t0:t0 + 512], r=[sc["XT"]], w=[x1T])
                    for f in range(NFC):
                        bg, bu = PS[2 * (f % 2)], PS[1 + 2 * (f % 2)]
                        for c in range(8):
                            kb.op("pe", lambda e, c=c, f=f, bg=bg: e.matmul(bg[:, :], lhsT=wg[:, c, f * 128:(f + 1) * 128], rhs=x1T[:, c, :],
                                                                            start=(c == 0), stop=(c == 7)), r=[wg, x1T], w=[bg])
                        for c in range(8):
                            kb.op("pe", lambda e, c=c, f=f, bu=bu: e.matmul(bu[:, :], lhsT=wu[:, c, f * 128:(f + 1) * 128], rhs=x1T[:, c, :],
                                                                            start=(c == 0), stop=(c == 7)), r=[wu, x1T], w=[bu])
                        sgt, hdt = sg[f % 2], hd[f % 2]
                        kb.op("act", lambda e, sgt=sgt, bg=bg: e.activation(out=sgt[:, :], in_=bg[:, :], func=AF.Silu), r=[bg], w=[sgt])
                        kb.op("dve", lambda e, hdt=hdt, sgt=sgt, bu=bu: e.tensor_tensor(out=hdt[:, :], in0=bu[:, :], in1=sgt[:, :], op=ALU.mult),
                              r=[bu, sgt], w=[hdt])
                        kb.dma(sc["HIDT"].h[f * 128:(f + 1) * 128, t0:t0 + 512], hdt[:, :], r=[hdt], w=[sc["HIDT"]])
            kb.barrier()
        with ExitStack() as es:
            std_psum(es)
            stg = [sb(es, "stg%d" % i, [128, 2048], F32) for i in range(2)]
            wd = sb(es, "wd", [128, NFC, D], BF16)
            load_w_bf16(es, wd, lambda c, n0, n1: wd[:, c, n0:n1], w_down[l], NFC, D, stg)
            rl = sb(es, "rl", [128, 2 * D], F32)
            kb.dma(rl[:], rep_l[l, :, 2 * D:4 * D], w=[rl])
            tmp = {"st6": sb(es, "st6", [128, 2, 6], F32), "mv": sb(es, "mv", [128, 2], F32), "rs": sb(es, "rs", [128, 1], F32)}
            HIDs = [sb(es, "HID%d" % i, [128, NFC, 512], BF16) for i in range(2)]
            x1f = [sb(es, "x1f%d" % i, [128, D], F32) for i in range(2)]
            z = [sb(es, "z%d" % i, [128, D], F32) for i in range(2)]
            of2 = [sb(es, "of2%d" % i, [128, D], F32) for i in range(2)]
            ob = [sb(es, "ob%d" % i, [128, D], BF16) for i in range(2)]
            xts = [sb(es, "xts%d" % i, [128, D], BF16) for i in range(2)]
            bi = 0
            for j, S in enumerate(jobs):
                sc = SC[j]
                X1, XNo = sc["XN"][1], sc["XN"][0]
                for blk in range(S // 512):
                    t0 = blk * 512
                    HID = HIDs[bi % 2]
                    bi += 1
                    kb.dma(HID[:], sc["HIDT"].h.rearrange("(f p) s -> p f s", p=128)[:, :, t0:t0 + 512], r=[sc["HIDT"]], w=[HID])
                    for tt in range(4):
                        tk0 = t0 + tt * 128
                        zt, obt, x1, o2, xt_ = z[tt % 2], ob[tt % 2], x1f[tt % 2], of2[tt % 2], xts[tt % 2]
                        kb.dma(x1[:], X1.h[tk0:tk0 + 128, :], r=[X1], w=[x1])
                        for nb in range(2):
                            for f in range(NFC):
                                kb.op("pe", lambda e, f=f, nb=nb, tt=tt: e.matmul(PS[nb + 2 * (tt % 2)][:, :], lhsT=HID[:, f, tt * 128:(tt + 1) * 128],
                                                                                   rhs=wd[:, f, nb * 512:(nb + 1) * 512],
                                                                                   start=(f == 0), stop=(f == NFC - 1)), r=[HID, wd], w=[PS[nb + 2 * (tt % 2)]])
                            kb.op("dve", lambda e, nb=nb, x1=x1, zt=zt, tt=tt: e.scalar_tensor_tensor(
                                out=zt[:, nb * 512:(nb + 1) * 512], in0=x1[:, nb * 512:(nb + 1) * 512], scalar=alpha,
                                in1=PS[nb + 2 * (tt % 2)][:, :], op0=ALU.mult, op1=ALU.add), r=[x1, PS[nb + 2 * (tt % 2)]], w=[zt])
                        layer_norm_tile(zt, rl, rl, 0, D, o2, obt, tmp)
                        if last:
                            kb.dma(yout[j].h[tk0:tk0 + 128, :], o2[:], r=[o2], w=[yout[j]])
                        else:
                            kb.dma(XNo.h[tk0:tk0 + 128, :], o2[:], r=[o2], w=[XNo])
                            transpose_to_xt(obt, xt_, sc["XT"], tk0)
            kb.barrier()
    kb.barrier()
    es_glob.close()
    return nc


def _tables(jobs, smax):
    half = 16
    inv = 1.0 / (10000.0 ** (np.arange(0, 32, 2, dtype=np.float32) / 32.0))
    ang = np.arange(smax, dtype=np.float32)[:, None] * inv[None, :].astype(np.float32)
    cos = np.cos(ang).astype(np.float32).T
    sin = np.sin(ang).astype(np.float32).T
    cos2 = np.concatenate([cos, cos], 0)
    sin2 = np.concatenate([-sin, sin], 0)
    pedge = np.zeros((len(jobs), 128, 2, 16), np.float32)
    for j, S in enumerate(jobs):
        for g, w in enumerate((2, 4, 8, 16)):
            c, p0 = g // 2, (g % 2) * 64
            for i in range(8):
                t = i
                cnt = min(t + w // 2, S) - max(t - w // 2, 0)
                pedge[j, p0:p0 + 64, c, i] = 1.0 / cnt
                t = S - 8 + i
                cnt = min(t + w // 2, S) - max(t - w // 2, 0)
                pedge[j, p0:p0 + 64, c, 8 + i] = 1.0 / cnt
    s_ = np.arange(128)[:, None]
    j_ = np.arange(128)[None, :]
    maskf = np.where(s_ <= j_, 0.0, BIG).astype(np.float32)
    maskb = np.where(s_ >= j_, 0.0, BIG).astype(np.float32)
    sel = np.zeros((4, 4, 128), np.float32)
    for h in range(4):
        sel[h, h, :] = 1.0
    return dict(cos2=np.ascontiguousarray(cos2), sin2=np.ascontiguousarray(sin2), pedge=pedge, maskf=maskf, maskb=maskb,
                sel=sel, ident=np.eye(128, dtype=np.float32))


def _common_inputs(inp, L, jobs, smax):
    f = lambda a: np.ascontiguousarray(np.asarray(a, dtype=np.float32))
    rep = lambda v: np.broadcast_to(f(v)[None, :], (128, f(v).shape[0]))
    m = {}
    for k in ("w_in", "w_uq", "w_ukv", "w_pool", "w_out", "w_gate", "w_up", "w_down"):
        m[k] = f(inp[k])[:L]
    m["rep_in"] = np.ascontiguousarray(np.concatenate([rep(inp["ln_in_g"]), rep(inp["ln_in_b"])], 1))
    m["rep_l"] = np.ascontiguousarray(np.stack([
        np.concatenate([rep(inp["ln1_g"][l]), rep(inp["ln1_b"][l]), rep(inp["ln2_g"][l]), rep(inp["ln2_b"][l]),
                        rep(inp["mlstm_norm_g"][l])], 1) for l in range(L)]))
    m["qg"] = np.ascontiguousarray(f(inp["q_norm_g"])[:L].reshape(L, 2, 128).transpose(0, 2, 1))
    m["kvg"] = np.ascontiguousarray(f(inp["kv_norm_g"])[:L].reshape(L, 128, 1))
    m["psc"] = np.ascontiguousarray(f(inp["pool_scale"])[:L].reshape(L, 4, 64).transpose(0, 2, 1))
    m["gbias"] = np.ascontiguousarray(f(inp["mlstm_gate_bias"])[:L].reshape(L, 4, 4).transpose(0, 2, 1))
    m.update(_tables(jobs, smax))
    return m


_CACHE = {}


def run(inp, L, job_inputs_per_core, jobs, dbg=False):
    smax = max(jobs)
    key = (L, tuple(jobs))
    if key not in _CACHE:
        _CACHE[key] = build_program(L, jobs, smax, dbg)
    nc = _CACHE[key]
    common = _common_inputs(inp, L, jobs, smax)
    in_maps = []
    for xs in job_inputs_per_core:
        m = dict(common)
        for j, x in enumerate(xs):
            m["xin%d" % j] = np.ascontiguousarray(x, dtype=np.float32)
        in_maps.append(m)
    res = run_bass_kernel_spmd(nc, in_maps, core_ids=list(range(len(in_maps))))
    if dbg:
        return [[r["yout%d" % j] for j in range(len(jobs))] + [r["dbg_mixt"]] for r in res.results]
    return [[r["yout%d" % j] for j in range(len(jobs))] for r in res.results]


def kernel(**inputs):
    xp = np.asarray(inputs["x_prompt"], dtype=np.float32)
    xs = np.asarray(inputs["x_sample"], dtype=np.float32)
    L = int(np.asarray(inputs["w_in"]).shape[0])
    jobs = [xp.shape[1], xs.shape[1], xs.shape[1]]
    per_core = [[xp[c % 2], xs[2 * c], xs[2 * c + 1]] for c in range(8)]
    outs = run(inputs, L, per_core, jobs)
    y_prompt = np.stack([outs[0][0], outs[1][0]], 0).astype(np.float32)
    y_sample = np.stack([outs[c][1 + i] for c in range(8) for i in range(2)], 0).astype(np.float32)
    return (y_prompt, y_sample)
```

```python
import numpy as np
from contextlib import ExitStack
import concourse.bass as bass
import concourse.mybir as mybir
from concourse.bass_utils import run_bass_kernel_spmd

F32 = mybir.dt.float32
BF16 = mybir.dt.bfloat16
AF = mybir.ActivationFunctionType
ALU = mybir.AluOpType

D = 1024
NQ, NKV, NR, NP_, NM, NG = 256, 128, 32, 256, 256, 16
IN_W = 1712
DFF = 2816
NFC = DFF // 128
EPS = 1e-5
BIG = 1e30
SAME_ENG_SYNC = True


class Buf:
    __slots__ = ("w", "r")

    def __init__(self):
        self.w = None
        self.r = {}


class T:
    def __init__(self, h):
        self.h = h
        self.b = Buf()

    def __getitem__(self, idx):
        return self.h[idx]


class KB:
    ND = 8

    def __init__(self, nc):
        self.nc = nc
        self.E = {"pe": nc.tensor, "act": nc.scalar, "dve": nc.vector, "pool": nc.gpsimd, "sp": nc.sync}
        self.sem = {}
        self.val = {}
        for e in self.E:
            self.sem[e] = nc.alloc_semaphore(name="c_" + e)
            self.val[e] = 0
        for q in ("sp", "pool"):
            for i in range(self.ND):
                self.sem[(q, i)] = nc.alloc_semaphore(name="d_%s%d" % (q, i))
                self.val[(q, i)] = 0
        self.dslot = {"sp": 0, "pool": 0}
        self.waited = {e: {} for e in self.E}
        self.pending = []

    def _wait(self, e, sk, v):
        if v <= 0:
            return
        if self.waited[e].get(sk, 0) < v:
            self.E[e].wait_ge(self.sem[sk], v)
            self.waited[e][sk] = v

    def _deps(self, e, r, w):
        deps = {}
        for t in list(r) + list(w):
            ev = t.b.w
            if ev is not None:
                deps[ev[0]] = max(deps.get(ev[0], 0), ev[1])
        for t in w:
            for sk, v in t.b.r.items():
                deps[sk] = max(deps.get(sk, 0), v)
        for sk, v in deps.items():
            if sk == e and (e == "pe" or not SAME_ENG_SYNC):
                continue
            self._wait(e, sk, v)

    def _mark(self, ev, r, w):
        for t in r:
            t.b.r[ev[0]] = max(t.b.r.get(ev[0], 0), ev[1])
        for t in w:
            t.b.w = ev
            t.b.r = {}

    def op(self, e, fn, r=(), w=()):
        self._deps(e, r, w)
        ins = fn(self.E[e])
        self.val[e] += 1
        ins.then_inc(self.sem[e], 1)
        self._mark((e, self.val[e]), r, w)

    def dma(self, out, in_, r=(), w=(), q="sp"):
        self._deps(q, r, w)
        slot = self.dslot[q]
        self.dslot[q] = (slot + 1) % self.ND
        sk = (q, slot)
        self._wait(q, sk, self.val[sk])
        ins = self.E[q].dma_start(out=out, in_=in_)
        self.val[sk] += 16
        ins.then_inc(self.sem[sk], 16)
        self._mark((sk, self.val[sk]), r, w)

    def dma_later(self, out, in_, r=(), w=()):
        self.pending.append((out, in_, r, w))

    def flush(self):
        p, self.pending = self.pending, []
        for (out, in_, r, w) in p:
            self.dma(out, in_, r=r, w=w)

    def barrier(self):
        self.flush()
        for e in self.E:
            for sk in self.sem:
                if sk != e:
                    self._wait(e, sk, self.val[sk])


def build_program(L, jobs, smax, dbg=False):
    nc = bass.Bass("TRN2", target_bir_lowering=False)
    kb = KB(nc)
    NJ = len(jobs)

    def din(name, shape, dt=F32):
        return nc.dram_tensor(name, list(shape), dt, kind="ExternalInput").ap()

    def dscr(name, shape, dt):
        return T(nc.dram_tensor(name, list(shape), dt).ap())

    xin = [din("xin%d" % j, [S, D]) for j, S in enumerate(jobs)]
    yout = [T(nc.dram_tensor("yout%d" % j, [S, D], F32, kind="ExternalOutput").ap()) for j, S in enumerate(jobs)]
    w_in = din("w_in", [L, D, IN_W])
    w_uq = din("w_uq", [L, NQ, 768])
    w_ukv = din("w_ukv", [L, NKV, 1024])
    w_pool = din("w_pool", [L, 4, 64, 64])
    w_out = din("w_out", [L, D, D])
    w_gate = din("w_gate", [L, D, DFF])
    w_up = din("w_up", [L, D, DFF])
    w_down = din("w_down", [L, DFF, D])
    rep_in = din("rep_in", [128, 2 * D])
    rep_l = din("rep_l", [L, 128, 4 * D + 256])
    qg = din("qg", [L, 128, 2])
    kvg = din("kvg", [L, 128, 1])
    psc = din("psc", [L, 64, 4])
    gbias = din("gbias", [L, 4, 4])
    cos2 = din("cos2", [32, smax])
    sin2 = din("sin2", [32, smax])
    pedge = din("pedge", [NJ, 128, 2, 16])
    maskf_d = din("maskf", [128, 128])
    maskb_d = din("maskb", [128, 128])
    sel_d = din("sel", [4, 4, 128])
    ident_d = din("ident", [128, 128])

    SC = []
    for j, S in enumerate(jobs):
        s = {}
        s["XN"] = [dscr("XN%d_%d" % (i, j), [S, D], F32) for i in range(2)]
        s["XT"] = dscr("XT%d" % j, [D, S], BF16)
        s["CQT"] = dscr("CQT%d" % j, [NQ, S], BF16)
        s["CKVT"] = dscr("CKVT%d" % j, [NKV, S], BF16)
        s["KRT"] = dscr("KRT%d" % j, [32, S], BF16)
        s["POOLT"] = dscr("POOLT%d" % j, [256, S], F32)
        s["QMT"] = dscr("QMT%d" % j, [256, S], BF16)
        s["KMT"] = dscr("KMT%d" % j, [256, S], BF16)
        s["G4"] = dscr("G4%d" % j, [4, 4, S], F32)
        s["VM"] = dscr("VM%d" % j, [S, 256], BF16)
        s["KM"] = dscr("KM%d" % j, [S, 256], BF16)
        s["OM"] = dscr("OM%d" % j, [S, 256], F32)
        s["MIXT"] = dscr("MIXT%d" % j, [D, S], BF16)
        s["HF"] = dscr("HF%d" % j, [S, 256], F32)
        s["HIDT"] = dscr("HIDT%d" % j, [DFF, S], BF16)
        SC.append(s)

    es_glob = ExitStack()
    uniq = [0]

    def sb(es, name, shape, dt):
        uniq[0] += 1
        return T(es.enter_context(nc.sbuf_tensor("%s_%d" % (name, uniq[0]), list(shape), dt)))

    PS = [None] * 7
    PSBh = [None]

    class _PSB:
        @property
        def h(self):
            return PSBh[0].h

        @property
        def b(self):
            return PSBh[0].b

        def __getitem__(self, idx):
            return PSBh[0].h[idx]

    PSB = _PSB()

    def std_psum(es):
        for i in range(7):
            uniq[0] += 1
            PS[i] = T(es.enter_context(nc.psum_tensor("ps%d_%d" % (i, uniq[0]), [128, 512], F32)))
        uniq[0] += 1
        PSBh[0] = T(es.enter_context(nc.psum_tensor("psb_%d" % uniq[0], [128, 1024], BF16)))

    identf = sb(es_glob, "identf", [128, 128], F32)
    identb = sb(es_glob, "identb", [128, 128], BF16)
    onesf = sb(es_glob, "onesf", [128, 128], F32)
    onesb = sb(es_glob, "onesb", [128, 128], BF16)
    kb.dma(identf[:], ident_d[:, :], w=[identf])
    kb.op("dve", lambda e: e.tensor_copy(out=identb[:], in_=identf[:]), r=[identf], w=[identb])
    kb.op("dve", lambda e: e.memset(onesf[:], 1.0), w=[onesf])
    kb.op("dve", lambda e: e.memset(onesb[:], 1.0), w=[onesb])

    stg_ctr = [0]

    def load_w_bf16(es_stage, dst, dst_ap_fn, src_ap, nk, ncols, stg):
        CW = 2048
        for c in range(nk):
            for n0 in range(0, ncols, CW):
                n1 = min(ncols, n0 + CW)
                st = stg[stg_ctr[0] % len(stg)]
                stg_ctr[0] += 1
                kb.dma(st[:, 0:n1 - n0], src_ap[c * 128:(c + 1) * 128, n0:n1], w=[st])
                eng = "pool" if (stg_ctr[0] % 2) else "dve"
                kb.op(eng, lambda e, st=st, c=c, n0=n0, n1=n1: e.tensor_copy(out=dst_ap_fn(c, n0, n1), in_=st[:, 0:n1 - n0]),
                      r=[st], w=[dst])

    def rsqrt_eps(t, out_ap, in_ap, rd, scale=1.0):
        kb.op("dve", lambda e: e.tensor_scalar(out=out_ap, in0=in_ap, scalar1=scale, scalar2=EPS, op0=ALU.mult, op1=ALU.add),
              r=rd, w=[t])
        kb.op("act", lambda e: e.activation(out=out_ap, in_=out_ap, func=AF.Sqrt), r=[], w=[t])
        kb.op("dve", lambda e: e.reciprocal(out=out_ap, in_=out_ap), r=[], w=[t])

    def layer_norm_tile(z, gt, bt, g_off, b_off, outf, outb, tmp):
        st6, mv, rs = tmp["st6"], tmp["mv"], tmp["rs"]
        for hh in range(2):
            kb.op("dve", lambda e, hh=hh: e.bn_stats(out=st6[:, hh, :], in_=z[:, hh * 512:(hh + 1) * 512]), r=[z], w=[st6])
        kb.op("dve", lambda e: e.bn_aggr(out=mv[:, :], in_=st6[:, :, :]), r=[st6], w=[mv])
        rsqrt_eps(rs, rs[:, :], mv[:, 1:2], [mv])
        kb.op("dve", lambda e: e.tensor_scalar(out=outf[:, :], in0=z[:, :], scalar1=mv[:, 0:1], scalar2=rs[:, 0:1],
                                               op0=ALU.subtract, op1=ALU.mult), r=[z, mv, rs], w=[outf])
        kb.op("pool", lambda e: e.tensor_tensor(out=outf[:, :], in0=outf[:, :], in1=gt[:, g_off:g_off + D], op=ALU.mult),
              r=[gt], w=[outf])
        kb.op("dve", lambda e: e.tensor_tensor(out=outf[:, :], in0=outf[:, :], in1=bt[:, b_off:b_off + D], op=ALU.add),
              r=[bt], w=[outf])
        kb.op("act", lambda e: e.activation(out=outb[:, :], in_=outf[:, :], func=AF.Copy), r=[outf], w=[outb])

    def transpose_to_xt(outb, xts, XT_T, tok0):
        for c in range(8):
            kb.op("pe", lambda e, c=c: e.transpose(out=PSB[:, c * 128:(c + 1) * 128], in_=outb[:, c * 128:(c + 1) * 128],
                                                   identity=identb[:]), r=[outb, identb], w=[PSB])
        kb.op("act", lambda e: e.activation(out=xts[:, :], in_=PSB[:, :], func=AF.Copy), r=[PSB], w=[xts])
        kb.dma_later(XT_T.h.rearrange("(c p) s -> p c s", p=128)[:, :, tok0:tok0 + 128],
                     xts.h.rearrange("p (c t) -> p c t", c=8), r=[xts], w=[XT_T])

    with ExitStack() as es:
        std_psum(es)
        rin = sb(es, "rin", [128, 2 * D], F32)
        kb.dma(rin[:], rep_in[:, :], w=[rin])
        tmp = {"st6": sb(es, "st6", [128, 2, 6], F32), "mv": sb(es, "mv", [128, 2], F32), "rs": sb(es, "rs", [128, 1], F32)}
        zs = [sb(es, "pz%d" % i, [128, D], F32) for i in range(2)]
        ofs = [sb(es, "pof%d" % i, [128, D], F32) for i in range(2)]
        obs = [sb(es, "pob%d" % i, [128, D], BF16) for i in range(2)]
        xtss = [sb(es, "pxt%d" % i, [128, D], BF16) for i in range(2)]
        it = 0
        for j, S in enumerate(jobs):
            for t in range(S // 128):
                z, of, ob, xts = zs[it % 2], ofs[it % 2], obs[it % 2], xtss[it % 2]
                it += 1
                kb.dma(z[:], xin[j][t * 128:(t + 1) * 128, :], w=[z])
                kb.flush()
                layer_norm_tile(z, rin, rin, 0, D, of, ob, tmp)
                kb.dma_later(SC[j]["XN"][0].h[t * 128:(t + 1) * 128, :], of[:], r=[of], w=[SC[j]["XN"][0]])
                transpose_to_xt(ob, xts, SC[j]["XT"], t * 128)
        kb.barrier()

    for l in range(L):
        last = l == L - 1
        with ExitStack() as es:
            std_psum(es)
            stg = [sb(es, "stg%d" % i, [128, 2048], F32) for i in range(2)]
            win = sb(es, "win", [128, 8, IN_W], BF16)
            load_w_bf16(es, win, lambda c, n0, n1: win[:, c, n0:n1], w_in[l], 8, IN_W, stg)
            wkr_sw = sb(es, "wkrsw", [128, 8, 96], BF16)
            kb.op("dve", lambda e: e.tensor_copy(out=wkr_sw[:, :, 0:64], in_=win[:, :, 320:384]), r=[win], w=[wkr_sw])
            kb.op("dve", lambda e: e.tensor_copy(out=wkr_sw[:, :, 64:80], in_=win[:, :, 400:416]), r=[win], w=[wkr_sw])
            kb.op("dve", lambda e: e.tensor_copy(out=wkr_sw[:, :, 80:96], in_=win[:, :, 384:400]), r=[win], w=[wkr_sw])
            qgt = sb(es, "qgt", [128, 2], F32)
            kvgt = sb(es, "kvgt", [128, 1], F32)
            kb.dma(qgt[:], qg[l], w=[qgt])
            kb.dma(kvgt[:], kvg[l], w=[kvgt])
            xTs = [sb(es, "xT%d" % i, [128, 8, 512], BF16) for i in range(2)]
            cst = [sb(es, "cs%d" % i, [96, 512], F32) for i in range(2)]
            snt = [sb(es, "sn%d" % i, [96, 512], F32) for i in range(2)]
            sq = sb(es, "sq", [128, 512], BF16)
            rstd = sb(es, "rstd", [128, 512], F32)
            ev = [sb(es, "ev%d" % i, [128, 512], BF16) for i in range(2)]
            evf = [sb(es, "evf%d" % i, [128, 512], F32) for i in range(2)]
            kr1 = sb(es, "kr1", [96, 512], F32)
            kr2 = sb(es, "kr2", [96, 512], F32)
            krb = sb(es, "krb", [96, 512], BF16)
            tmb = [sb(es, "tmb%d" % i, [128, 512], BF16) for i in range(2)]
            tmf = [sb(es, "tmf%d" % i, [128, 256], F32) for i in range(2)]
            evc = 0
            for j, S in enumerate(jobs):
                sc = SC[j]
                for blk in range(S // 512):
                    t0 = blk * 512
                    xT = xTs[blk % 2]
                    kb.dma(xT[:], sc["XT"].h.rearrange("(c p) s -> p c s", p=128)[:, :, t0:t0 + 512], r=[sc["XT"]], w=[xT])
                    cs, sn = cst[blk % 2], snt[blk % 2]
                    kb.dma(cs[64:96, :], cos2[:, t0:t0 + 512], w=[cs])
                    kb.dma(sn[64:96, :], sin2[:, t0:t0 + 512], w=[sn])

                    def fm(col0, m, bank, lhs=None):
                        for c in range(8):
                            lt = (win[:, c, col0:col0 + m] if lhs is None else lhs[:, c, 0:m])
                            kb.op("pe", lambda e, c=c, lt=lt: e.matmul(PS[bank][0:m, :], lhsT=lt, rhs=xT[:, c, :],
                                                                        start=(c == 0), stop=(c == 7)),
                                  r=[win if lhs is None else lhs, xT], w=[PS[bank]])

                    for (col0, nchunk, gtile, dst, key) in ((0, 2, qgt, "CQT", "q"), (256, 1, kvgt, "CKVT", "kv")):
                        for cc in range(nchunk):
                            fm(col0 + cc * 128, 128, cc)
                        for cc in range(nchunk):
                            kb.op("act", lambda e, cc=cc: e.activation(out=sq[:, :], in_=PS[cc][:, :], func=AF.Square),
                                  r=[PS[cc]], w=[sq])
                            kb.op("pe", lambda e, cc=cc: e.matmul(PS[2][:, :], lhsT=onesb[:, :], rhs=sq[:, :],
                                                                  start=(cc == 0), stop=(cc == nchunk - 1)),
                                  r=[onesb, sq], w=[PS[2]])
                        nfeat = 128.0 * nchunk
                        rsqrt_eps(rstd, rstd[:, :], PS[2][:, :], [PS[2]], scale=1.0 / nfeat)
                        for cc in range(nchunk):
                            o = ev[evc % 2]
                            evc += 1
                            kb.op("dve", lambda e, cc=cc, o=o, gtile=gtile: e.scalar_tensor_tensor(
                                out=o[:, :], in0=PS[cc][:, :], scalar=gtile[:, cc:cc + 1], in1=rstd[:, :],
                                op0=ALU.mult, op1=ALU.mult), r=[PS[cc], gtile, rstd], w=[o])
                            kb.dma(sc[dst].h[cc * 128:(cc + 1) * 128, t0:t0 + 512], o[:, :], r=[o], w=[sc[dst]])
                    fm(320, 96, 3)
                    fm(0, 96, 4, lhs=wkr_sw)
                    kb.op("dve", lambda e: e.tensor_tensor(out=kr1[64:96, :], in0=PS[3][64:96, :], in1=cs[64:96, :], op=ALU.mult),
                          r=[PS[3], cs], w=[kr1])
                    kb.op("dve", lambda e: e.tensor_tensor(out=kr2[64:96, :], in0=PS[4][64:96, :], in1=sn[64:96, :], op=ALU.mult),
                          r=[PS[4], sn], w=[kr2])
                    kb.op("pool", lambda e: e.tensor_tensor(out=krb[64:96, :], in0=kr1[64:96, :], in1=kr2[64:96, :], op=ALU.add),
                          r=[kr1, kr2], w=[krb])
                    kb.dma(sc["KRT"].h[:, t0:t0 + 512], krb[64:96, :], r=[krb], w=[sc["KRT"]])
                    bi = 0
                    for (col0, dst, kind) in ((416, "POOLT", "f"), (544, "POOLT", "f"), (672, "QMT", "b"), (800, "QMT", "b"),
                                              (928, "KMT", "k"), (1056, "KMT", "k")):
                        bank = 5 + (bi % 2)
                        bi += 1
                        fm(col0, 128, bank)
                        r0 = ((col0 - 416) % 256) if dst == "POOLT" else ((col0 - 672) % 256)
                        if kind == "f":
                            o = evf[evc % 2]
                            evc += 1
                            kb.op("act", lambda e, o=o, bank=bank: e.activation(out=o[:, :], in_=PS[bank][:, :], func=AF.Copy),
                                  r=[PS[bank]], w=[o])
                        else:
                            o = ev[evc % 2]
                            evc += 1
                            scl = 0.125 if kind == "k" else 1.0
                            kb.op("act", lambda e, o=o, bank=bank, scl=scl: e.activation(out=o[:, :], in_=PS[bank][:, :],
                                                                                           func=AF.Copy, scale=scl),
                                  r=[PS[bank]], w=[o])
                        kb.dma(sc[dst].h[r0:r0 + 128, t0:t0 + 512], o[:, :], r=[o], w=[sc[dst]])
                    for ty in range(4):
                        fm(1696 + 4 * ty, 4, 3 + (ty % 2))
                        o = evf[evc % 2]
                        evc += 1
                        bank = 3 + (ty % 2)
                        kb.op("act", lambda e, o=o, bank=bank: e.activation(out=o[0:4, :], in_=PS[bank][0:4, :], func=AF.Copy),
                              r=[PS[bank]], w=[o])
                        kb.dma(sc["G4"].h[ty, :, t0:t0 + 512], o[0:4, :], r=[o], w=[sc["G4"]])
                    for tt in range(4):
                        tk0 = t0 + tt * 128
                        for c in range(8):
                            kb.op("pe", lambda e, c=c, tt=tt: e.matmul(PS[0][:, :], lhsT=xT[:, c, tt * 128:(tt + 1) * 128],
                                                                        rhs=win[:, c, 928:1440], start=(c == 0), stop=(c == 7)),
                                  r=[xT, win], w=[PS[0]])
                        for c in range(8):
                            kb.op("pe", lambda e, c=c, tt=tt: e.matmul(PS[1][:, 0:256], lhsT=xT[:, c, tt * 128:(tt + 1) * 128],
                                                                        rhs=win[:, c, 1440:1696], start=(c == 0), stop=(c == 7)),
                                  r=[xT, win], w=[PS[1]])
                        ob = tmb[tt % 2]
                        of = tmf[tt % 2]
                        kb.op("act", lambda e, ob=ob: e.activation(out=ob[:, 0:256], in_=PS[0][:, 0:256], func=AF.Copy, scale=0.125),
                              r=[PS[0]], w=[ob])
                        kb.op("dve", lambda e, ob=ob: e.tensor_copy(out=ob[:, 256:512], in_=PS[0][:, 256:512]), r=[PS[0]], w=[ob])
                        kb.op("act", lambda e, of=of: e.activation(out=of[:, :], in_=PS[1][:, 0:256], func=AF.Copy), r=[PS[1]], w=[of])
                        kb.dma(sc["KM"].h[tk0:tk0 + 128, :], ob[:, 0:256], r=[ob], w=[sc["KM"]])
                        kb.dma(sc["VM"].h[tk0:tk0 + 128, :], ob[:, 256:512], r=[ob], w=[sc["VM"]])
                        kb.dma(sc["OM"].h[tk0:tk0 + 128, :], of[:, :], r=[of], w=[sc["OM"]])
            kb.barrier()

        for j, S in enumerate(jobs):
            sc = SC[j]
            NKC = S // 128
            NQB = S // 512
            with ExitStack() as es:
                uniq[0] += 1
                STT = [T(es.enter_context(nc.psum_tensor("st%d_%d" % (i_, uniq[0]), [128, 1024], F32))) for i_ in range(3)]
                OTT = T(es.enter_context(nc.psum_tensor("ot_%d" % uniq[0], [128, 512], F32)))
                MSC = T(es.enter_context(nc.psum_tensor("msc_%d" % uniq[0], [128, 512], F32)))
                stg = [sb(es, "stg%d" % i, [128, 2048], F32) for i in range(2)]
                wq = sb(es, "wq", [128, 2, 768], BF16)
                wqs = sb(es, "wqs", [128, 2, 768], BF16)
                wkv = sb(es, "wkv", [128, 1, 1024], BF16)
                load_w_bf16(es, wq, lambda c, n0, n1: wq[:, c, n0:n1], w_uq[l], 2, 768, stg)
                load_w_bf16(es, wkv, lambda c, n0, n1: wkv[:, c, n0:n1], w_ukv[l], 1, 1024, stg)
                wq4 = wq.h.rearrange("p c (h d) -> p c h d", h=8)
                wqs4 = wqs.h.rearrange("p c (h d) -> p c h d", h=8)
                kb.op("dve", lambda e: e.tensor_copy(out=wqs[:, :, :], in_=wq[:, :, :]), r=[wq], w=[wqs])
                kb.op("dve", lambda e: e.tensor_copy(out=wqs4[:, :, :, 64:80], in_=wq4[:, :, :, 80:96]), r=[wq], w=[wqs])
                kb.op("dve", lambda e: e.tensor_copy(out=wqs4[:, :, :, 80:96], in_=wq4[:, :, :, 64:80]), r=[wq], w=[wqs])
                ckv = sb(es, "ckv", [128, S], BF16)
                KT = sb(es, "KT", [96, S], BF16)
                VA = sb(es, "VA", [128, NKC, 65], BF16)
                kb.dma(ckv[:], sc["CKVT"].h[:, :], r=[sc["CKVT"]], w=[ckv])
                kb.dma(KT[64:96, :], sc["KRT"].h[:, :], r=[sc["KRT"]], w=[KT])
                kb.op("dve", lambda e: e.memset(VA[:, :, 64:65], 1.0), w=[VA])
                cqs = [sb(es, "cq%d" % i, [128, 2, 512], BF16) for i in range(2)]
                cst = [sb(es, "acs%d" % i, [96, 512], F32) for i in range(2)]
                snt = [sb(es, "asn%d" % i, [96, 512], F32) for i in range(2)]
                QTs = [sb(es, "QT%d" % i, [96, 512], BF16) for i in range(2)]
                q1 = sb(es, "q1", [96, 512], F32)
                q2 = sb(es, "q2", [96, 512], F32)
                PTs = [sb(es, "PT%d" % i, [128, 1024], BF16) for i in range(3)]
                den = sb(es, "den", [65, 512], F32)
                bcs = sb(es, "bcs", [64, 512], F32)
                ots = [sb(es, "ot%d" % i, [64, 512], BF16) for i in range(2)]
                scale = 96.0 ** -0.5
                qbc = 0
                otcs = [sb(es, "otc%d" % i, [65, 512], F32) for i in range(2)]
                qt_ready = {}

                def build_q(h, qb, slot):
                    q0 = qb * 512
                    cq, cs, sn, QT = cqs[slot], cst[slot], snt[slot], QTs[slot]
                    kb.dma(cq[:], sc["CQT"].h.rearrange("(c p) s -> p c s", p=128)[:, :, q0:q0 + 512], r=[sc["CQT"]], w=[cq])
                    kb.dma(cs[64:96, :], cos2[:, q0:q0 + 512], w=[cs])
                    kb.dma(sn[64:96, :], sin2[:, q0:q0 + 512], w=[sn])
                    for c in range(2):
                        kb.op("pe", lambda e, c=c: e.matmul(MSC[0:96, :], lhsT=wq[:, c, h * 96:(h + 1) * 96], rhs=cq[:, c, :],
                                                            start=(c == 0), stop=(c == 1)), r=[wq, cq], w=[MSC])
                    kb.op("dve", lambda e: e.tensor_copy(out=QT[0:64, :], in_=MSC[0:64, :]), r=[MSC], w=[QT])
                    kb.op("dve", lambda e: e.tensor_tensor(out=q1[64:96, :], in0=MSC[64:96, :], in1=cs[64:96, :], op=ALU.mult),
                          r=[MSC, cs], w=[q1])
                    for c in range(2):
                        kb.op("pe", lambda e, c=c: e.matmul(MSC[0:96, :], lhsT=wqs[:, c, h * 96:(h + 1) * 96], rhs=cq[:, c, :],
                                                            start=(c == 0), stop=(c == 1)), r=[wqs, cq], w=[MSC])
                    kb.op("dve", lambda e: e.tensor_tensor(out=q2[64:96, :], in0=MSC[64:96, :], in1=sn[64:96, :], op=ALU.mult),
                          r=[MSC, sn], w=[q2])
                    kb.op("pool", lambda e: e.tensor_tensor(out=QT[64:96, :], in0=q1[64:96, :], in1=q2[64:96, :], op=ALU.add),
                          r=[q1, q2], w=[QT])
                    return QT
                for h in range(8):
                    for blk in range(NQB):
                        kb.op("pe", lambda e, blk=blk: e.matmul(MSC[0:64, :], lhsT=wkv[:, 0, h * 128:h * 128 + 64],
                                                                 rhs=ckv[:, blk * 512:(blk + 1) * 512], start=True, stop=True),
                              r=[wkv, ckv], w=[MSC])
                        kb.op("dve", lambda e, blk=blk: e.tensor_copy(out=KT[0:64, blk * 512:(blk + 1) * 512], in_=MSC[0:64, :]),
                              r=[MSC], w=[KT])
                    for g in range(NKC // 8):
                        for i in range(8):
                            kc = g * 8 + i
                            kb.op("pe", lambda e, kc=kc, i=i: e.matmul(MSC[:, i * 64:(i + 1) * 64], lhsT=ckv[:, kc * 128:(kc + 1) * 128],
                                                                        rhs=wkv[:, 0, h * 128 + 64:h * 128 + 128], start=True, stop=True),
                                  r=[wkv, ckv], w=[MSC])
                        kb.op("act", lambda e, g=g: e.activation(out=VA[:, g * 8:(g + 1) * 8, 0:64],
                                                                 in_=MSC.h.rearrange("p (i d) -> p i d", i=8), func=AF.Copy),
                              r=[MSC], w=[VA])
                    for qb in range(NQB):
                        q0 = qb * 512
                        OT = OTT
                        ot = ots[qbc % 2]
                        otc = otcs[qbc % 2]
                        if (h, qb) in qt_ready:
                            QT = qt_ready.pop((h, qb))
                        else:
                            QT = build_q(h, qb, qbc % 2)
                        nxt = (h, qb + 1) if qb + 1 < NQB else ((h + 1, 0) if h + 1 < 8 else None)
                        nslot = (qbc + 1) % 2
                        qbc += 1
                        npair = NKC // 2

                        def mm1(p):
                            st = STT[p % 3]
                            for i in range(2):
                                kc = 2 * p + i
                                kb.op("pe", lambda e, kc=kc, i=i: e.matmul(st[:, i * 512:(i + 1) * 512], lhsT=KT[0:96, kc * 128:(kc + 1) * 128],
                                                                            rhs=QT[0:96, :], start=True, stop=True),
                                      r=[KT, QT], w=[st])

                        def ex(p):
                            st = STT[p % 3]
                            pt = PTs[p % 3]
                            kb.op("act", lambda e: e.activation(out=pt[:, :], in_=st[:, :], func=AF.Exp, scale=scale),
                                  r=[st], w=[pt])

                        def mm2(p):
                            pt = PTs[p % 3]
                            for i in range(2):
                                kc = 2 * p + i
                                kb.op("pe", lambda e, kc=kc, i=i: e.matmul(OT[0:65, :], lhsT=VA[:, kc, :], rhs=pt[:, i * 512:(i + 1) * 512],
                                                                            start=(kc == 0), stop=(kc == NKC - 1)),
                                      r=[VA, pt], w=[OT])

                        mm1(0)
                        if npair > 1:
                            mm1(1)
                        if nxt is not None:
                            qt_ready[nxt] = build_q(nxt[0], nxt[1], nslot)
                        for p in range(npair):
                            ex(p)
                            if p + 2 < npair:
                                mm1(p + 2)
                            mm2(p)
                        kb.op("dve", lambda e: e.tensor_copy(out=otc[0:65, :], in_=OT[0:65, :]), r=[OT], w=[otc])
                        kb.op("dve", lambda e: e.reciprocal(out=den[64:65, :], in_=otc[64:65, :]), r=[otc], w=[den])
                        kb.op("pe", lambda e: e.matmul(MSC[0:64, :], lhsT=onesf[64:65, 0:64], rhs=den[64:65, :], start=True, stop=True),
                              r=[onesf, den], w=[MSC])
                        kb.op("dve", lambda e: e.tensor_tensor(out=ot[:, :], in0=otc[0:64, :], in1=MSC[0:64, :], op=ALU.mult),
                              r=[otc, MSC], w=[ot])
                        kb.dma(sc["MIXT"].h[h * 64:(h + 1) * 64, q0:q0 + 512], ot[:, :], r=[ot], w=[sc["MIXT"]])
                kb.barrier()

        with ExitStack() as es:
            std_psum(es)
            stg = [sb(es, "stg%d" % i, [128, 2048], F32) for i in range(2)]
            wp = sb(es, "wp", [128, 2, 64], BF16)
            for g in range(4):
                st = stg[g % 2]
                p0 = (g % 2) * 64
                kb.dma(st[p0:p0 + 64, 0:64], w_pool[l, g], w=[st])
                kb.op("dve", lambda e, st=st, p0=p0, g=g: e.tensor_copy(out=wp[p0:p0 + 64, g // 2, :], in_=st[p0:p0 + 64, 0:64]),
                      r=[st], w=[wp])
            psct = sb(es, "psct", [64, 4], F32)
            kb.dma(psct[:], psc[l], w=[psct])
            PB = 2048
            xps = [sb(es, "xp%d" % i, [128, 2, PB + 16], F32) for i in range(2)]
            a2 = sb(es, "a2", [128, PB + 16], F32)
            a4 = sb(es, "a4", [128, PB + 16], F32)
            yb = [sb(es, "yb%d" % i, [128, PB], BF16) for i in range(2)]
            yf = sb(es, "yf", [128, PB], F32)
            ped = sb(es, "ped", [128, 2, 16], F32)
            po = [sb(es, "po%d" % i, [64, 512], BF16) for i in range(2)]
            WINS = (2, 4, 8, 16)
            bc = 0
            for j, S in enumerate(jobs):
                sc = SC[j]
                kb.dma(ped[:], pedge[j], w=[ped])
                pb = min(PB, S)
                for blk in range(S // pb):
                    t0 = blk * pb
                    xp = xps[bc % 2]
                    bc += 1
                    lo = max(0, t0 - 8)
                    hi = min(S, t0 + pb + 8)
                    if lo > t0 - 8:
                        kb.op("pool", lambda e: e.memset(xp[:, :, 0:8], 0.0), w=[xp])
                    if hi < t0 + pb + 8:
                        kb.op("pool", lambda e: e.memset(xp[:, :, pb + 8:pb + 16], 0.0), w=[xp])
                    kb.dma(xp[:, :, 8 + (lo - t0):8 + (hi - t0)],
                           sc["POOLT"].h.rearrange("(c p) s -> p c s", p=128)[:, :, lo:hi], r=[sc["POOLT"]], w=[xp])
                    for c in range(2):
                        n = pb + 16
                        x = xp.h[:, c, :]
                        kb.op("dve", lambda e, x=x: e.tensor_tensor(out=a2[:, 0:n - 1], in0=x[:, 0:n - 1], in1=x[:, 1:n], op=ALU.add),
                              r=[xp], w=[a2])
                        kb.op("pool", lambda e: e.tensor_tensor(out=a4[:, 0:n - 3], in0=a2[:, 0:n - 3], in1=a2[:, 2:n - 1], op=ALU.add),
                              r=[a2], w=[a4])
                        if c == 1:
                            kb.op("dve", lambda e: e.tensor_tensor(out=a2[:, 0:n - 7], in0=a4[:, 0:n - 7], in1=a4[:, 4:n - 3], op=ALU.add),
                                  r=[a4], w=[a2])
                            kb.op("pool", lambda e: e.tensor_tensor(out=a4[64:128, 0:n - 15], in0=a2[64:128, 0:n - 15],
                                                                    in1=a2[64:128, 8:n - 7], op=ALU.add), r=[a2], w=[a4])
                        for half in range(2):
                            w = WINS[2 * c + half]
                            src = a2 if half == 0 else a4
                            p0 = half * 64
                            o0 = 8 - w // 2
                            kb.op("dve", lambda e, src=src, p0=p0, w=w, o0=o0, x=x: e.scalar_tensor_tensor(
                                out=yf[p0:p0 + 64, 0:pb], in0=src[p0:p0 + 64, o0:o0 + pb], scalar=1.0 / w,
                                in1=x[p0:p0 + 64, 8:8 + pb], op0=ALU.mult, op1=ALU.subtract), r=[src, xp], w=[yf])
                            if t0 == 0:
                                kb.op("dve", lambda e, src=src, p0=p0, o0=o0, x=x, c=c: e.tensor_tensor(
                                    out=yf[p0:p0 + 64, 0:8], in0=src[p0:p0 + 64, o0:o0 + 8], in1=ped[p0:p0 + 64, c, 0:8], op=ALU.mult),
                                    r=[src, ped], w=[yf])
                                kb.op("dve", lambda e, p0=p0, x=x: e.tensor_tensor(
                                    out=yf[p0:p0 + 64, 0:8], in0=yf[p0:p0 + 64, 0:8], in1=x[p0:p0 + 64, 8:16], op=ALU.subtract),
                                    r=[xp], w=[yf])
                            if t0 + pb == S:
                                kb.op("dve", lambda e, src=src, p0=p0, o0=o0, x=x, c=c: e.tensor_tensor(
                                    out=yf[p0:p0 + 64, pb - 8:pb], in0=src[p0:p0 + 64, o0 + pb - 8:o0 + pb], in1=ped[p0:p0 + 64, c, 8:16],
                                    op=ALU.mult), r=[src, ped], w=[yf])
                                kb.op("dve", lambda e, p0=p0, x=x: e.tensor_tensor(
                                    out=yf[p0:p0 + 64, pb - 8:pb], in0=yf[p0:p0 + 64, pb - 8:pb], in1=x[p0:p0 + 64, pb:pb + 8],
                                    op=ALU.subtract), r=[xp], w=[yf])
                        ybt = yb[c]
                        kb.op("act", lambda e, ybt=ybt: e.activation(out=ybt[:, 0:pb], in_=yf[:, 0:pb], func=AF.Copy), r=[yf], w=[ybt])
                        for half in range(2):
                            g = 2 * c + half
                            p0 = half * 64
                            for sb_ in range(pb // 512):
                                bank = 5 + (sb_ % 2)
                                kb.op("pe", lambda e, p0=p0, c=c, sb_=sb_, ybt=ybt, bank=bank: e.matmul(
                                    PS[bank][0:64, :], lhsT=wp[p0:p0 + 64, c, :], rhs=ybt[p0:p0 + 64, sb_ * 512:(sb_ + 1) * 512],
                                    start=True, stop=True), r=[wp, ybt], w=[PS[bank]])
                                o = po[sb_ % 2]
                                kb.op("act", lambda e, o=o, bank=bank, g=g: e.activation(out=o[:, :], in_=PS[bank][0:64, :], func=AF.Copy,
                                                                                         scale=psct[:, g:g + 1]),
                                      r=[PS[bank], psct], w=[o])
                                kb.dma(sc["MIXT"].h[512 + g * 64:512 + (g + 1) * 64, t0 + sb_ * 512:t0 + (sb_ + 1) * 512], o[:, :],
                                       r=[o], w=[sc["MIXT"]])
            kb.barrier()

        for j, S in enumerate(jobs):
            sc = SC[j]
            SCH = min(S, 2048)
            NSC = S // SCH
            NCH = SCH // 128
            NCT = S // 128
            with ExitStack() as es:
                std_psum(es)
                gb = sb(es, "gb", [4, 4], F32)
                ngb = sb(es, "ngb", [4, 4], F32)
                kb.dma(gb[:], gbias[l], w=[gb])
                kb.op("dve", lambda e: e.tensor_scalar(out=ngb[:, :], in0=gb[:, :], scalar1=-1.0, scalar2=None, op0=ALU.mult),
                      r=[gb], w=[ngb])
                selt = sb(es, "selt", [4, 4, 128], F32)
                kb.dma(selt[:], sel_d[:, :, :], w=[selt])
                masks = [sb(es, "mkf", [128, 128], F32), sb(es, "mkb", [128, 128], F32)]
                kb.dma(masks[0][:], maskf_d[:, :], w=[masks[0]])
                kb.dma(masks[1][:], maskb_d[:, :], w=[masks[1]])
                repn = sb(es, "repn", [128, 256], F32)
                kb.dma(repn[:], rep_l[l, :, 4 * D:4 * D + 256], w=[repn])
                ones4 = sb(es, "ones4", [4, SCH], F32)
                kb.op("pool", lambda e: e.memset(ones4[:], 1.0), w=[ones4])
                gi = sb(es, "gi", [4, SCH], F32)
                gf = sb(es, "gf", [4, SCH], F32)
                t1 = sb(es, "gt1", [4, SCH], F32)
                t2 = sb(es, "gt2", [4, SCH], F32)
                Mo = sb(es, "Mo", [4, SCH], F32)
                WIo = sb(es, "WIo", [4, SCH], F32)
                A_ = sb(es, "A_", [4, SCH], F32)
                NMt = sb(es, "NMt", [4, SCH], F32)
                WS = sb(es, "WS", [4, SCH], F32)
                EM = sb(es, "EM", [4, SCH], F32)
                STK = sb(es, "STK", [128, SCH], F32)
                carNB = sb(es, "carNB", [4, 1], F32)
                carM = sb(es, "carM", [4, 1], F32)
                kb.op("pool", lambda e: e.memset(STK[:], 0.0), w=[STK])
                qT = sb(es, "mqT", [64, 4, SCH], BF16)
                kT = sb(es, "mkT", [64, 4, SCH], BF16)
                CN = [sb(es, "CN%d" % h, [64, 65], F32) for h in range(4)]
                CNb = [sb(es, "CNb%d" % h, [64, 65], BF16) for h in range(4)]
                tq = [sb(es, "tq%d" % i, [128, 16], F32) for i in range(2)]
                vas = [sb(es, "va%d" % i, [128, 4, 65], BF16) for i in range(2)]
                vms = [sb(es, "vm%d" % i, [128, 256], BF16) for i in range(2)]
                kms = [sb(es, "km%d" % i, [128, 256], BF16) for i in range(2)]
                Dm = [sb(es, "Dm%d" % i, [128, 128], F32) for i in range(4)]
                Em = [sb(es, "Em%d" % i, [128, 128], F32) for i in range(4)]
                PTm = [sb(es, "PTm%d" % i, [128, 128], BF16) for i in range(4)]
                intra = [sb(es, "intra%d" % i, [128, 65], F32) for i in range(4)]
                HN = [sb(es, "HN%d" % i, [128, 65], F32) for i in range(4)]
                dn = [sb(es, "dn%d" % i, [128, 2], F32) for i in range(4)]
                dcs = [sb(es, "dcs%d" % i, [64, 1], F32) for i in range(4)]
                KW = [sb(es, "KW%d" % i, [128, 64], BF16) for i in range(4)]
                HT = [sb(es, "HT%d" % i, [128, 256], F32) for i in range(2)]
                hfl = [sb(es, "hfl%d" % i, [128, 256], F32) for i in range(2)]
                oml = [sb(es, "oml%d" % i, [128, 256], F32) for i in range(2)]
                gst = sb(es, "gst", [128, 4, 6], F32)
                gmv = sb(es, "gmv", [128, 4, 2], F32)
                grs = sb(es, "grs", [128, 4], F32)
                yb_ = [sb(es, "myb%d" % i, [128, 256], BF16) for i in range(2)]
                yT = [sb(es, "myT%d" % i, [128, 256], BF16) for i in range(2)]
                it = 0
                un = 0
                for d in range(2):
                    kb.flush()
                    kb.op("dve", lambda e: e.memset(carNB[:], 0.0), w=[carNB])
                    kb.op("dve", lambda e: e.memset(carM[:], 0.0), w=[carM])
                    gci = 0
                    for k in range(NSC):
                        o0 = k * SCH if d == 0 else (NSC - 1 - k) * SCH
                        kb.dma(gi[:], sc["G4"].h[2 * d, :, o0:o0 + SCH], r=[sc["G4"]], w=[gi])
                        kb.dma(gf[:], sc["G4"].h[2 * d + 1, :, o0:o0 + SCH], r=[sc["G4"]], w=[gf])
                        kb.dma(qT[:], sc["QMT"].h.rearrange("(h p) s -> p h s", p=64)[:, :, o0:o0 + SCH], r=[sc["QMT"]], w=[qT])
                        kb.dma(kT[:], sc["KMT"].h.rearrange("(h p) s -> p h s", p=64)[:, :, o0:o0 + SCH], r=[sc["KMT"]], w=[kT])
                        kb.op("dve", lambda e, d=d: e.tensor_scalar(out=gi[:, :], in0=gi[:, :], scalar1=gb[:, 2 * d:2 * d + 1], scalar2=None,
                                                                    op0=ALU.add), r=[gb], w=[gi])
                        kb.op("act", lambda e, d=d: e.activation(out=t1[:, :], in_=gf[:, :], func=AF.Exp, scale=-1.0,
                                                                 bias=ngb[:, 2 * d + 1:2 * d + 2]), r=[gf, ngb], w=[t1])
                        kb.op("act", lambda e: e.activation(out=t1[:, :], in_=t1[:, :], func=AF.Ln, scale=1.0, bias=onesf[0:4, 0:1]),
                              r=[onesf], w=[t1])
                        if d == 0:
                            isrc, fsrc = gi, t1
                        else:
                            kb.op("dve", lambda e: e.tensor_copy(out=t2[:, :], in_=gi[:, ::-1]), r=[gi], w=[t2])
                            kb.op("dve", lambda e: e.tensor_copy(out=gf[:, :], in_=t1[:, ::-1]), r=[t1], w=[gf])
                            isrc, fsrc = t2, gf
                        kb.op("dve", lambda e, fsrc=fsrc: e.tensor_tensor_scan(out=NMt[:, :], data0=ones4[:, :], data1=fsrc[:, :],
                                                                               initial=carNB[:, 0:1], op0=ALU.mult, op1=ALU.add),
                              r=[ones4, fsrc, carNB], w=[NMt])
                        kb.op("dve", lambda e: e.tensor_copy(out=carNB[:, 0:1], in_=NMt[:, SCH - 1:SCH]), r=[NMt], w=[carNB])
                        kb.op("dve", lambda e, isrc=isrc: e.tensor_tensor(out=A_[:, :], in0=isrc[:, :], in1=NMt[:, :], op=ALU.add),
                              r=[isrc, NMt], w=[A_])
                        Mf = t1 if d == 0 else gi
                        kb.op("dve", lambda e, Mf=Mf: e.tensor_tensor_scan(out=Mf[:, :], data0=ones4[:, :], data1=A_[:, :],
                                                                           initial=carM[:, 0:1], op0=ALU.mult, op1=ALU.max),
                              r=[ones4, A_, carM], w=[Mf])
                        kb.op("dve", lambda e, Mf=Mf: e.tensor_tensor(out=EM[:, :], in0=NMt[:, :], in1=Mf[:, :], op=ALU.subtract),
                              r=[NMt, Mf], w=[EM])
                        kb.op("act", lambda e: e.activation(out=EM[:, :], in_=EM[:, :], func=AF.Exp), r=[], w=[EM])
                        kb.op("dve", lambda e, Mf=Mf: e.tensor_scalar(out=NMt[:, :], in0=Mf[:, :], scalar1=-1.0, scalar2=None, op0=ALU.mult),
                              r=[Mf], w=[NMt])
                        WIf = gf if d == 0 else t1
                        for c in range(NCH):
                            c0 = c * 128
                            bprev = carM[:, 0:1] if c == 0 else Mf[:, c0 - 1:c0]
                            kb.op("act", lambda e, c0=c0, bprev=bprev, WIf=WIf: e.activation(out=WIf[:, c0:c0 + 128], in_=NMt[:, c0:c0 + 128],
                                                                                             func=AF.Exp, bias=bprev),
                                  r=[NMt, Mf, carM], w=[WIf])
                            kb.op("act", lambda e, c0=c0: e.activation(out=WS[:, c0:c0 + 128], in_=A_[:, c0:c0 + 128], func=AF.Exp,
                                                                       bias=NMt[:, c0 + 127:c0 + 128]), r=[A_, NMt], w=[WS])
                        kb.op("dve", lambda e, Mf=Mf: e.tensor_copy(out=carM[:, 0:1], in_=Mf[:, SCH - 1:SCH]), r=[Mf], w=[carM])
                        if d == 0:
                            kb.op("pool", lambda e, Mf=Mf: e.tensor_copy(out=Mo[:, :], in_=Mf[:, :]), r=[Mf], w=[Mo])
                            kb.op("pool", lambda e, WIf=WIf: e.tensor_copy(out=WIo[:, :], in_=WIf[:, :]), r=[WIf], w=[WIo])
                            srcs = (A_, WIo, EM, WS)
                        else:
                            kb.op("dve", lambda e, Mf=Mf: e.tensor_copy(out=Mo[:, :], in_=Mf[:, ::-1]), r=[Mf], w=[Mo])
                            kb.op("dve", lambda e, WIf=WIf: e.tensor_copy(out=WIo[:, :], in_=WIf[:, ::-1]), r=[WIf], w=[WIo])
                            kb.op("dve", lambda e: e.tensor_copy(out=t2[:, :], in_=A_[:, ::-1]), r=[A_], w=[t2])
                            kb.op("dve", lambda e: e.tensor_copy(out=A_[:, :], in_=EM[:, ::-1]), r=[EM], w=[A_])
                            kb.op("dve", lambda e: e.tensor_copy(out=EM[:, :], in_=WS[:, ::-1]), r=[WS], w=[EM])
                            srcs = (t2, WIo, A_, EM)
                        for kk, s_ in enumerate(srcs):
                            kb.dma(STK[32 * kk:32 * kk + 4, :], s_[:, :], r=[s_], w=[STK])
                        order = range(NCH) if d == 0 else range(NCH - 1, -1, -1)
                        for c in order:
                            c0 = c * 128
                            g0 = o0 + c0
                            ci = gci
                            gci += 1
                            sl = it % 2
                            it += 1
                            tqs, va, vm, km, ht = tq[sl], vas[sl], vms[sl], kms[sl], HT[sl]
                            kb.op("pe", lambda e, c0=c0: e.transpose(out=PS[1][:, 384:512], in_=STK[:, c0:c0 + 128], identity=identf[:]),
                                  r=[STK, identf], w=[PS[1]])
                            kb.op("act", lambda e, tqs=tqs: e.activation(
                                out=tqs.h.rearrange("p (k h) -> p k h", k=4),
                                in_=PS[1].h[:, 384:512].rearrange("p (k x) -> p k x", k=4)[:, :, 0:4], func=AF.Copy), r=[PS[1]], w=[tqs])
                            kb.dma(vm[:], sc["VM"].h[g0:g0 + 128, :], r=[sc["VM"]], w=[vm])
                            kb.dma(km[:], sc["KM"].h[g0:g0 + 128, :], r=[sc["KM"]], w=[km])
                            if d == 1:
                                kb.dma(hfl[sl][:], sc["HF"].h[g0:g0 + 128, :], r=[sc["HF"]], w=[hfl[sl]])
                                kb.dma(oml[sl][:], sc["OM"].h[g0:g0 + 128, :], r=[sc["OM"]], w=[oml[sl]])
                            kb.flush()
                            kb.op("pool", lambda e, va=va, vm=vm: e.tensor_copy(out=va[:, :, 0:64], in_=vm.h.rearrange("p (h x) -> p h x", h=4)),
                                  r=[vm], w=[va])
                            kb.op("pool", lambda e, va=va: e.memset(va[:, :, 64:65], 1.0), w=[va])
                            def RG(h):
                                if h < 3:
                                    A_b, B_b = PS[2 * h], PS[2 * h + 1]
                                    return dict(A=A_b, B=B_b, st=A_b[:, 0:128], mb=A_b[:, 128:256], ia=B_b[:, 0:65], ie=B_b[:, 128:193],
                                                dc=B_b[0:64, 256:321], dec=B_b[0:64, 330:331])
                                A_b = PS[6]
                                return dict(A=A_b, B=A_b, st=A_b[:, 0:128], mb=A_b[:, 128:256], ia=A_b[:, 256:321], ie=A_b[:, 321:386],
                                            dc=A_b[0:64, 386:451], dec=A_b[0:64, 451:452])
                            R4 = [RG(h) for h in range(4)]
                            for h in range(4):
                                g = R4[h]
                                kb.op("pe", lambda e, h=h, g=g: e.matmul(g["st"], lhsT=kT[:, h, c0:c0 + 128], rhs=qT[:, h, c0:c0 + 128],
                                                                          start=True, stop=True), r=[kT, qT], w=[g["A"]])
                                kb.op("pe", lambda e, h=h, g=g: e.matmul(g["mb"], lhsT=selt[:, h, :], rhs=Mo[:, c0:c0 + 128],
                                                                          start=True, stop=True), r=[selt, Mo], w=[g["A"]])
                            for h in range(4):
                                g = R4[h]
                                kb.op("dve", lambda e, h=h, g=g: e.scalar_tensor_tensor(
                                    out=Dm[h][:, :], in0=g["mb"], scalar=tqs[:, h:h + 1], in1=masks[d][:, :],
                                    op0=ALU.subtract, op1=ALU.max), r=[g["A"], tqs, masks[d]], w=[Dm[h]])
                            for h in range(4):
                                kb.op("act", lambda e, h=h: e.activation(out=Em[h][:, :], in_=Dm[h][:, :], func=AF.Exp, scale=-1.0),
                                      r=[Dm[h]], w=[Em[h]])
                            for h in range(4):
                                g = R4[h]
                                kb.op("dve", lambda e, h=h, g=g: e.tensor_tensor(out=PTm[h][:, :], in0=g["st"], in1=Em[h][:, :], op=ALU.mult),
                                      r=[g["A"], Em[h]], w=[PTm[h]])
                            for h in range(4):
                                g = R4[h]
                                kb.op("pe", lambda e, h=h, g=g: e.matmul(g["ia"], lhsT=PTm[h][:, :], rhs=va[:, h, :], start=True, stop=True),
                                      r=[PTm[h], va], w=[g["B"]])
                            for h in range(4):
                                g = R4[h]
                                if ci == 0:
                                    kb.op("act", lambda e, h=h, g=g: e.activation(out=HN[h][:, :], in_=g["ia"], func=AF.Copy),
                                          r=[g["B"]], w=[HN[h]])
                                else:
                                    kb.op("act", lambda e, h=h, g=g: e.activation(out=intra[h][:, :], in_=g["ia"], func=AF.Copy),
                                          r=[g["B"]], w=[intra[h]])
                                    kb.op("pe", lambda e, h=h, g=g: e.matmul(g["ie"], lhsT=qT[:, h, c0:c0 + 128], rhs=CNb[h][:, :],
                                                                              start=True, stop=True), r=[qT, CNb[h]], w=[g["B"]])
                            if ci > 0:
                                for h in range(4):
                                    g = R4[h]
                                    kb.op("dve", lambda e, h=h, g=g: e.scalar_tensor_tensor(
                                        out=HN[h][:, :], in0=g["ie"], scalar=tqs[:, 4 + h:5 + h], in1=intra[h][:, :],
                                        op0=ALU.mult, op1=ALU.add), r=[g["B"], tqs, intra[h]], w=[HN[h]])
                            for h in range(4):
                                kb.op("dve", lambda e, h=h: e.scalar_tensor_tensor(
                                    out=dn[h][:, 0:1], in0=HN[h][:, 64:65], scalar=-1.0, in1=HN[h][:, 64:65],
                                    op0=ALU.mult, op1=ALU.max), r=[HN[h]], w=[dn[h]])
                            for h in range(4):
                                kb.op("dve", lambda e, h=h: e.tensor_scalar(
                                    out=dn[h][:, 0:1], in0=dn[h][:, 0:1], scalar1=tqs[:, 8 + h:9 + h], scalar2=None,
                                    op0=ALU.max), r=[tqs], w=[dn[h]])
                            for h in range(4):
                                kb.op("dve", lambda e, h=h: e.reciprocal(out=dn[h][:, 1:2], in_=dn[h][:, 0:1]), r=[], w=[dn[h]])
                            for h in range(4):
                                kb.op("dve", lambda e, h=h: e.tensor_scalar(
                                    out=ht[:, h * 64:(h + 1) * 64], in0=HN[h][:, 0:64], scalar1=dn[h][:, 1:2], scalar2=None, op0=ALU.mult),
                                    r=[HN[h], dn[h]], w=[ht])
                            if ci < NCT - 1:
                                col = c0 + 127 if d == 0 else c0
                                for h in range(4):
                                    g = R4[h]
                                    kb.op("pool", lambda e, h=h: e.tensor_scalar(
                                        out=KW[h][:, :], in0=km[:, h * 64:(h + 1) * 64], scalar1=tqs[:, 12 + h:13 + h], scalar2=None,
                                        op0=ALU.mult), r=[km, tqs], w=[KW[h]])
                                    kb.op("pe", lambda e, h=h, g=g: e.matmul(g["dc"], lhsT=KW[h][:, :], rhs=va[:, h, :], start=True, stop=True),
                                          r=[KW[h], va], w=[g["B"]])
                                    if ci > 0:
                                        kb.op("pe", lambda e, h=h, g=g: e.matmul(g["dec"], lhsT=selt[:, h, 0:64], rhs=WIo[:, col:col + 1],
                                                                                  start=True, stop=True), r=[selt, WIo], w=[g["B"]])
                                for h in range(4):
                                    g = R4[h]
                                    if ci == 0:
                                        kb.op("act", lambda e, h=h, g=g: e.activation(out=CN[h][:, :], in_=g["dc"], func=AF.Copy),
                                              r=[g["B"]], w=[CN[h]])
                                    else:
                                        kb.op("dve", lambda e, h=h, g=g: e.tensor_copy(out=dcs[h][:, 0:1], in_=g["dec"]),
                                              r=[g["B"]], w=[dcs[h]])
                                for h in range(4):
                                    g = R4[h]
                                    if ci > 0:
                                        kb.op("dve", lambda e, h=h, g=g: e.scalar_tensor_tensor(
                                            out=CN[h][:, :], in0=CN[h][:, :], scalar=dcs[h][:, 0:1], in1=g["dc"],
                                            op0=ALU.mult, op1=ALU.add), r=[g["B"], dcs[h]], w=[CN[h]])
                                for h in range(4):
                                    kb.op("act", lambda e, h=h: e.activation(out=CNb[h][:, :], in_=CN[h][:, :], func=AF.Copy),
                                          r=[CN[h]], w=[CNb[h]])
                            if d == 0:
                                kb.dma_later(sc["HF"].h[g0:g0 + 128, :], ht[:, :], r=[ht], w=[sc["HF"]])
                            else:
                                hf, om, ybt, yTt = hfl[sl], oml[sl], yb_[sl], yT[sl]
                                kb.op("dve", lambda e, hf=hf, ht=ht: e.tensor_tensor(out=hf[:, :], in0=hf[:, :], in1=ht[:, :], op=ALU.add),
                                      r=[ht], w=[hf])
                                for h in range(4):
                                    kb.op("dve", lambda e, h=h, hf=hf: e.bn_stats(out=gst[:, h, :], in_=hf[:, h * 64:(h + 1) * 64]),
                                          r=[hf], w=[gst])
                                    kb.op("dve", lambda e, h=h: e.bn_aggr(out=gmv[:, h, :], in_=gst[:, h, :]), r=[gst], w=[gmv])
                                rsqrt_eps(grs, grs[:, :], gmv[:, :, 1], [gmv])
                                for h in range(4):
                                    kb.op("dve", lambda e, h=h, hf=hf: e.tensor_scalar(
                                        out=hf[:, h * 64:(h + 1) * 64], in0=hf[:, h * 64:(h + 1) * 64], scalar1=gmv[:, h, 0:1],
                                        scalar2=grs[:, h:h + 1], op0=ALU.subtract, op1=ALU.mult), r=[gmv, grs], w=[hf])
                                kb.op("act", lambda e, om=om: e.activation(out=om[:, :], in_=om[:, :], func=AF.Sigmoid), r=[], w=[om])
                                kb.op("pool", lambda e, hf=hf: e.tensor_tensor(out=hf[:, :], in0=hf[:, :], in1=repn[:, :], op=ALU.mult),
                                      r=[repn], w=[hf])
                                kb.op("dve", lambda e, hf=hf, om=om, ybt=ybt: e.tensor_tensor(out=ybt[:, :], in0=hf[:, :], in1=om[:, :],
                                                                                             op=ALU.mult), r=[hf, om], w=[ybt])
                                for cc in range(2):
                                    kb.op("pe", lambda e, cc=cc, ybt=ybt: e.transpose(out=PSB[:, cc * 128:(cc + 1) * 128],
                                                                                      in_=ybt[:, cc * 128:(cc + 1) * 128], identity=identb[:]),
                                          r=[ybt, identb], w=[PSB])
                                kb.op("act", lambda e, yTt=yTt: e.activation(out=yTt[:, :], in_=PSB[:, 0:256], func=AF.Copy), r=[PSB], w=[yTt])
                                kb.dma_later(sc["MIXT"].h[768:1024, :].rearrange("(c p) s -> p c s", p=128)[:, :, g0:g0 + 128],
                                             yTt.h.rearrange("p (c t) -> p c t", c=2), r=[yTt], w=[sc["MIXT"]])
                kb.barrier()

        if dbg and l == 0:
            dbg_out = T(nc.dram_tensor("dbg_mixt", [D, jobs[0]], BF16, kind="ExternalOutput").ap())
            with ExitStack() as es:
                dt_ = sb(es, "dbgt", [128, 8, jobs[0]], BF16)
                kb.dma(dt_[:], SC[0]["MIXT"].h.rearrange("(c p) s -> p c s", p=128), r=[SC[0]["MIXT"]], w=[dt_])
                kb.dma(dbg_out.h.rearrange("(c p) s -> p c s", p=128), dt_[:], r=[dt_], w=[dbg_out])
                kb.barrier()
        alpha = float(8.0 ** 0.25)
        with ExitStack() as es:
            std_psum(es)
            stg = [sb(es, "stg%d" % i, [128, 2048], F32) for i in range(2)]
            wo = sb(es, "wo", [128, 8, D], BF16)
            load_w_bf16(es, wo, lambda c, n0, n1: wo[:, c, n0:n1], w_out[l], 8, D, stg)
            rl = sb(es, "rl", [128, 2 * D], F32)
            kb.dma(rl[:], rep_l[l, :, 0:2 * D], w=[rl])
            tmp = {"st6": sb(es, "st6", [128, 2, 6], F32), "mv": sb(es, "mv", [128, 2], F32), "rs": sb(es, "rs", [128, 1], F32)}
            mts = [sb(es, "mt%d" % i, [128, 8, 512], BF16) for i in range(2)]
            xres = [sb(es, "xres%d" % i, [128, D], F32) for i in range(2)]
            z = [sb(es, "z%d" % i, [128, D], F32) for i in range(2)]
            x1f = [sb(es, "x1f%d" % i, [128, D], F32) for i in range(2)]
            ob = [sb(es, "ob%d" % i, [128, D], BF16) for i in range(2)]
            xts = [sb(es, "xts%d" % i, [128, D], BF16) for i in range(2)]
            bi = 0
            for j, S in enumerate(jobs):
                sc = SC[j]
                XNi, XNo = sc["XN"][0], sc["XN"][1]
                for blk in range(S // 512):
                    t0 = blk * 512
                    mt = mts[bi % 2]
                    bi += 1
                    kb.dma(mt[:], sc["MIXT"].h.rearrange("(c p) s -> p c s", p=128)[:, :, t0:t0 + 512], r=[sc["MIXT"]], w=[mt])
                    for tt in range(4):
                        tk0 = t0 + tt * 128
                        xr, zt, obt, x1, xt_ = xres[tt % 2], z[tt % 2], ob[tt % 2], x1f[tt % 2], xts[tt % 2]
                        kb.dma(xr[:], XNi.h[tk0:tk0 + 128, :], r=[XNi], w=[xr])
                        kb.flush()
                        for nb in range(2):
                            for c in range(8):
                                kb.op("pe", lambda e, c=c, nb=nb, tt=tt: e.matmul(PS[nb + 2 * (tt % 2)][:, :], lhsT=mt[:, c, tt * 128:(tt + 1) * 128],
                                                                                   rhs=wo[:, c, nb * 512:(nb + 1) * 512],
                                                                                   start=(c == 0), stop=(c == 7)), r=[mt, wo], w=[PS[nb + 2 * (tt % 2)]])
                            kb.op("dve", lambda e, nb=nb, xr=xr, zt=zt, tt=tt: e.scalar_tensor_tensor(
                                out=zt[:, nb * 512:(nb + 1) * 512], in0=xr[:, nb * 512:(nb + 1) * 512], scalar=alpha,
                                in1=PS[nb + 2 * (tt % 2)][:, :], op0=ALU.mult, op1=ALU.add), r=[xr, PS[nb + 2 * (tt % 2)]], w=[zt])
                        layer_norm_tile(zt, rl, rl, 0, D, x1, obt, tmp)
                        kb.dma_later(XNo.h[tk0:tk0 + 128, :], x1[:], r=[x1], w=[XNo])
                        transpose_to_xt(obt, xt_, sc["XT"], tk0)
            kb.barrier()
        with ExitStack() as es:
            std_psum(es)
            stg = [sb(es, "stg%d" % i, [128, 2048], F32) for i in range(2)]
            wg = sb(es, "wg", [128, 8, DFF], BF16)
            wu = sb(es, "wu", [128, 8, DFF], BF16)
            load_w_bf16(es, wg, lambda c, n0, n1: wg[:, c, n0:n1], w_gate[l], 8, DFF, stg)
            load_w_bf16(es, wu, lambda c, n0, n1: wu[:, c, n0:n1], w_up[l], 8, DFF, stg)
            x1Ts = [sb(es, "x1T%d" % i, [128, 8, 512], BF16) for i in range(2)]
            sg = [sb(es, "sg%d" % i, [128, 512], F32) for i in range(2)]
            hd = [sb(es, "hd%d" % i, [128, 512], BF16) for i in range(2)]
            bi = 0
            for j, S in enumerate(jobs):
                sc = SC[j]
                for blk in range(S // 512):
                    t0 = blk * 512
                    x1T = x1Ts[bi % 2]
                    bi += 1
                    kb.dma(x1T[:], sc["XT"].h.rearrange("(c p) s -> p c s", p=128)[:, :, t0:t0 + 512], r=[sc["XT"]], w=[x1T])
                    for f in range(NFC):
                        bg, bu = PS[2 * (f % 2)], PS[1 + 2 * (f % 2)]
                        for c in range(8):
                            kb.op("pe", lambda e, c=c, f=f, bg=bg: e.matmul(bg[:, :], lhsT=wg[:, c, f * 128:(f + 1) * 128], rhs=x1T[:, c, :],
                                                                            start=(c == 0), stop=(c == 7)), r=[wg, x1T], w=[bg])
                        for c in range(8):
                            kb.op("pe", lambda e, c=c, f=f, bu=bu: e.matmul(bu[:, :], lhsT=wu[:, c, f * 128:(f + 1) * 128], rhs=x1T[:, c, :],
                                                                            start=(c == 0), stop=(c == 7)), r=[wu, x1T], w=[bu])
                        sgt, hdt = sg[f % 2], hd[f % 2]
                        kb.op("act", lambda e, sgt=sgt, bg=bg: e.activation(out=sgt[:, :], in_=bg[:, :], func=AF.Silu), r=[bg], w=[sgt])
                        kb.op("dve", lambda e, hdt=hdt, sgt=sgt, bu=bu: e.tensor_tensor(out=hdt[:, :], in0=bu[:, :], in1=sgt[:, :], op=ALU.mult),
                              r=[bu, sgt], w=[hdt])
                        kb.dma(sc["HIDT"].h[f * 128:(f + 1) * 128, t0:t0 + 512], hdt[:, :], r=[hdt], w=[sc["HIDT"]])
            kb.barrier()
        with ExitStack() as es:
            std_psum(es)
            stg = [sb(es, "stg%d" % i, [128, 2048], F32) for i in range(2)]
            wd = sb(es, "wd", [128, NFC, D], BF16)
            load_w_bf16(es, wd, lambda c, n0, n1: wd[:, c, n0:n1], w_down[l], NFC, D, stg)
            rl = sb(es, "rl", [128, 2 * D], F32)
            kb.dma(rl[:], rep_l[l, :, 2 * D:4 * D], w=[rl])
            tmp = {"st6": sb(es, "st6", [128, 2, 6], F32), "mv": sb(es, "mv", [128, 2], F32), "rs": sb(es, "rs", [128, 1], F32)}
            HIDs = [sb(es, "HID%d" % i, [128, NFC, 512], BF16) for i in range(2)]
            x1f = [sb(es, "x1f%d" % i, [128, D], F32) for i in range(2)]
            z = [sb(es, "z%d" % i, [128, D], F32) for i in range(2)]
            of2 = [sb(es, "of2%d" % i, [128, D], F32) for i in range(2)]
            ob = [sb(es, "ob%d" % i, [128, D], BF16) for i in range(2)]
            xts = [sb(es, "xts%d" % i, [128, D], BF16) for i in range(2)]
            bi = 0
            for j, S in enumerate(jobs):
                sc = SC[j]
                X1, XNo = sc["XN"][1], sc["XN"][0]
                for blk in range(S // 512):
                    t0 = blk * 512
                    HID = HIDs[bi % 2]
                    bi += 1
                    kb.dma(HID[:], sc["HIDT"].h.rearrange("(f p) s -> p f s", p=128)[:, :, t0:t0 + 512], r=[sc["HIDT"]], w=[HID])
                    for tt in range(4):
                        tk0 = t0 + tt * 128
                        zt, obt, x1, o2, xt_ = z[tt % 2], ob[tt % 2], x1f[tt % 2], of2[tt % 2], xts[tt % 2]
                        kb.dma(x1[:], X1.h[tk0:tk0 + 128, :], r=[X1], w=[x1])
                        kb.flush()
                        for nb in range(2):
                            for f in range(NFC):
                                kb.op("pe", lambda e, f=f, nb=nb, tt=tt: e.matmul(PS[nb + 2 * (tt % 2)][:, :], lhsT=HID[:, f, tt * 128:(tt + 1) * 128],
                                                                                   rhs=wd[:, f, nb * 512:(nb + 1) * 512],
                                                                                   start=(f == 0), stop=(f == NFC - 1)), r=[HID, wd], w=[PS[nb + 2 * (tt % 2)]])
                            kb.op("dve", lambda e, nb=nb, x1=x1, zt=zt, tt=tt: e.scalar_tensor_tensor(
                                out=zt[:, nb * 512:(nb + 1) * 512], in0=x1[:, nb * 512:(nb + 1) * 512], scalar=alpha,
                                in1=PS[nb + 2 * (tt % 2)][:, :], op0=ALU.mult, op1=ALU.add), r=[x1, PS[nb + 2 * (tt % 2)]], w=[zt])
                        layer_norm_tile(zt, rl, rl, 0, D, o2, obt, tmp)
                        if last:
                            kb.dma_later(yout[j].h[tk0:tk0 + 128, :], o2[:], r=[o2], w=[yout[j]])
                        else:
                            kb.dma_later(XNo.h[tk0:tk0 + 128, :], o2[:], r=[o2], w=[XNo])
                            transpose_to_xt(obt, xt_, sc["XT"], tk0)
            kb.barrier()
    kb.barrier()
    es_glob.close()
    return nc


def _tables(jobs, smax):
    half = 16
    inv = 1.0 / (10000.0 ** (np.arange(0, 32, 2, dtype=np.float32) / 32.0))
    ang = np.arange(smax, dtype=np.float32)[:, None] * inv[None, :].astype(np.float32)
    cos = np.cos(ang).astype(np.float32).T
    sin = np.sin(ang).astype(np.float32).T
    cos2 = np.concatenate([cos, cos], 0)
    sin2 = np.concatenate([-sin, sin], 0)
    pedge = np.zeros((len(jobs), 128, 2, 16), np.float32)
    for j, S in enumerate(jobs):
        for g, w in enumerate((2, 4, 8, 16)):
            c, p0 = g // 2, (g % 2) * 64
            for i in range(8):
                t = i
                cnt = min(t + w // 2, S) - max(t - w // 2, 0)
                pedge[j, p0:p0 + 64, c, i] = 1.0 / cnt
                t = S - 8 + i
                cnt = min(t + w // 2, S) - max(t - w // 2, 0)
                pedge[j, p0:p0 + 64, c, 8 + i] = 1.0 / cnt
    s_ = np.arange(128)[:, None]
    j_ = np.arange(128)[None, :]
    maskf = np.where(s_ <= j_, 0.0, BIG).astype(np.float32)
    maskb = np.where(s_ >= j_, 0.0, BIG).astype(np.float32)
    sel = np.zeros((4, 4, 128), np.float32)
    for h in range(4):
        sel[h, h, :] = 1.0
    return dict(cos2=np.ascontiguousarray(cos2), sin2=np.ascontiguousarray(sin2), pedge=pedge, maskf=maskf, maskb=maskb,
                sel=sel, ident=np.eye(128, dtype=np.float32))


def _common_inputs(inp, L, jobs, smax):
    f = lambda a: np.ascontiguousarray(np.asarray(a, dtype=np.float32))
    rep = lambda v: np.broadcast_to(f(v)[None, :], (128, f(v).shape[0]))
    m = {}
    for k in ("w_in", "w_uq", "w_ukv", "w_pool", "w_out", "w_gate", "w_up", "w_down"):
        m[k] = f(inp[k])[:L]
    m["rep_in"] = np.ascontiguousarray(np.concatenate([rep(inp["ln_in_g"]), rep(inp["ln_in_b"])], 1))
    m["rep_l"] = np.ascontiguousarray(np.stack([
        np.concatenate([rep(inp["ln1_g"][l]), rep(inp["ln1_b"][l]), rep(inp["ln2_g"][l]), rep(inp["ln2_b"][l]),
                        rep(inp["mlstm_norm_g"][l])], 1) for l in range(L)]))
    m["qg"] = np.ascontiguousarray(f(inp["q_norm_g"])[:L].reshape(L, 2, 128).transpose(0, 2, 1))
    m["kvg"] = np.ascontiguousarray(f(inp["kv_norm_g"])[:L].reshape(L, 128, 1))
    m["psc"] = np.ascontiguousarray(f(inp["pool_scale"])[:L].reshape(L, 4, 64).transpose(0, 2, 1))
    m["gbias"] = np.ascontiguousarray(f(inp["mlstm_gate_bias"])[:L].reshape(L, 4, 4).transpose(0, 2, 1))
    m.update(_tables(jobs, smax))
    return m


_CACHE = {}


def run(inp, L, job_inputs_per_core, jobs, dbg=False):
    smax = max(jobs)
    key = (L, tuple(jobs))
    if key not in _CACHE:
        _CACHE[key] = build_program(L, jobs, smax, dbg)
    nc = _CACHE[key]
    common = _common_inputs(inp, L, jobs, smax)
    in_maps = []
    for xs in job_inputs_per_core:
        m = dict(common)
        for j, x in enumerate(xs):
            m["xin%d" % j] = np.ascontiguousarray(x, dtype=np.float32)
        in_maps.append(m)
    res = run_bass_kernel_spmd(nc, in_maps, core_ids=list(range(len(in_maps))))
    if dbg:
        return [[r["yout%d" % j] for j in range(len(jobs))] + [r["dbg_mixt"]] for r in res.results]
    return [[r["yout%d" % j] for j in range(len(jobs))] for r in res.results]


def kernel(**inputs):
    xp = np.asarray(inputs["x_prompt"], dtype=np.float32)
    xs = np.asarray(inputs["x_sample"], dtype=np.float32)
    L = int(np.asarray(inputs["w_in"]).shape[0])
    jobs = [xp.shape[1], xs.shape[1], xs.shape[1]]
    per_core = [[xp[c % 2], xs[2 * c], xs[2 * c + 1]] for c in range(8)]
    outs = run(inputs, L, per_core, jobs)
    y_prompt = np.stack([outs[0][0], outs[1][0]], 0).astype(np.float32)
    y_sample = np.stack([outs[c][1 + i] for c in range(8) for i in range(2)], 0).astype(np.float32)
    return (y_prompt, y_sample)
```

```python
import numpy as np
from contextlib import ExitStack
import concourse.bass as bass
import concourse.mybir as mybir
from concourse.bass_utils import run_bass_kernel_spmd

F32 = mybir.dt.float32
BF16 = mybir.dt.bfloat16
AF = mybir.ActivationFunctionType
ALU = mybir.AluOpType

D = 1024
NQ, NKV, NR, NP_, NM, NG = 256, 128, 32, 256, 256, 16
IN_W = 1712
DFF = 2816
NFC = DFF // 128
EPS = 1e-5
BIG = 1e30
SAME_ENG_SYNC = True


class Buf:
    __slots__ = ("w", "r")

    def __init__(self):
        self.w = None
        self.r = {}


class T:
    def __init__(self, h):
        self.h = h
        self.b = Buf()

    def __getitem__(self, idx):
        return self.h[idx]


class KB:
    ND = 8

    def __init__(self, nc):
        self.nc = nc
        self.E = {"pe": nc.tensor, "act": nc.scalar, "dve": nc.vector, "pool": nc.gpsimd, "sp": nc.sync}
        self.sem = {}
        self.val = {}
        for e in self.E:
            self.sem[e] = nc.alloc_semaphore(name="c_" + e)
            self.val[e] = 0
        for q in ("sp", "pool"):
            for i in range(self.ND):
                self.sem[(q, i)] = nc.alloc_semaphore(name="d_%s%d" % (q, i))
                self.val[(q, i)] = 0
        self.dslot = {"sp": 0, "pool": 0}
        self.waited = {e: {} for e in self.E}
        self.pending = []

    def _wait(self, e, sk, v):
        if v <= 0:
            return
        if self.waited[e].get(sk, 0) < v:
            self.E[e].wait_ge(self.sem[sk], v)
            self.waited[e][sk] = v

    def _deps(self, e, r, w):
        deps = {}
        for t in list(r) + list(w):
            ev = t.b.w
            if ev is not None:
                deps[ev[0]] = max(deps.get(ev[0], 0), ev[1])
        for t in w:
            for sk, v in t.b.r.items():
                deps[sk] = max(deps.get(sk, 0), v)
        for sk, v in deps.items():
            if sk == e and (e == "pe" or not SAME_ENG_SYNC):
                continue
            self._wait(e, sk, v)

    def _mark(self, ev, r, w):
        for t in r:
            t.b.r[ev[0]] = max(t.b.r.get(ev[0], 0), ev[1])
        for t in w:
            t.b.w = ev
            t.b.r = {}

    def op(self, e, fn, r=(), w=()):
        self._deps(e, r, w)
        ins = fn(self.E[e])
        self.val[e] += 1
        ins.then_inc(self.sem[e], 1)
        self._mark((e, self.val[e]), r, w)

    def dma(self, out, in_, r=(), w=(), q="sp"):
        self._deps(q, r, w)
        slot = self.dslot[q]
        self.dslot[q] = (slot + 1) % self.ND
        sk = (q, slot)
        self._wait(q, sk, self.val[sk])
        ins = self.E[q].dma_start(out=out, in_=in_)
        self.val[sk] += 16
        ins.then_inc(self.sem[sk], 16)
        self._mark((sk, self.val[sk]), r, w)

    def dma_later(self, out, in_, r=(), w=()):
        self.pending.append((out, in_, r, w))

    def flush(self):
        p, self.pending = self.pending, []
        for (out, in_, r, w) in p:
            self.dma(out, in_, r=r, w=w)

    def barrier(self):
        self.flush()
        for e in self.E:
            for sk in self.sem:
                if sk != e:
                    self._wait(e, sk, self.val[sk])


def build_program(L, jobs, smax, dbg=False):
    nc = bass.Bass("TRN2", target_bir_lowering=False)
    kb = KB(nc)
    NJ = len(jobs)

    def din(name, shape, dt=F32):
        return nc.dram_tensor(name, list(shape), dt, kind="ExternalInput").ap()

    def dscr(name, shape, dt):
        return T(nc.dram_tensor(name, list(shape), dt).ap())

    xin = [din("xin%d" % j, [S, D]) for j, S in enumerate(jobs)]
    yout = [T(nc.dram_tensor("yout%d" % j, [S, D], F32, kind="ExternalOutput").ap()) for j, S in enumerate(jobs)]
    w_in = din("w_in", [L, D, IN_W])
    w_uq = din("w_uq", [L, NQ, 768])
    w_ukv = din("w_ukv", [L, NKV, 1024])
    w_pool = din("w_pool", [L, 4, 64, 64])
    w_out = din("w_out", [L, D, D])
    w_gate = din("w_gate", [L, D, DFF])
    w_up = din("w_up", [L, D, DFF])
    w_down = din("w_down", [L, DFF, D])
    rep_in = din("rep_in", [128, 2 * D])
    rep_l = din("rep_l", [L, 128, 4 * D + 256])
    qg = din("qg", [L, 128, 2])
    kvg = din("kvg", [L, 128, 1])
    psc = din("psc", [L, 64, 4])
    gbias = din("gbias", [L, 4, 4])
    cos2 = din("cos2", [32, smax])
    sin2 = din("sin2", [32, smax])
    pedge = din("pedge", [NJ, 128, 2, 16])
    maskf_d = din("maskf", [128, 128])
    maskb_d = din("maskb", [128, 128])
    sel_d = din("sel", [4, 4, 128])
    ident_d = din("ident", [128, 128])

    SC = []
    for j, S in enumerate(jobs):
        s = {}
        s["XN"] = [dscr("XN%d_%d" % (i, j), [S, D], F32) for i in range(2)]
        s["XT"] = dscr("XT%d" % j, [D, S], BF16)
        s["CQT"] = dscr("CQT%d" % j, [NQ, S], BF16)
        s["CKVT"] = dscr("CKVT%d" % j, [NKV, S], BF16)
        s["KRT"] = dscr("KRT%d" % j, [32, S], BF16)
        s["POOLT"] = dscr("POOLT%d" % j, [256, S], F32)
        s["QMT"] = dscr("QMT%d" % j, [256, S], BF16)
        s["KMT"] = dscr("KMT%d" % j, [256, S], BF16)
        s["G4"] = dscr("G4%d" % j, [4, 4, S], F32)
        s["VM"] = dscr("VM%d" % j, [S, 256], BF16)
        s["KM"] = dscr("KM%d" % j, [S, 256], BF16)
        s["OM"] = dscr("OM%d" % j, [S, 256], F32)
        s["MIXT"] = dscr("MIXT%d" % j, [D, S], BF16)
        s["HF"] = dscr("HF%d" % j, [S, 256], F32)
        s["HIDT"] = dscr("HIDT%d" % j, [DFF, S], BF16)
        SC.append(s)

    es_glob = ExitStack()
    uniq = [0]

    def sb(es, name, shape, dt):
        uniq[0] += 1
        return T(es.enter_context(nc.sbuf_tensor("%s_%d" % (name, uniq[0]), list(shape), dt)))

    PS = [None] * 7
    PSBh = [None]

    class _PSB:
        @property
        def h(self):
            return PSBh[0].h

        @property
        def b(self):
            return PSBh[0].b

        def __getitem__(self, idx):
            return PSBh[0].h[idx]

    PSB = _PSB()

    def std_psum(es):
        for i in range(7):
            uniq[0] += 1
            PS[i] = T(es.enter_context(nc.psum_tensor("ps%d_%d" % (i, uniq[0]), [128, 512], F32)))
        uniq[0] += 1
        PSBh[0] = T(es.enter_context(nc.psum_tensor("psb_%d" % uniq[0], [128, 1024], BF16)))

    identf = sb(es_glob, "identf", [128, 128], F32)
    identb = sb(es_glob, "identb", [128, 128], BF16)
    onesf = sb(es_glob, "onesf", [128, 128], F32)
    onesb = sb(es_glob, "onesb", [128, 128], BF16)
    kb.dma(identf[:], ident_d[:, :], w=[identf])
    kb.op("dve", lambda e: e.tensor_copy(out=identb[:], in_=identf[:]), r=[identf], w=[identb])
    kb.op("dve", lambda e: e.memset(onesf[:], 1.0), w=[onesf])
    kb.op("dve", lambda e: e.memset(onesb[:], 1.0), w=[onesb])

    stg_ctr = [0]

    def load_w_bf16(es_stage, dst, dst_ap_fn, src_ap, nk, ncols, stg):
        CW = 2048
        for c in range(nk):
            for n0 in range(0, ncols, CW):
                n1 = min(ncols, n0 + CW)
                st = stg[stg_ctr[0] % len(stg)]
                stg_ctr[0] += 1
                kb.dma(st[:, 0:n1 - n0], src_ap[c * 128:(c + 1) * 128, n0:n1], w=[st])
                eng = "pool" if (stg_ctr[0] % 2) else "dve"
                kb.op(eng, lambda e, st=st, c=c, n0=n0, n1=n1: e.tensor_copy(out=dst_ap_fn(c, n0, n1), in_=st[:, 0:n1 - n0]),
                      r=[st], w=[dst])

    def rsqrt_eps(t, out_ap, in_ap, rd, scale=1.0):
        kb.op("dve", lambda e: e.tensor_scalar(out=out_ap, in0=in_ap, scalar1=scale, scalar2=EPS, op0=ALU.mult, op1=ALU.add),
              r=rd, w=[t])
        kb.op("act", lambda e: e.activation(out=out_ap, in_=out_ap, func=AF.Sqrt), r=[], w=[t])
        kb.op("dve", lambda e: e.reciprocal(out=out_ap, in_=out_ap), r=[], w=[t])

    def layer_norm_tile(z, gt, bt, g_off, b_off, outf, outb, tmp):
        st6, mv, rs = tmp["st6"], tmp["mv"], tmp["rs"]
        for hh in range(2):
            kb.op("dve", lambda e, hh=hh: e.bn_stats(out=st6[:, hh, :], in_=z[:, hh * 512:(hh + 1) * 512]), r=[z], w=[st6])
        kb.op("dve", lambda e: e.bn_aggr(out=mv[:, :], in_=st6[:, :, :]), r=[st6], w=[mv])
        rsqrt_eps(rs, rs[:, :], mv[:, 1:2], [mv])
        kb.op("dve", lambda e: e.tensor_scalar(out=outf[:, :], in0=z[:, :], scalar1=mv[:, 0:1], scalar2=rs[:, 0:1],
                                               op0=ALU.subtract, op1=ALU.mult), r=[z, mv, rs], w=[outf])
        kb.op("pool", lambda e: e.tensor_tensor(out=outf[:, :], in0=outf[:, :], in1=gt[:, g_off:g_off + D], op=ALU.mult),
              r=[gt], w=[outf])
        kb.op("dve", lambda e: e.tensor_tensor(out=outf[:, :], in0=outf[:, :], in1=bt[:, b_off:b_off + D], op=ALU.add),
              r=[bt], w=[outf])
        kb.op("act", lambda e: e.activation(out=outb[:, :], in_=outf[:, :], func=AF.Copy), r=[outf], w=[outb])

    def transpose_to_xt(outb, xts, XT_T, tok0):
        for c in range(8):
            kb.op("pe", lambda e, c=c: e.transpose(out=PSB[:, c * 128:(c + 1) * 128], in_=outb[:, c * 128:(c + 1) * 128],
                                                   identity=identb[:]), r=[outb, identb], w=[PSB])
        kb.op("act", lambda e: e.activation(out=xts[:, :], in_=PSB[:, :], func=AF.Copy), r=[PSB], w=[xts])
        kb.dma_later(XT_T.h.rearrange("(c p) s -> p c s", p=128)[:, :, tok0:tok0 + 128],
                     xts.h.rearrange("p (c t) -> p c t", c=8), r=[xts], w=[XT_T])

    with ExitStack() as es:
        std_psum(es)
        rin = sb(es, "rin", [128, 2 * D], F32)
        kb.dma(rin[:], rep_in[:, :], w=[rin])
        tmp = {"st6": sb(es, "st6", [128, 2, 6], F32), "mv": sb(es, "mv", [128, 2], F32), "rs": sb(es, "rs", [128, 1], F32)}
        zs = [sb(es, "pz%d" % i, [128, D], F32) for i in range(2)]
        ofs = [sb(es, "pof%d" % i, [128, D], F32) for i in range(2)]
        obs = [sb(es, "pob%d" % i, [128, D], BF16) for i in range(2)]
        xtss = [sb(es, "pxt%d" % i, [128, D], BF16) for i in range(2)]
        it = 0
        for j, S in enumerate(jobs):
            for t in range(S // 128):
                z, of, ob, xts = zs[it % 2], ofs[it % 2], obs[it % 2], xtss[it % 2]
                it += 1
                kb.dma(z[:], xin[j][t * 128:(t + 1) * 128, :], w=[z])
                kb.flush()
                layer_norm_tile(z, rin, rin, 0, D, of, ob, tmp)
                kb.dma_later(SC[j]["XN"][0].h[t * 128:(t + 1) * 128, :], of[:], r=[of], w=[SC[j]["XN"][0]])
                transpose_to_xt(ob, xts, SC[j]["XT"], t * 128)
        kb.barrier()

    for l in range(L):
        last = l == L - 1
        with ExitStack() as es:
            std_psum(es)
            stg = [sb(es, "stg%d" % i, [128, 2048], F32) for i in range(2)]
            win = sb(es, "win", [128, 8, IN_W], BF16)
            load_w_bf16(es, win, lambda c, n0, n1: win[:, c, n0:n1], w_in[l], 8, IN_W, stg)
            wkr_sw = sb(es, "wkrsw", [128, 8, 96], BF16)
            kb.op("dve", lambda e: e.tensor_copy(out=wkr_sw[:, :, 0:64], in_=win[:, :, 320:384]), r=[win], w=[wkr_sw])
            kb.op("dve", lambda e: e.tensor_copy(out=wkr_sw[:, :, 64:80], in_=win[:, :, 400:416]), r=[win], w=[wkr_sw])
            kb.op("dve", lambda e: e.tensor_copy(out=wkr_sw[:, :, 80:96], in_=win[:, :, 384:400]), r=[win], w=[wkr_sw])
            qgt = sb(es, "qgt", [128, 2], F32)
            kvgt = sb(es, "kvgt", [128, 1], F32)
            kb.dma(qgt[:], qg[l], w=[qgt])
            kb.dma(kvgt[:], kvg[l], w=[kvgt])
            xTs = [sb(es, "xT%d" % i, [128, 8, 512], BF16) for i in range(2)]
            cst = [sb(es, "cs%d" % i, [96, 512], F32) for i in range(2)]
            snt = [sb(es, "sn%d" % i, [96, 512], F32) for i in range(2)]
            sq = sb(es, "sq", [128, 512], BF16)
            rstd = sb(es, "rstd", [128, 512], F32)
            ev = [sb(es, "ev%d" % i, [128, 512], BF16) for i in range(2)]
            evf = [sb(es, "evf%d" % i, [128, 512], F32) for i in range(2)]
            kr1 = sb(es, "kr1", [96, 512], F32)
            kr2 = sb(es, "kr2", [96, 512], F32)
            krb = sb(es, "krb", [96, 512], BF16)
            tmb = [sb(es, "tmb%d" % i, [128, 512], BF16) for i in range(2)]
            tmf = [sb(es, "tmf%d" % i, [128, 256], F32) for i in range(2)]
            evc = 0
            for j, S in enumerate(jobs):
                sc = SC[j]
                for blk in range(S // 512):
                    t0 = blk * 512
                    xT = xTs[blk % 2]
                    kb.dma(xT[:], sc["XT"].h.rearrange("(c p) s -> p c s", p=128)[:, :, t0:t0 + 512], r=[sc["XT"]], w=[xT])
                    cs, sn = cst[blk % 2], snt[blk % 2]
                    kb.dma(cs[64:96, :], cos2[:, t0:t0 + 512], w=[cs])
                    kb.dma(sn[64:96, :], sin2[:, t0:t0 + 512], w=[sn])

                    def fm(col0, m, bank, lhs=None):
                        for c in range(8):
                            lt = (win[:, c, col0:col0 + m] if lhs is None else lhs[:, c, 0:m])
                            kb.op("pe", lambda e, c=c, lt=lt: e.matmul(PS[bank][0:m, :], lhsT=lt, rhs=xT[:, c, :],
                                                                        start=(c == 0), stop=(c == 7)),
                                  r=[win if lhs is None else lhs, xT], w=[PS[bank]])

                    for (col0, nchunk, gtile, dst, key) in ((0, 2, qgt, "CQT", "q"), (256, 1, kvgt, "CKVT", "kv")):
                        for cc in range(nchunk):
                            fm(col0 + cc * 128, 128, cc)
                        for cc in range(nchunk):
                            kb.op("act", lambda e, cc=cc: e.activation(out=sq[:, :], in_=PS[cc][:, :], func=AF.Square),
                                  r=[PS[cc]], w=[sq])
                            kb.op("pe", lambda e, cc=cc: e.matmul(PS[2][:, :], lhsT=onesb[:, :], rhs=sq[:, :],
                                                                  start=(cc == 0), stop=(cc == nchunk - 1)),
                                  r=[onesb, sq], w=[PS[2]])
                        nfeat = 128.0 * nchunk
                        rsqrt_eps(rstd, rstd[:, :], PS[2][:, :], [PS[2]], scale=1.0 / nfeat)
                        for cc in range(nchunk):
                            o = ev[evc % 2]
                            evc += 1
                            kb.op("dve", lambda e, cc=cc, o=o, gtile=gtile: e.scalar_tensor_tensor(
                                out=o[:, :], in0=PS[cc][:, :], scalar=gtile[:, cc:cc + 1], in1=rstd[:, :],
                                op0=ALU.mult, op1=ALU.mult), r=[PS[cc], gtile, rstd], w=[o])
                            kb.dma(sc[dst].h[cc * 128:(cc + 1) * 128, t0:t0 + 512], o[:, :], r=[o], w=[sc[dst]])
                    fm(320, 96, 3)
                    fm(0, 96, 4, lhs=wkr_sw)
                    kb.op("dve", lambda e: e.tensor_tensor(out=kr1[64:96, :], in0=PS[3][64:96, :], in1=cs[64:96, :], op=ALU.mult),
                          r=[PS[3], cs], w=[kr1])
                    kb.op("dve", lambda e: e.tensor_tensor(out=kr2[64:96, :], in0=PS[4][64:96, :], in1=sn[64:96, :], op=ALU.mult),
                          r=[PS[4], sn], w=[kr2])
                    kb.op("pool", lambda e: e.tensor_tensor(out=krb[64:96, :], in0=kr1[64:96, :], in1=kr2[64:96, :], op=ALU.add),
                          r=[kr1, kr2], w=[krb])
                    kb.dma(sc["KRT"].h[:, t0:t0 + 512], krb[64:96, :], r=[krb], w=[sc["KRT"]])
                    bi = 0
                    for (col0, dst, kind) in ((416, "POOLT", "f"), (544, "POOLT", "f"), (672, "QMT", "b"), (800, "QMT", "b"),
                                              (928, "KMT", "k"), (1056, "KMT", "k")):
                        bank = 5 + (bi % 2)
                        bi += 1
                        fm(col0, 128, bank)
                        r0 = ((col0 - 416) % 256) if dst == "POOLT" else ((col0 - 672) % 256)
                        if kind == "f":
                            o = evf[evc % 2]
                            evc += 1
                            kb.op("act", lambda e, o=o, bank=bank: e.activation(out=o[:, :], in_=PS[bank][:, :], func=AF.Copy),
                                  r=[PS[bank]], w=[o])
                        else:
                            o = ev[evc % 2]
                            evc += 1
                            scl = 0.125 if kind == "k" else 1.0
                            kb.op("act", lambda e, o=o, bank=bank, scl=scl: e.activation(out=o[:, :], in_=PS[bank][:, :],
                                                                                           func=AF.Copy, scale=scl),
                                  r=[PS[bank]], w=[o])
                        kb.dma(sc[dst].h[r0:r0 + 128, t0:t0 + 512], o[:, :], r=[o], w=[sc[dst]])
                    for ty in range(4):
                        fm(1696 + 4 * ty, 4, 3 + (ty % 2))
                        o = evf[evc % 2]
                        evc += 1
                        bank = 3 + (ty % 2)
                        kb.op("act", lambda e, o=o, bank=bank: e.activation(out=o[0:4, :], in_=PS[bank][0:4, :], func=AF.Copy),
                              r=[PS[bank]], w=[o])
                        kb.dma(sc["G4"].h[ty, :, t0:t0 + 512], o[0:4, :], r=[o], w=[sc["G4"]])
                    for tt in range(4):
                        tk0 = t0 + tt * 128
                        for c in range(8):
                            kb.op("pe", lambda e, c=c, tt=tt: e.matmul(PS[0][:, :], lhsT=xT[:, c, tt * 128:(tt + 1) * 128],
                                                                        rhs=win[:, c, 928:1440], start=(c == 0), stop=(c == 7)),
                                  r=[xT, win], w=[PS[0]])
                        for c in range(8):
                            kb.op("pe", lambda e, c=c, tt=tt: e.matmul(PS[1][:, 0:256], lhsT=xT[:, c, tt * 128:(tt + 1) * 128],
                                                                        rhs=win[:, c, 1440:1696], start=(c == 0), stop=(c == 7)),
                                  r=[xT, win], w=[PS[1]])
                        ob = tmb[tt % 2]
                        of = tmf[tt % 2]
                        kb.op("act", lambda e, ob=ob: e.activation(out=ob[:, 0:256], in_=PS[0][:, 0:256], func=AF.Copy, scale=0.125),
                              r=[PS[0]], w=[ob])
                        kb.op("dve", lambda e, ob=ob: e.tensor_copy(out=ob[:, 256:512], in_=PS[0][:, 256:512]), r=[PS[0]], w=[ob])
                        kb.op("act", lambda e, of=of: e.activation(out=of[:, :], in_=PS[1][:, 0:256], func=AF.Copy), r=[PS[1]], w=[of])
                        kb.dma(sc["KM"].h[tk0:tk0 + 128, :], ob[:, 0:256], r=[ob], w=[sc["KM"]])
                        kb.dma(sc["VM"].h[tk0:tk0 + 128, :], ob[:, 256:512], r=[ob], w=[sc["VM"]])
                        kb.dma(sc["OM"].h[tk0:tk0 + 128, :], of[:, :], r=[of], w=[sc["OM"]])
            kb.barrier()

        for j, S in enumerate(jobs):
            sc = SC[j]
            NKC = S // 128
            NQB = S // 512
            with ExitStack() as es:
                uniq[0] += 1
                STT = [T(es.enter_context(nc.psum_tensor("st%d_%d" % (i_, uniq[0]), [128, 1024], F32))) for i_ in range(3)]
                OTT = T(es.enter_context(nc.psum_tensor("ot_%d" % uniq[0], [128, 512], F32)))
                MSC = T(es.enter_context(nc.psum_tensor("msc_%d" % uniq[0], [128, 512], F32)))
                stg = [sb(es, "stg%d" % i, [128, 2048], F32) for i in range(2)]
                wq = sb(es, "wq", [128, 2, 768], BF16)
                wqs = sb(es, "wqs", [128, 2, 768], BF16)
                wkv = sb(es, "wkv", [128, 1, 1024], BF16)
                load_w_bf16(es, wq, lambda c, n0, n1: wq[:, c, n0:n1], w_uq[l], 2, 768, stg)
                load_w_bf16(es, wkv, lambda c, n0, n1: wkv[:, c, n0:n1], w_ukv[l], 1, 1024, stg)
                wq4 = wq.h.rearrange("p c (h d) -> p c h d", h=8)
                wqs4 = wqs.h.rearrange("p c (h d) -> p c h d", h=8)
                kb.op("dve", lambda e: e.tensor_copy(out=wqs[:, :, :], in_=wq[:, :, :]), r=[wq], w=[wqs])
                kb.op("dve", lambda e: e.tensor_copy(out=wqs4[:, :, :, 64:80], in_=wq4[:, :, :, 80:96]), r=[wq], w=[wqs])
                kb.op("dve", lambda e: e.tensor_copy(out=wqs4[:, :, :, 80:96], in_=wq4[:, :, :, 64:80]), r=[wq], w=[wqs])
                ckv = sb(es, "ckv", [128, S], BF16)
                KT = sb(es, "KT", [96, S], BF16)
                VA = sb(es, "VA", [128, NKC, 65], BF16)
                kb.dma(ckv[:], sc["CKVT"].h[:, :], r=[sc["CKVT"]], w=[ckv])
                kb.dma(KT[64:96, :], sc["KRT"].h[:, :], r=[sc["KRT"]], w=[KT])
                kb.op("dve", lambda e: e.memset(VA[:, :, 64:65], 1.0), w=[VA])
                cqs = [sb(es, "cq%d" % i, [128, 2, 512], BF16) for i in range(2)]
                cst = [sb(es, "acs%d" % i, [96, 512], F32) for i in range(2)]
                snt = [sb(es, "asn%d" % i, [96, 512], F32) for i in range(2)]
                QTs = [sb(es, "QT%d" % i, [96, 512], BF16) for i in range(2)]
                q1 = sb(es, "q1", [96, 512], F32)
                q2 = sb(es, "q2", [96, 512], F32)
                PTs = [sb(es, "PT%d" % i, [128, 1024], BF16) for i in range(3)]
                den = sb(es, "den", [65, 512], F32)
                bcs = sb(es, "bcs", [64, 512], F32)
                ots = [sb(es, "ot%d" % i, [64, 512], BF16) for i in range(2)]
                scale = 96.0 ** -0.5
                qbc = 0
                otcs = [sb(es, "otc%d" % i, [65, 512], F32) for i in range(2)]
                qt_ready = {}

                def build_q(h, qb, slot):
                    q0 = qb * 512
                    cq, cs, sn, QT = cqs[slot], cst[slot], snt[slot], QTs[slot]
                    kb.dma(cq[:], sc["CQT"].h.rearrange("(c p) s -> p c s", p=128)[:, :, q0:q0 + 512], r=[sc["CQT"]], w=[cq])
                    kb.dma(cs[64:96, :], cos2[:, q0:q0 + 512], w=[cs])
                    kb.dma(sn[64:96, :], sin2[:, q0:q0 + 512], w=[sn])
                    for c in range(2):
                        kb.op("pe", lambda e, c=c: e.matmul(MSC[0:96, :], lhsT=wq[:, c, h * 96:(h + 1) * 96], rhs=cq[:, c, :],
                                                            start=(c == 0), stop=(c == 1)), r=[wq, cq], w=[MSC])
                    kb.op("dve", lambda e: e.tensor_copy(out=QT[0:64, :], in_=MSC[0:64, :]), r=[MSC], w=[QT])
                    kb.op("dve", lambda e: e.tensor_tensor(out=q1[64:96, :], in0=MSC[64:96, :], in1=cs[64:96, :], op=ALU.mult),
                          r=[MSC, cs], w=[q1])
                    for c in range(2):
                        kb.op("pe", lambda e, c=c: e.matmul(MSC[0:96, :], lhsT=wqs[:, c, h * 96:(h + 1) * 96], rhs=cq[:, c, :],
                                                            start=(c == 0), stop=(c == 1)), r=[wqs, cq], w=[MSC])
                    kb.op("dve", lambda e: e.tensor_tensor(out=q2[64:96, :], in0=MSC[64:96, :], in1=sn[64:96, :], op=ALU.mult),
                          r=[MSC, sn], w=[q2])
                    kb.op("pool", lambda e: e.tensor_tensor(out=QT[64:96, :], in0=q1[64:96, :], in1=q2[64:96, :], op=ALU.add),
                          r=[q1, q2], w=[QT])
                    return QT
                for h in range(8):
                    for blk in range(NQB):
                        kb.op("pe", lambda e, blk=blk: e.matmul(MSC[0:64, :], lhsT=wkv[:, 0, h * 128:h * 128 + 64],
                                                                 rhs=ckv[:, blk * 512:(blk + 1) * 512], start=True, stop=True),
                              r=[wkv, ckv], w=[MSC])
                        kb.op("dve", lambda e, blk=blk: e.tensor_copy(out=KT[0:64, blk * 512:(blk + 1) * 512], in_=MSC[0:64, :]),
                              r=[MSC], w=[KT])
                    for g in range(NKC // 8):
                        for i in range(8):
                            kc = g * 8 + i
                            kb.op("pe", lambda e, kc=kc, i=i: e.matmul(MSC[:, i * 64:(i + 1) * 64], lhsT=ckv[:, kc * 128:(kc + 1) * 128],
                                                                        rhs=wkv[:, 0, h * 128 + 64:h * 128 + 128], start=True, stop=True),
                                  r=[wkv, ckv], w=[MSC])
                        kb.op("act", lambda e, g=g: e.activation(out=VA[:, g * 8:(g + 1) * 8, 0:64],
                                                                 in_=MSC.h.rearrange("p (i d) -> p i d", i=8), func=AF.Copy),
                              r=[MSC], w=[VA])
                    for qb in range(NQB):
                        q0 = qb * 512
                        OT = OTT
                        ot = ots[qbc % 2]
                        otc = otcs[qbc % 2]
                        if (h, qb) in qt_ready:
                            QT = qt_ready.pop((h, qb))
                        else:
                            QT = build_q(h, qb, qbc % 2)
                        nxt = (h, qb + 1) if qb + 1 < NQB else ((h + 1, 0) if h + 1 < 8 else None)
                        nslot = (qbc + 1) % 2
                        qbc += 1
                        npair = NKC // 2

                        def mm1(p):
                            st = STT[p % 3]
                            for i in range(2):
                                kc = 2 * p + i
                                kb.op("pe", lambda e, kc=kc, i=i: e.matmul(st[:, i * 512:(i + 1) * 512], lhsT=KT[0:96, kc * 128:(kc + 1) * 128],
                                                                            rhs=QT[0:96, :], start=True, stop=True),
                                      r=[KT, QT], w=[st])

                        def ex(p):
                            st = STT[p % 3]
                            pt = PTs[p % 3]
                            kb.op("act", lambda e: e.activation(out=pt[:, :], in_=st[:, :], func=AF.Exp, scale=scale),
                                  r=[st], w=[pt])

                        def mm2(p):
                            pt = PTs[p % 3]
                            for i in range(2):
                                kc = 2 * p + i
                                kb.op("pe", lambda e, kc=kc, i=i: e.matmul(OT[0:65, :], lhsT=VA[:, kc, :], rhs=pt[:, i * 512:(i + 1) * 512],
                                                                            start=(kc == 0), stop=(kc == NKC - 1)),
                                      r=[VA, pt], w=[OT])

                        mm1(0)
                        if npair > 1:
                            mm1(1)
                        if nxt is not None:
                            qt_ready[nxt] = build_q(nxt[0], nxt[1], nslot)
                        for p in range(npair):
                            ex(p)
                            if p + 2 < npair:
                                mm1(p + 2)
                            mm2(p)
                        kb.op("dve", lambda e: e.tensor_copy(out=otc[0:65, :], in_=OT[0:65, :]), r=[OT], w=[otc])
                        kb.op("dve", lambda e: e.reciprocal(out=den[64:65, :], in_=otc[64:65, :]), r=[otc], w=[den])
                        kb.op("pe", lambda e: e.matmul(MSC[0:64, :], lhsT=onesf[64:65, 0:64], rhs=den[64:65, :], start=True, stop=True),
                              r=[onesf, den], w=[MSC])
                        kb.op("dve", lambda e: e.tensor_tensor(out=ot[:, :], in0=otc[0:64, :], in1=MSC[0:64, :], op=ALU.mult),
                              r=[otc, MSC], w=[ot])
                        kb.dma(sc["MIXT"].h[h * 64:(h + 1) * 64, q0:q0 + 512], ot[:, :], r=[ot], w=[sc["MIXT"]])
                kb.barrier()

        with ExitStack() as es:
            std_psum(es)
            stg = [sb(es, "stg%d" % i, [128, 2048], F32) for i in range(2)]
            wp = sb(es, "wp", [128, 2, 64], BF16)
            for g in range(4):
                st = stg[g % 2]
                p0 = (g % 2) * 64
                kb.dma(st[p0:p0 + 64, 0:64], w_pool[l, g], w=[st])
                kb.op("dve", lambda e, st=st, p0=p0, g=g: e.tensor_copy(out=wp[p0:p0 + 64, g // 2, :], in_=st[p0:p0 + 64, 0:64]),
                      r=[st], w=[wp])
            psct = sb(es, "psct", [64, 4], F32)
            kb.dma(psct[:], psc[l], w=[psct])
            PB = 2048
            xps = [sb(es, "xp%d" % i, [128, 2, PB + 16], F32) for i in range(2)]
            a2 = sb(es, "a2", [128, PB + 16], F32)
            a4 = sb(es, "a4", [128, PB + 16], F32)
            yb = [sb(es, "yb%d" % i, [128, PB], BF16) for i in range(2)]
            yf = sb(es, "yf", [128, PB], F32)
            ped = sb(es, "ped", [128, 2, 16], F32)
            po = [sb(es, "po%d" % i, [64, 512], BF16) for i in range(2)]
            WINS = (2, 4, 8, 16)
            bc = 0
            for j, S in enumerate(jobs):
                sc = SC[j]
                kb.dma(ped[:], pedge[j], w=[ped])
                pb = min(PB, S)
                for blk in range(S // pb):
                    t0 = blk * pb
                    xp = xps[bc % 2]
                    bc += 1
                    lo = max(0, t0 - 8)
                    hi = min(S, t0 + pb + 8)
                    if lo > t0 - 8:
                        kb.op("pool", lambda e: e.memset(xp[:, :, 0:8], 0.0), w=[xp])
                    if hi < t0 + pb + 8:
                        kb.op("pool", lambda e: e.memset(xp[:, :, pb + 8:pb + 16], 0.0), w=[xp])
                    kb.dma(xp[:, :, 8 + (lo - t0):8 + (hi - t0)],
                           sc["POOLT"].h.rearrange("(c p) s -> p c s", p=128)[:, :, lo:hi], r=[sc["POOLT"]], w=[xp])
                    for c in range(2):
                        n = pb + 16
                        x = xp.h[:, c, :]
                        kb.op("dve", lambda e, x=x: e.tensor_tensor(out=a2[:, 0:n - 1], in0=x[:, 0:n - 1], in1=x[:, 1:n], op=ALU.add),
                              r=[xp], w=[a2])
                        kb.op("pool", lambda e: e.tensor_tensor(out=a4[:, 0:n - 3], in0=a2[:, 0:n - 3], in1=a2[:, 2:n - 1], op=ALU.add),
                              r=[a2], w=[a4])
                        if c == 1:
                            kb.op("dve", lambda e: e.tensor_tensor(out=a2[:, 0:n - 7], in0=a4[:, 0:n - 7], in1=a4[:, 4:n - 3], op=ALU.add),
                                  r=[a4], w=[a2])
                            kb.op("pool", lambda e: e.tensor_tensor(out=a4[64:128, 0:n - 15], in0=a2[64:128, 0:n - 15],
                                                                    in1=a2[64:128, 8:n - 7], op=ALU.add), r=[a2], w=[a4])
                        for half in range(2):
                            w = WINS[2 * c + half]
                            src = a2 if half == 0 else a4
                            p0 = half * 64
                            o0 = 8 - w // 2
                            kb.op("dve", lambda e, src=src, p0=p0, w=w, o0=o0, x=x: e.scalar_tensor_tensor(
                                out=yf[p0:p0 + 64, 0:pb], in0=src[p0:p0 + 64, o0:o0 + pb], scalar=1.0 / w,
                                in1=x[p0:p0 + 64, 8:8 + pb], op0=ALU.mult, op1=ALU.subtract), r=[src, xp], w=[yf])
                            if t0 == 0:
                                kb.op("dve", lambda e, src=src, p0=p0, o0=o0, x=x, c=c: e.tensor_tensor(
                                    out=yf[p0:p0 + 64, 0:8], in0=src[p0:p0 + 64, o0:o0 + 8], in1=ped[p0:p0 + 64, c, 0:8], op=ALU.mult),
                                    r=[src, ped], w=[yf])
                                kb.op("dve", lambda e, p0=p0, x=x: e.tensor_tensor(
                                    out=yf[p0:p0 + 64, 0:8], in0=yf[p0:p0 + 64, 0:8], in1=x[p0:p0 + 64, 8:16], op=ALU.subtract),
                                    r=[xp], w=[yf])
                            if t0 + pb == S:
                                kb.op("dve", lambda e, src=src, p0=p0, o0=o0, x=x, c=c: e.tensor_tensor(
                                    out=yf[p0:p0 + 64, pb - 8:pb], in0=src[p0:p0 + 64, o0 + pb - 8:o0 + pb], in1=ped[p0:p0 + 64, c, 8:16],
                                    op=ALU.mult), r=[src, ped], w=[yf])
                                kb.op("dve", lambda e, p0=p0, x=x: e.tensor_tensor(
                                    out=yf[p0:p0 + 64, pb - 8:pb], in0=yf[p0:p0 + 64, pb - 8:pb], in1=x[p0:p0 + 64, pb:pb + 8],
                                    op=ALU.subtract), r=[xp], w=[yf])
                        ybt = yb[c]
                        kb.op("act", lambda e, ybt=ybt: e.activation(out=ybt[:, 0:pb], in_=yf[:, 0:pb], func=AF.Copy), r=[yf], w=[ybt])
                        for half in range(2):
                            g = 2 * c + half
                            p0 = half * 64
                            for sb_ in range(pb // 512):
                                bank = 5 + (sb_ % 2)
                                kb.op("pe", lambda e, p0=p0, c=c, sb_=sb_, ybt=ybt, bank=bank: e.matmul(
                                    PS[bank][0:64, :], lhsT=wp[p0:p0 + 64, c, :], rhs=ybt[p0:p0 + 64, sb_ * 512:(sb_ + 1) * 512],
                                    start=True, stop=True), r=[wp, ybt], w=[PS[bank]])
                                o = po[sb_ % 2]
                                kb.op("act", lambda e, o=o, bank=bank, g=g: e.activation(out=o[:, :], in_=PS[bank][0:64, :], func=AF.Copy,
                                                                                         scale=psct[:, g:g + 1]),
                                      r=[PS[bank], psct], w=[o])
                                kb.dma(sc["MIXT"].h[512 + g * 64:512 + (g + 1) * 64, t0 + sb_ * 512:t0 + (sb_ + 1) * 512], o[:, :],
                                       r=[o], w=[sc["MIXT"]])
            kb.barrier()

        for j, S in enumerate(jobs):
            sc = SC[j]
            SCH = min(S, 2048)
            NSC = S // SCH
            NCH = SCH // 128
            NCT = S // 128
            with ExitStack() as es:
                std_psum(es)
                gb = sb(es, "gb", [4, 4], F32)
                ngb = sb(es, "ngb", [4, 4], F32)
                kb.dma(gb[:], gbias[l], w=[gb])
                kb.op("dve", lambda e: e.tensor_scalar(out=ngb[:, :], in0=gb[:, :], scalar1=-1.0, scalar2=None, op0=ALU.mult),
                      r=[gb], w=[ngb])
                selt = sb(es, "selt", [4, 4, 128], F32)
                kb.dma(selt[:], sel_d[:, :, :], w=[selt])
                masks = [sb(es, "mkf", [128, 128], F32), sb(es, "mkb", [128, 128], F32)]
                kb.dma(masks[0][:], maskf_d[:, :], w=[masks[0]])
                kb.dma(masks[1][:], maskb_d[:, :], w=[masks[1]])
                repn = sb(es, "repn", [128, 256], F32)
                kb.dma(repn[:], rep_l[l, :, 4 * D:4 * D + 256], w=[repn])
                ones4 = sb(es, "ones4", [4, SCH], F32)
                kb.op("pool", lambda e: e.memset(ones4[:], 1.0), w=[ones4])
                gi = sb(es, "gi", [4, SCH], F32)
                gf = sb(es, "gf", [4, SCH], F32)
                t1 = sb(es, "gt1", [4, SCH], F32)
                t2 = sb(es, "gt2", [4, SCH], F32)
                Mo = sb(es, "Mo", [4, SCH], F32)
                WIo = sb(es, "WIo", [4, SCH], F32)
                A_ = sb(es, "A_", [4, SCH], F32)
                NMt = sb(es, "NMt", [4, SCH], F32)
                WS = sb(es, "WS", [4, SCH], F32)
                EM = sb(es, "EM", [4, SCH], F32)
                STK = sb(es, "STK", [128, SCH], F32)
                carNB = sb(es, "carNB", [4, 1], F32)
                carM = sb(es, "carM", [4, 1], F32)
                kb.op("pool", lambda e: e.memset(STK[:], 0.0), w=[STK])
                qT = sb(es, "mqT", [64, 4, SCH], BF16)
                kT = sb(es, "mkT", [64, 4, SCH], BF16)
                CN = [sb(es, "CN%d" % h, [64, 65], F32) for h in range(4)]
                CNb = [sb(es, "CNb%d" % h, [64, 65], BF16) for h in range(4)]
                tq = [sb(es, "tq%d" % i, [128, 16], F32) for i in range(2)]
                vas = [sb(es, "va%d" % i, [128, 4, 65], BF16) for i in range(2)]
                vms = [sb(es, "vm%d" % i, [128, 256], BF16) for i in range(2)]
                kms = [sb(es, "km%d" % i, [128, 256], BF16) for i in range(2)]
                Dm = [sb(es, "Dm%d" % i, [128, 128], F32) for i in range(4)]
                Em = [sb(es, "Em%d" % i, [128, 128], F32) for i in range(4)]
                PTm = [sb(es, "PTm%d" % i, [128, 128], BF16) for i in range(4)]
                intra = [sb(es, "intra%d" % i, [128, 65], F32) for i in range(4)]
                HN = [sb(es, "HN%d" % i, [128, 65], F32) for i in range(4)]
                dn = [sb(es, "dn%d" % i, [128, 2], F32) for i in range(4)]
                dcs = [sb(es, "dcs%d" % i, [64, 1], F32) for i in range(4)]
                KW = [sb(es, "KW%d" % i, [128, 64], BF16) for i in range(4)]
                HT = [sb(es, "HT%d" % i, [128, 256], F32) for i in range(2)]
                hfl = [sb(es, "hfl%d" % i, [128, 256], F32) for i in range(2)]
                oml = [sb(es, "oml%d" % i, [128, 256], F32) for i in range(2)]
                gst = sb(es, "gst", [128, 4, 6], F32)
                gmv = sb(es, "gmv", [128, 4, 2], F32)
                grs = sb(es, "grs", [128, 4], F32)
                yb_ = [sb(es, "myb%d" % i, [128, 256], BF16) for i in range(2)]
                yT = [sb(es, "myT%d" % i, [128, 256], BF16) for i in range(2)]
                it = 0
                un = 0
                for d in range(2):
                    kb.flush()
                    kb.op("dve", lambda e: e.memset(carNB[:], 0.0), w=[carNB])
                    kb.op("dve", lambda e: e.memset(carM[:], 0.0), w=[carM])
                    gci = 0
                    for k in range(NSC):
                        o0 = k * SCH if d == 0 else (NSC - 1 - k) * SCH
                        kb.dma(gi[:], sc["G4"].h[2 * d, :, o0:o0 + SCH], r=[sc["G4"]], w=[gi])
                        kb.dma(gf[:], sc["G4"].h[2 * d + 1, :, o0:o0 + SCH], r=[sc["G4"]], w=[gf])
                        kb.dma(qT[:], sc["QMT"].h.rearrange("(h p) s -> p h s", p=64)[:, :, o0:o0 + SCH], r=[sc["QMT"]], w=[qT])
                        kb.dma(kT[:], sc["KMT"].h.rearrange("(h p) s -> p h s", p=64)[:, :, o0:o0 + SCH], r=[sc["KMT"]], w=[kT])
                        kb.op("dve", lambda e, d=d: e.tensor_scalar(out=gi[:, :], in0=gi[:, :], scalar1=gb[:, 2 * d:2 * d + 1], scalar2=None,
                                                                    op0=ALU.add), r=[gb], w=[gi])
                        kb.op("act", lambda e, d=d: e.activation(out=t1[:, :], in_=gf[:, :], func=AF.Exp, scale=-1.0,
                                                                 bias=ngb[:, 2 * d + 1:2 * d + 2]), r=[gf, ngb], w=[t1])
                        kb.op("act", lambda e: e.activation(out=t1[:, :], in_=t1[:, :], func=AF.Ln, scale=1.0, bias=onesf[0:4, 0:1]),
                              r=[onesf], w=[t1])
                        if d == 0:
                            isrc, fsrc = gi, t1
                        else:
                            kb.op("dve", lambda e: e.tensor_copy(out=t2[:, :], in_=gi[:, ::-1]), r=[gi], w=[t2])
                            kb.op("dve", lambda e: e.tensor_copy(out=gf[:, :], in_=t1[:, ::-1]), r=[t1], w=[gf])
                            isrc, fsrc = t2, gf
                        kb.op("dve", lambda e, fsrc=fsrc: e.tensor_tensor_scan(out=NMt[:, :], data0=ones4[:, :], data1=fsrc[:, :],
                                                                               initial=carNB[:, 0:1], op0=ALU.mult, op1=ALU.add),
                              r=[ones4, fsrc, carNB], w=[NMt])
                        kb.op("dve", lambda e: e.tensor_copy(out=carNB[:, 0:1], in_=NMt[:, SCH - 1:SCH]), r=[NMt], w=[carNB])
                        kb.op("dve", lambda e, isrc=isrc: e.tensor_tensor(out=A_[:, :], in0=isrc[:, :], in1=NMt[:, :], op=ALU.add),
                              r=[isrc, NMt], w=[A_])
                        Mf = t1 if d == 0 else gi
                        kb.op("dve", lambda e, Mf=Mf: e.tensor_tensor_scan(out=Mf[:, :], data0=ones4[:, :], data1=A_[:, :],
                                                                           initial=carM[:, 0:1], op0=ALU.mult, op1=ALU.max),
                              r=[ones4, A_, carM], w=[Mf])
                        kb.op("dve", lambda e, Mf=Mf: e.tensor_tensor(out=EM[:, :], in0=NMt[:, :], in1=Mf[:, :], op=ALU.subtract),
                              r=[NMt, Mf], w=[EM])
                        kb.op("act", lambda e: e.activation(out=EM[:, :], in_=EM[:, :], func=AF.Exp), r=[], w=[EM])
                        kb.op("dve", lambda e, Mf=Mf: e.tensor_scalar(out=NMt[:, :], in0=Mf[:, :], scalar1=-1.0, scalar2=None, op0=ALU.mult),
                              r=[Mf], w=[NMt])
                        WIf = gf if d == 0 else t1
                        for c in range(NCH):
                            c0 = c * 128
                            bprev = carM[:, 0:1] if c == 0 else Mf[:, c0 - 1:c0]
                            kb.op("act", lambda e, c0=c0, bprev=bprev, WIf=WIf: e.activation(out=WIf[:, c0:c0 + 128], in_=NMt[:, c0:c0 + 128],
                                                                                             func=AF.Exp, bias=bprev),
                                  r=[NMt, Mf, carM], w=[WIf])
                            kb.op("act", lambda e, c0=c0: e.activation(out=WS[:, c0:c0 + 128], in_=A_[:, c0:c0 + 128], func=AF.Exp,
                                                                       bias=NMt[:, c0 + 127:c0 + 128]), r=[A_, NMt], w=[WS])
                        kb.op("dve", lambda e, Mf=Mf: e.tensor_copy(out=carM[:, 0:1], in_=Mf[:, SCH - 1:SCH]), r=[Mf], w=[carM])
                        if d == 0:
                            kb.op("pool", lambda e, Mf=Mf: e.tensor_copy(out=Mo[:, :], in_=Mf[:, :]), r=[Mf], w=[Mo])
                            kb.op("pool", lambda e, WIf=WIf: e.tensor_copy(out=WIo[:, :], in_=WIf[:, :]), r=[WIf], w=[WIo])
                            srcs = (A_, WIo, EM, WS)
                        else:
                            kb.op("dve", lambda e, Mf=Mf: e.tensor_copy(out=Mo[:, :], in_=Mf[:, ::-1]), r=[Mf], w=[Mo])
                            kb.op("dve", lambda e, WIf=WIf: e.tensor_copy(out=WIo[:, :], in_=WIf[:, ::-1]), r=[WIf], w=[WIo])
                            kb.op("dve", lambda e: e.tensor_copy(out=t2[:, :], in_=A_[:, ::-1]), r=[A_], w=[t2])
                            kb.op("dve", lambda e: e.tensor_copy(out=A_[:, :], in_=EM[:, ::-1]), r=[EM], w=[A_])
                            kb.op("dve", lambda e: e.tensor_copy(out=EM[:, :], in_=WS[:, ::-1]), r=[WS], w=[EM])
                            srcs = (t2, WIo, A_, EM)
                        for kk, s_ in enumerate(srcs):
                            kb.dma(STK[32 * kk:32 * kk + 4, :], s_[:, :], r=[s_], w=[STK])
                        order = range(NCH) if d == 0 else range(NCH - 1, -1, -1)
                        for c in order:
                            c0 = c * 128
                            g0 = o0 + c0
                            ci = gci
                            gci += 1
                            sl = it % 2
                            it += 1
                            tqs, va, vm, km, ht = tq[sl], vas[sl], vms[sl], kms[sl], HT[sl]
                            kb.op("pe", lambda e, c0=c0: e.transpose(out=PS[1][:, 384:512], in_=STK[:, c0:c0 + 128], identity=identf[:]),
                                  r=[STK, identf], w=[PS[1]])
                            kb.op("act", lambda e, tqs=tqs: e.activation(
                                out=tqs.h.rearrange("p (k h) -> p k h", k=4),
                                in_=PS[1].h[:, 384:512].rearrange("p (k x) -> p k x", k=4)[:, :, 0:4], func=AF.Copy), r=[PS[1]], w=[tqs])
                            kb.dma(vm[:], sc["VM"].h[g0:g0 + 128, :], r=[sc["VM"]], w=[vm])
                            kb.dma(km[:], sc["KM"].h[g0:g0 + 128, :], r=[sc["KM"]], w=[km])
                            if d == 1:
                                kb.dma(hfl[sl][:], sc["HF"].h[g0:g0 + 128, :], r=[sc["HF"]], w=[hfl[sl]])
                                kb.dma(oml[sl][:], sc["OM"].h[g0:g0 + 128, :], r=[sc["OM"]], w=[oml[sl]])
                            kb.flush()
                            kb.op("pool", lambda e, va=va, vm=vm: e.tensor_copy(out=va[:, :, 0:64], in_=vm.h.rearrange("p (h x) -> p h x", h=4)),
                                  r=[vm], w=[va])
                            kb.op("pool", lambda e, va=va: e.memset(va[:, :, 64:65], 1.0), w=[va])
                            def RG(h):
                                if h < 3:
                                    A_b, B_b = PS[2 * h], PS[2 * h + 1]
                                    return dict(A=A_b, B=B_b, st=A_b[:, 0:128], mb=A_b[:, 128:256], ia=B_b[:, 0:65], ie=B_b[:, 128:193],
                                                dc=B_b[0:64, 256:321], dec=B_b[0:64, 330:331])
                                A_b = PS[6]
                                return dict(A=A_b, B=A_b, st=A_b[:, 0:128], mb=A_b[:, 128:256], ia=A_b[:, 256:321], ie=A_b[:, 321:386],
                                            dc=A_b[0:64, 386:451], dec=A_b[0:64, 451:452])
                            R4 = [RG(h) for h in range(4)]
                            for h in range(4):
                                g = R4[h]
                                kb.op("pe", lambda e, h=h, g=g: e.matmul(g["st"], lhsT=kT[:, h, c0:c0 + 128], rhs=qT[:, h, c0:c0 + 128],
                                                                          start=True, stop=True), r=[kT, qT], w=[g["A"]])
                                kb.op("pe", lambda e, h=h, g=g: e.matmul(g["mb"], lhsT=selt[:, h, :], rhs=Mo[:, c0:c0 + 128],
                                                                          start=True, stop=True), r=[selt, Mo], w=[g["A"]])
                            for h in range(4):
                                g = R4[h]
                                kb.op("dve", lambda e, h=h, g=g: e.scalar_tensor_tensor(
                                    out=Dm[h][:, :], in0=g["mb"], scalar=tqs[:, h:h + 1], in1=masks[d][:, :],
                                    op0=ALU.subtract, op1=ALU.max), r=[g["A"], tqs, masks[d]], w=[Dm[h]])
                            for h in range(4):
                                kb.op("act", lambda e, h=h: e.activation(out=Em[h][:, :], in_=Dm[h][:, :], func=AF.Exp, scale=-1.0),
                                      r=[Dm[h]], w=[Em[h]])
                            for h in range(4):
                                g = R4[h]
                                kb.op("dve", lambda e, h=h, g=g: e.tensor_tensor(out=PTm[h][:, :], in0=g["st"], in1=Em[h][:, :], op=ALU.mult),
                                      r=[g["A"], Em[h]], w=[PTm[h]])
                            for h in range(4):
                                g = R4[h]
                                kb.op("pe", lambda e, h=h, g=g: e.matmul(g["ia"], lhsT=PTm[h][:, :], rhs=va[:, h, :], start=True, stop=True),
                                      r=[PTm[h], va], w=[g["B"]])
                            for h in range(4):
                                g = R4[h]
                                if ci == 0:
                                    kb.op("act", lambda e, h=h, g=g: e.activation(out=HN[h][:, :], in_=g["ia"], func=AF.Copy),
                                          r=[g["B"]], w=[HN[h]])
                                else:
                                    kb.op("act", lambda e, h=h, g=g: e.activation(out=intra[h][:, :], in_=g["ia"], func=AF.Copy),
                                          r=[g["B"]], w=[intra[h]])
                                    kb.op("pe", lambda e, h=h, g=g: e.matmul(g["ie"], lhsT=qT[:, h, c0:c0 + 128], rhs=CNb[h][:, :],
                                                                              start=True, stop=True), r=[qT, CNb[h]], w=[g["B"]])
                            if ci > 0:
                                for h in range(4):
                                    g = R4[h]
                                    kb.op("dve", lambda e, h=h, g=g: e.scalar_tensor_tensor(
                                        out=HN[h][:, :], in0=g["ie"], scalar=tqs[:, 4 + h:5 + h], in1=intra[h][:, :],
                                        op0=ALU.mult, op1=ALU.add), r=[g["B"], tqs, intra[h]], w=[HN[h]])
                            for h in range(4):
                                kb.op("dve", lambda e, h=h: e.scalar_tensor_tensor(
                                    out=dn[h][:, 0:1], in0=HN[h][:, 64:65], scalar=-1.0, in1=HN[h][:, 64:65],
                                    op0=ALU.mult, op1=ALU.max), r=[HN[h]], w=[dn[h]])
                            for h in range(4):
                                kb.op("dve", lambda e, h=h: e.tensor_scalar(
                                    out=dn[h][:, 0:1], in0=dn[h][:, 0:1], scalar1=tqs[:, 8 + h:9 + h], scalar2=None,
                                    op0=ALU.max), r=[tqs], w=[dn[h]])
                            for h in range(4):
                                kb.op("dve", lambda e, h=h: e.reciprocal(out=dn[h][:, 1:2], in_=dn[h][:, 0:1]), r=[], w=[dn[h]])
                            for h in range(4):
                                kb.op("dve", lambda e, h=h: e.tensor_scalar(
                                    out=ht[:, h * 64:(h + 1) * 64], in0=HN[h][:, 0:64], scalar1=dn[h][:, 1:2], scalar2=None, op0=ALU.mult),
                                    r=[HN[h], dn[h]], w=[ht])
                            if ci < NCT - 1:
                                col = c0 + 127 if d == 0 else c0
                                for h in range(4):
                                    g = R4[h]
                                    kb.op("pool", lambda e, h=h: e.tensor_scalar(
                                        out=KW[h][:, :], in0=km[:, h * 64:(h + 1) * 64], scalar1=tqs[:, 12 + h:13 + h], scalar2=None,
                                        op0=ALU.mult), r=[km, tqs], w=[KW[h]])
                                    kb.op("pe", lambda e, h=h, g=g: e.matmul(g["dc"], lhsT=KW[h][:, :], rhs=va[:, h, :], start=True, stop=True),
                                          r=[KW[h], va], w=[g["B"]])
                                    if ci > 0:
                                        kb.op("pe", lambda e, h=h, g=g: e.matmul(g["dec"], lhsT=selt[:, h, 0:64], rhs=WIo[:, col:col + 1],
                                                                                  start=True, stop=True), r=[selt, WIo], w=[g["B"]])
                                for h in range(4):
                                    g = R4[h]
                                    if ci == 0:
                                        kb.op("act", lambda e, h=h, g=g: e.activation(out=CN[h][:, :], in_=g["dc"], func=AF.Copy),
                                              r=[g["B"]], w=[CN[h]])
                                    else:
                                        kb.op("dve", lambda e, h=h, g=g: e.tensor_copy(out=dcs[h][:, 0:1], in_=g["dec"]),
                                              r=[g["B"]], w=[dcs[h]])
                                for h in range(4):
                                    g = R4[h]
                                    if ci > 0:
                                        kb.op("dve", lambda e, h=h, g=g: e.scalar_tensor_tensor(
                                            out=CN[h][:, :], in0=CN[h][:, :], scalar=dcs[h][:, 0:1], in1=g["dc"],
                                            op0=ALU.mult, op1=ALU.add), r=[g["B"], dcs[h]], w=[CN[h]])
                                for h in range(4):
                                    kb.op("act", lambda e, h=h: e.activation(out=CNb[h][:, :], in_=CN[h][:, :], func=AF.Copy),
                                          r=[CN[h]], w=[CNb[h]])
                            if d == 0:
                                kb.dma_later(sc["HF"].h[g0:g0 + 128, :], ht[:, :], r=[ht], w=[sc["HF"]])
                            else:
                                hf, om, ybt, yTt = hfl[sl], oml[sl], yb_[sl], yT[sl]
                                kb.op("dve", lambda e, hf=hf, ht=ht: e.tensor_tensor(out=hf[:, :], in0=hf[:, :], in1=ht[:, :], op=ALU.add),
                                      r=[ht], w=[hf])
                                for h in range(4):
                                    kb.op("dve", lambda e, h=h, hf=hf: e.bn_stats(out=gst[:, h, :], in_=hf[:, h * 64:(h + 1) * 64]),
                                          r=[hf], w=[gst])
                                    kb.op("dve", lambda e, h=h: e.bn_aggr(out=gmv[:, h, :], in_=gst[:, h, :]), r=[gst], w=[gmv])
                                rsqrt_eps(grs, grs[:, :], gmv[:, :, 1], [gmv])
                                for h in range(4):
                                    kb.op("dve", lambda e, h=h, hf=hf: e.tensor_scalar(
                                        out=hf[:, h * 64:(h + 1) * 64], in0=hf[:, h * 64:(h + 1) * 64], scalar1=gmv[:, h, 0:1],
                                        scalar2=grs[:, h:h + 1], op0=ALU.subtract, op1=ALU.mult), r=[gmv, grs], w=[hf])
                                kb.op("act", lambda e, om=om: e.activation(out=om[:, :], in_=om[:, :], func=AF.Sigmoid), r=[], w=[om])
                                kb.op("pool", lambda e, hf=hf: e.tensor_tensor(out=hf[:, :], in0=hf[:, :], in1=repn[:, :], op=ALU.mult),
                                      r=[repn], w=[hf])
                                kb.op("dve", lambda e, hf=hf, om=om, ybt=ybt: e.tensor_tensor(out=ybt[:, :], in0=hf[:, :], in1=om[:, :],
                                                                                             op=ALU.mult), r=[hf, om], w=[ybt])
                                for cc in range(2):
                                    kb.op("pe", lambda e, cc=cc, ybt=ybt: e.transpose(out=PSB[:, cc * 128:(cc + 1) * 128],
                                                                                      in_=ybt[:, cc * 128:(cc + 1) * 128], identity=identb[:]),
                                          r=[ybt, identb], w=[PSB])
                                kb.op("act", lambda e, yTt=yTt: e.activation(out=yTt[:, :], in_=PSB[:, 0:256], func=AF.Copy), r=[PSB], w=[yTt])
                                kb.dma_later(sc["MIXT"].h[768:1024, :].rearrange("(c p) s -> p c s", p=128)[:, :, g0:g0 + 128],
                                             yTt.h.rearrange("p (c t) -> p c t", c=2), r=[yTt], w=[sc["MIXT"]])
                kb.barrier()

        if dbg and l == 0:
            dbg_out = T(nc.dram_tensor("dbg_mixt", [D, jobs[0]], BF16, kind="ExternalOutput").ap())
            with ExitStack() as es:
                dt_ = sb(es, "dbgt", [128, 8, jobs[0]], BF16)
                kb.dma(dt_[:], SC[0]["MIXT"].h.rearrange("(c p) s -> p c s", p=128), r=[SC[0]["MIXT"]], w=[dt_])
                kb.dma(dbg_out.h.rearrange("(c p) s -> p c s", p=128), dt_[:], r=[dt_], w=[dbg_out])
                kb.barrier()
        alpha = float(8.0 ** 0.25)
        with ExitStack() as es:
            std_psum(es)
            stg = [sb(es, "stg%d" % i, [128, 2048], F32) for i in range(2)]
            wo = sb(es, "wo", [128, 8, D], BF16)
            load_w_bf16(es, wo, lambda c, n0, n1: wo[:, c, n0:n1], w_out[l], 8, D, stg)
            rl = sb(es, "rl", [128, 2 * D], F32)
            kb.dma(rl[:], rep_l[l, :, 0:2 * D], w=[rl])
            tmp = {"st6": sb(es, "st6", [128, 2, 6], F32), "mv": sb(es, "mv", [128, 2], F32), "rs": sb(es, "rs", [128, 1], F32)}
            mts = [sb(es, "mt%d" % i, [128, 8, 512], BF16) for i in range(2)]
            xres = [sb(es, "xres%d" % i, [128, D], F32) for i in range(2)]
            z = [sb(es, "z%d" % i, [128, D], F32) for i in range(2)]
            x1f = [sb(es, "x1f%d" % i, [128, D], F32) for i in range(2)]
            ob = [sb(es, "ob%d" % i, [128, D], BF16) for i in range(2)]
            xts = [sb(es, "xts%d" % i, [128, D], BF16) for i in range(2)]
            bi = 0
            for j, S in enumerate(jobs):
                sc = SC[j]
                XNi, XNo = sc["XN"][0], sc["XN"][1]
                for blk in range(S // 512):
                    t0 = blk * 512
                    mt = mts[bi % 2]
                    bi += 1
                    kb.dma(mt[:], sc["MIXT"].h.rearrange("(c p) s -> p c s", p=128)[:, :, t0:t0 + 512], r=[sc["MIXT"]], w=[mt])
                    def mm_a(tt):
                        for nb in range(2):
                            for c in range(8):
                                kb.op("pe", lambda e, c=c, nb=nb, tt=tt: e.matmul(PS[nb + 2 * (tt % 2)][:, :], lhsT=mt[:, c, tt * 128:(tt + 1) * 128],
                                                                                   rhs=wo[:, c, nb * 512:(nb + 1) * 512],
                                                                                   start=(c == 0), stop=(c == 7)), r=[mt, wo], w=[PS[nb + 2 * (tt % 2)]])
                    mm_a(0)
                    for tt in range(4):
                        tk0 = t0 + tt * 128
                        xr, zt, obt, x1, xt_ = xres[tt % 2], z[tt % 2], ob[tt % 2], x1f[tt % 2], xts[tt % 2]
                        kb.dma(xr[:], XNi.h[tk0:tk0 + 128, :], r=[XNi], w=[xr])
                        kb.flush()
                        if tt + 1 < 4:
                            mm_a(tt + 1)
                        for nb in range(2):
                            kb.op("dve", lambda e, nb=nb, xr=xr, zt=zt, tt=tt: e.scalar_tensor_tensor(
                                out=zt[:, nb * 512:(nb + 1) * 512], in0=xr[:, nb * 512:(nb + 1) * 512], scalar=alpha,
                                in1=PS[nb + 2 * (tt % 2)][:, :], op0=ALU.mult, op1=ALU.add), r=[xr, PS[nb + 2 * (tt % 2)]], w=[zt])
                        layer_norm_tile(zt, rl, rl, 0, D, x1, obt, tmp)
                        kb.dma_later(XNo.h[tk0:tk0 + 128, :], x1[:], r=[x1], w=[XNo])
                        transpose_to_xt(obt, xt_, sc["XT"], tk0)
            kb.barrier()
        with ExitStack() as es:
            std_psum(es)
            stg = [sb(es, "stg%d" % i, [128, 2048], F32) for i in range(2)]
            wg = sb(es, "wg", [128, 8, DFF], BF16)
            wu = sb(es, "wu", [128, 8, DFF], BF16)
            load_w_bf16(es, wg, lambda c, n0, n1: wg[:, c, n0:n1], w_gate[l], 8, DFF, stg)
            load_w_bf16(es, wu, lambda c, n0, n1: wu[:, c, n0:n1], w_up[l], 8, DFF, stg)
            x1Ts = [sb(es, "x1T%d" % i, [128, 8, 512], BF16) for i in range(2)]
            sg = [sb(es, "sg%d" % i, [128, 512], F32) for i in range(2)]
            hd = [sb(es, "hd%d" % i, [128, 512], BF16) for i in range(2)]
            bi = 0
            for j, S in enumerate(jobs):
                sc = SC[j]
                for blk in range(S // 512):
                    t0 = blk * 512
                    x1T = x1Ts[bi % 2]
                    bi += 1
                    kb.dma(x1T[:], sc["XT"].h.rearrange("(c p) s -> p c s", p=128)[:, :, t0:t0 + 512], r=[sc["XT"]], w=[x1T])
                    for f in range(NFC):
                        bg, bu = PS[2 * (f % 2)], PS[1 + 2 * (f % 2)]
                        for c in range(8):
                            kb.op("pe", lambda e, c=c, f=f, bg=bg: e.matmul(bg[:, :], lhsT=wg[:, c, f * 128:(f + 1) * 128], rhs=x1T[:, c, :],
                                                                            start=(c == 0), stop=(c == 7)), r=[wg, x1T], w=[bg])
                        for c in range(8):
                            kb.op("pe", lambda e, c=c, f=f, bu=bu: e.matmul(bu[:, :], lhsT=wu[:, c, f * 128:(f + 1) * 128], rhs=x1T[:, c, :],
                                                                            start=(c == 0), stop=(c == 7)), r=[wu, x1T], w=[bu])
                        sgt, hdt = sg[f % 2], hd[f % 2]
                        kb.op("act", lambda e, sgt=sgt, bg=bg: e.activation(out=sgt[:, :], in_=bg[:, :], func=AF.Silu), r=[bg], w=[sgt])
                        kb.op("dve", lambda e, hdt=hdt, sgt=sgt, bu=bu: e.tensor_tensor(out=hdt[:, :], in0=bu[:, :], in1=sgt[:, :], op=ALU.mult),
                              r=[bu, sgt], w=[hdt])
                        kb.dma(sc["HIDT"].h[f * 128:(f + 1) * 128, t0:t0 + 512], hdt[:, :], r=[hdt], w=[sc["HIDT"]])
            kb.barrier()
        with ExitStack() as es:
            std_psum(es)
            stg = [sb(es, "stg%d" % i, [128, 2048], F32) for i in range(2)]
            wd = sb(es, "wd", [128, NFC, D], BF16)
            load_w_bf16(es, wd, lambda c, n0, n1: wd[:, c, n0:n1], w_down[l], NFC, D, stg)
            rl = sb(es, "rl", [128, 2 * D], F32)
            kb.dma(rl[:], rep_l[l, :, 2 * D:4 * D], w=[rl])
            tmp = {"st6": sb(es, "st6", [128, 2, 6], F32), "mv": sb(es, "mv", [128, 2], F32), "rs": sb(es, "rs", [128, 1], F32)}
            HIDs = [sb(es, "HID%d" % i, [128, NFC, 512], BF16) for i in range(2)]
            x1f = [sb(es, "x1f%d" % i, [128, D], F32) for i in range(2)]
            z = [sb(es, "z%d" % i, [128, D], F32) for i in range(2)]
            of2 = [sb(es, "of2%d" % i, [128, D], F32) for i in range(2)]
            ob = [sb(es, "ob%d" % i, [128, D], BF16) for i in range(2)]
            xts = [sb(es, "xts%d" % i, [128, D], BF16) for i in range(2)]
            bi = 0
            for j, S in enumerate(jobs):
                sc = SC[j]
                X1, XNo = sc["XN"][1], sc["XN"][0]
                for blk in range(S // 512):
                    t0 = blk * 512
                    HID = HIDs[bi % 2]
                    bi += 1
                    kb.dma(HID[:], sc["HIDT"].h.rearrange("(f p) s -> p f s", p=128)[:, :, t0:t0 + 512], r=[sc["HIDT"]], w=[HID])
                    def mm_c(tt):
                        for nb in range(2):
                            for f in range(NFC):
                                kb.op("pe", lambda e, f=f, nb=nb, tt=tt: e.matmul(PS[nb + 2 * (tt % 2)][:, :], lhsT=HID[:, f, tt * 128:(tt + 1) * 128],
                                                                                   rhs=wd[:, f, nb * 512:(nb + 1) * 512],
                                                                                   start=(f == 0), stop=(f == NFC - 1)), r=[HID, wd], w=[PS[nb + 2 * (tt % 2)]])
                    mm_c(0)
                    for tt in range(4):
                        tk0 = t0 + tt * 128
                        zt, obt, x1, o2, xt_ = z[tt % 2], ob[tt % 2], x1f[tt % 2], of2[tt % 2], xts[tt % 2]
                        kb.dma(x1[:], X1.h[tk0:tk0 + 128, :], r=[X1], w=[x1])
                        kb.flush()
                        if tt + 1 < 4:
                            mm_c(tt + 1)
                        for nb in range(2):
                            kb.op("dve", lambda e, nb=nb, x1=x1, zt=zt, tt=tt: e.scalar_tensor_tensor(
                                out=zt[:, nb * 512:(nb + 1) * 512], in0=x1[:, nb * 512:(nb + 1) * 512], scalar=alpha,
                                in1=PS[nb + 2 * (tt % 2)][:, :], op0=ALU.mult, op1=ALU.add), r=[x1, PS[nb + 2 * (tt % 2)]], w=[zt])
                        layer_norm_tile(zt, rl, rl, 0, D, o2, obt, tmp)
                        if last:
                            kb.dma_later(yout[j].h[tk0:tk0 + 128, :], o2[:], r=[o2], w=[yout[j]])
                        else:
                            kb.dma_later(XNo.h[tk0:tk0 + 128, :], o2[:], r=[o2], w=[XNo])
                            transpose_to_xt(obt, xt_, sc["XT"], tk0)
            kb.barrier()
    kb.barrier()
    es_glob.close()
    return nc


def _tables(jobs, smax):
    half = 16
    inv = 1.0 / (10000.0 ** (np.arange(0, 32, 2, dtype=np.float32) / 32.0))
    ang = np.arange(smax, dtype=np.float32)[:, None] * inv[None, :].astype(np.float32)
    cos = np.cos(ang).astype(np.float32).T
    sin = np.sin(ang).astype(np.float32).T
    cos2 = np.concatenate([cos, cos], 0)
    sin2 = np.concatenate([-sin, sin], 0)
    pedge = np.zeros((len(jobs), 128, 2, 16), np.float32)
    for j, S in enumerate(jobs):
        for g, w in enumerate((2, 4, 8, 16)):
            c, p0 = g // 2, (g % 2) * 64
            for i in range(8):
                t = i
                cnt = min(t + w // 2, S) - max(t - w // 2, 0)
                pedge[j, p0:p0 + 64, c, i] = 1.0 / cnt
                t = S - 8 + i
                cnt = min(t + w // 2, S) - max(t - w // 2, 0)
                pedge[j, p0:p0 + 64, c, 8 + i] = 1.0 / cnt
    s_ = np.arange(128)[:, None]
    j_ = np.arange(128)[None, :]
    maskf = np.where(s_ <= j_, 0.0, BIG).astype(np.float32)
    maskb = np.where(s_ >= j_, 0.0, BIG).astype(np.float32)
    sel = np.zeros((4, 4, 128), np.float32)
    for h in range(4):
        sel[h, h, :] = 1.0
    return dict(cos2=np.ascontiguousarray(cos2), sin2=np.ascontiguousarray(sin2), pedge=pedge, maskf=maskf, maskb=maskb,
                sel=sel, ident=np.eye(128, dtype=np.float32))


def _common_inputs(inp, L, jobs, smax):
    f = lambda a: np.ascontiguousarray(np.asarray(a, dtype=np.float32))
    rep = lambda v: np.broadcast_to(f(v)[None, :], (128, f(v).shape[0]))
    m = {}
    for k in ("w_in", "w_uq", "w_ukv", "w_pool", "w_out", "w_gate", "w_up", "w_down"):
        m[k] = f(inp[k])[:L]
    m["rep_in"] = np.ascontiguousarray(np.concatenate([rep(inp["ln_in_g"]), rep(inp["ln_in_b"])], 1))
    m["rep_l"] = np.ascontiguousarray(np.stack([
        np.concatenate([rep(inp["ln1_g"][l]), rep(inp["ln1_b"][l]), rep(inp["ln2_g"][l]), rep(inp["ln2_b"][l]),
                        rep(inp["mlstm_norm_g"][l])], 1) for l in range(L)]))
    m["qg"] = np.ascontiguousarray(f(inp["q_norm_g"])[:L].reshape(L, 2, 128).transpose(0, 2, 1))
    m["kvg"] = np.ascontiguousarray(f(inp["kv_norm_g"])[:L].reshape(L, 128, 1))
    m["psc"] = np.ascontiguousarray(f(inp["pool_scale"])[:L].reshape(L, 4, 64).transpose(0, 2, 1))
    m["gbias"] = np.ascontiguousarray(f(inp["mlstm_gate_bias"])[:L].reshape(L, 4, 4).transpose(0, 2, 1))
    m.update(_tables(jobs, smax))
    return m


_CACHE = {}


def run(inp, L, job_inputs_per_core, jobs, dbg=False):
    smax = max(jobs)
    key = (L, tuple(jobs))
    if key not in _CACHE:
        _CACHE[key] = build_program(L, jobs, smax, dbg)
    nc = _CACHE[key]
    common = _common_inputs(inp, L, jobs, smax)
    in_maps = []
    for xs in job_inputs_per_core:
        m = dict(common)
        for j, x in enumerate(xs):
            m["xin%d" % j] = np.ascontiguousarray(x, dtype=np.float32)
        in_maps.append(m)
    res = run_bass_kernel_spmd(nc, in_maps, core_ids=list(range(len(in_maps))))
    if dbg:
        return [[r["yout%d" % j] for j in range(len(jobs))] + [r["dbg_mixt"]] for r in res.results]
    return [[r["yout%d" % j] for j in range(len(jobs))] for r in res.results]


def kernel(**inputs):
    xp = np.asarray(inputs["x_prompt"], dtype=np.float32)
    xs = np.asarray(inputs["x_sample"], dtype=np.float32)
    L = int(np.asarray(inputs["w_in"]).shape[0])
    jobs = [xp.shape[1], xs.shape[1], xs.shape[1]]
    per_core = [[xp[c % 2], xs[2 * c], xs[2 * c + 1]] for c in range(8)]
    outs = run(inputs, L, per_core, jobs)
    y_prompt = np.stack([outs[0][0], outs[1][0]], 0).astype(np.float32)
    y_sample = np.stack([outs[c][1 + i] for c in range(8) for i in range(2)], 0).astype(np.float32)
    return (y_prompt, y_sample)
```

```python
import numpy as np
from contextlib import ExitStack
import concourse.bass as bass
import concourse.mybir as mybir
from concourse.bass_utils import run_bass_kernel_spmd

F32 = mybir.dt.float32
BF16 = mybir.dt.bfloat16
AF = mybir.ActivationFunctionType
ALU = mybir.AluOpType

D = 1024
NQ, NKV, NR, NP_, NM, NG = 256, 128, 32, 256, 256, 16
IN_W = 1712
DFF = 2816
NFC = DFF // 128
EPS = 1e-5
BIG = 1e30
SAME_ENG_SYNC = True


class Buf:
    __slots__ = ("w", "r")

    def __init__(self):
        self.w = None
        self.r = {}


class T:
    def __init__(self, h):
        self.h = h
        self.b = Buf()

    def __getitem__(self, idx):
        return self.h[idx]


class KB:
    ND = 8

    def __init__(self, nc):
        self.nc = nc
        self.E = {"pe": nc.tensor, "act": nc.scalar, "dve": nc.vector, "pool": nc.gpsimd, "sp": nc.sync}
        self.sem = {}
        self.val = {}
        for e in self.E:
            self.sem[e] = nc.alloc_semaphore(name="c_" + e)
            self.val[e] = 0
        for q in ("sp", "pool"):
            for i in range(self.ND):
                self.sem[(q, i)] = nc.alloc_semaphore(name="d_%s%d" % (q, i))
                self.val[(q, i)] = 0
        self.dslot = {"sp": 0, "pool": 0}
        self.waited = {e: {} for e in self.E}
        self.pending = []

    def _wait(self, e, sk, v):
        if v <= 0:
            return
        if self.waited[e].get(sk, 0) < v:
            self.E[e].wait_ge(self.sem[sk], v)
            self.waited[e][sk] = v

    def _deps(self, e, r, w):
        deps = {}
        for t in list(r) + list(w):
            ev = t.b.w
            if ev is not None:
                deps[ev[0]] = max(deps.get(ev[0], 0), ev[1])
        for t in w:
            for sk, v in t.b.r.items():
                deps[sk] = max(deps.get(sk, 0), v)
        for sk, v in deps.items():
            if sk == e and (e == "pe" or not SAME_ENG_SYNC):
                continue
            self._wait(e, sk, v)

    def _mark(self, ev, r, w):
        for t in r:
            t.b.r[ev[0]] = max(t.b.r.get(ev[0], 0), ev[1])
        for t in w:
            t.b.w = ev
            t.b.r = {}

    def op(self, e, fn, r=(), w=()):
        self._deps(e, r, w)
        ins = fn(self.E[e])
        self.val[e] += 1
        ins.then_inc(self.sem[e], 1)
        self._mark((e, self.val[e]), r, w)

    def dma(self, out, in_, r=(), w=(), q="sp"):
        self._deps(q, r, w)
        slot = self.dslot[q]
        self.dslot[q] = (slot + 1) % self.ND
        sk = (q, slot)
        self._wait(q, sk, self.val[sk])
        ins = self.E[q].dma_start(out=out, in_=in_)
        self.val[sk] += 16
        ins.then_inc(self.sem[sk], 16)
        self._mark((sk, self.val[sk]), r, w)

    def dma_later(self, out, in_, r=(), w=()):
        self.pending.append((out, in_, r, w))

    def flush(self):
        p, self.pending = self.pending, []
        for (out, in_, r, w) in p:
            self.dma(out, in_, r=r, w=w)

    def barrier(self):
        self.flush()
        for e in self.E:
            for sk in self.sem:
                if sk != e:
                    self._wait(e, sk, self.val[sk])


def build_program(L, jobs, smax, dbg=False):
    nc = bass.Bass("TRN2", target_bir_lowering=False)
    kb = KB(nc)
    NJ = len(jobs)

    def din(name, shape, dt=F32):
        return nc.dram_tensor(name, list(shape), dt, kind="ExternalInput").ap()

    def dscr(name, shape, dt):
        return T(nc.dram_tensor(name, list(shape), dt).ap())

    xin = [din("xin%d" % j, [S, D]) for j, S in enumerate(jobs)]
    yout = [T(nc.dram_tensor("yout%d" % j, [S, D], F32, kind="ExternalOutput").ap()) for j, S in enumerate(jobs)]
    w_in = din("w_in", [L, D, IN_W])
    w_uq = din("w_uq", [L, NQ, 768])
    w_ukv = din("w_ukv", [L, NKV, 1024])
    w_pool = din("w_pool", [L, 4, 64, 64])
    w_out = din("w_out", [L, D, D])
    w_gate = din("w_gate", [L, D, DFF])
    w_up = din("w_up", [L, D, DFF])
    w_down = din("w_down", [L, DFF, D])
    rep_in = din("rep_in", [128, 2 * D])
    rep_l = din("rep_l", [L, 128, 4 * D + 256])
    qg = din("qg", [L, 128, 2])
    kvg = din("kvg", [L, 128, 1])
    psc = din("psc", [L, 64, 4])
    gbias = din("gbias", [L, 4, 4])
    cos2 = din("cos2", [32, smax])
    sin2 = din("sin2", [32, smax])
    pedge = din("pedge", [NJ, 128, 2, 16])
    maskf_d = din("maskf", [128, 128])
    maskb_d = din("maskb", [128, 128])
    sel_d = din("sel", [4, 4, 128])
    ident_d = din("ident", [128, 128])

    SC = []
    for j, S in enumerate(jobs):
        s = {}
        s["XN"] = [dscr("XN%d_%d" % (i, j), [S, D], F32) for i in range(2)]
        s["XT"] = dscr("XT%d" % j, [D, S], BF16)
        s["CQT"] = dscr("CQT%d" % j, [NQ, S], BF16)
        s["CKVT"] = dscr("CKVT%d" % j, [NKV, S], BF16)
        s["KRT"] = dscr("KRT%d" % j, [32, S], BF16)
        s["POOLT"] = dscr("POOLT%d" % j, [256, S], F32)
        s["QMT"] = dscr("QMT%d" % j, [256, S], BF16)
        s["KMT"] = dscr("KMT%d" % j, [256, S], BF16)
        s["G4"] = dscr("G4%d" % j, [4, 4, S], F32)
        s["VM"] = dscr("VM%d" % j, [S, 256], BF16)
        s["KM"] = dscr("KM%d" % j, [S, 256], BF16)
        s["OM"] = dscr("OM%d" % j, [S, 256], F32)
        s["MIXT"] = dscr("MIXT%d" % j, [D, S], BF16)
        s["HF"] = dscr("HF%d" % j, [S, 256], F32)
        s["HIDT"] = dscr("HIDT%d" % j, [DFF, S], BF16)
        SC.append(s)

    es_glob = ExitStack()
    uniq = [0]

    def sb(es, name, shape, dt):
        uniq[0] += 1
        return T(es.enter_context(nc.sbuf_tensor("%s_%d" % (name, uniq[0]), list(shape), dt)))

    PS = [None] * 7
    PSBh = [None]

    class _PSB:
        @property
        def h(self):
            return PSBh[0].h

        @property
        def b(self):
            return PSBh[0].b

        def __getitem__(self, idx):
            return PSBh[0].h[idx]

    PSB = _PSB()

    def std_psum(es):
        for i in range(7):
            uniq[0] += 1
            PS[i] = T(es.enter_context(nc.psum_tensor("ps%d_%d" % (i, uniq[0]), [128, 512], F32)))
        uniq[0] += 1
        PSBh[0] = T(es.enter_context(nc.psum_tensor("psb_%d" % uniq[0], [128, 1024], BF16)))

    identf = sb(es_glob, "identf", [128, 128], F32)
    identb = sb(es_glob, "identb", [128, 128], BF16)
    onesf = sb(es_glob, "onesf", [128, 128], F32)
    onesb = sb(es_glob, "onesb", [128, 128], BF16)
    kb.dma(identf[:], ident_d[:, :], w=[identf])
    kb.op("dve", lambda e: e.tensor_copy(out=identb[:], in_=identf[:]), r=[identf], w=[identb])
    kb.op("dve", lambda e: e.memset(onesf[:], 1.0), w=[onesf])
    kb.op("dve", lambda e: e.memset(onesb[:], 1.0), w=[onesb])

    stg_ctr = [0]

    def load_w_bf16(es_stage, dst, dst_ap_fn, src_ap, nk, ncols, stg):
        CW = 2048
        for c in range(nk):
            for n0 in range(0, ncols, CW):
                n1 = min(ncols, n0 + CW)
                st = stg[stg_ctr[0] % len(stg)]
                stg_ctr[0] += 1
                kb.dma(st[:, 0:n1 - n0], src_ap[c * 128:(c + 1) * 128, n0:n1], w=[st])
                eng = "pool" if (stg_ctr[0] % 2) else "dve"
                kb.op(eng, lambda e, st=st, c=c, n0=n0, n1=n1: e.tensor_copy(out=dst_ap_fn(c, n0, n1), in_=st[:, 0:n1 - n0]),
                      r=[st], w=[dst])

    def rsqrt_eps(t, out_ap, in_ap, rd, scale=1.0):
        kb.op("dve", lambda e: e.tensor_scalar(out=out_ap, in0=in_ap, scalar1=scale, scalar2=EPS, op0=ALU.mult, op1=ALU.add),
              r=rd, w=[t])
        kb.op("act", lambda e: e.activation(out=out_ap, in_=out_ap, func=AF.Sqrt), r=[], w=[t])
        kb.op("dve", lambda e: e.reciprocal(out=out_ap, in_=out_ap), r=[], w=[t])

    def layer_norm_tile(z, gt, bt, g_off, b_off, outf, outb, tmp):
        st6, mv, rs = tmp["st6"], tmp["mv"], tmp["rs"]
        for hh in range(2):
            kb.op("dve", lambda e, hh=hh: e.bn_stats(out=st6[:, hh, :], in_=z[:, hh * 512:(hh + 1) * 512]), r=[z], w=[st6])
        kb.op("dve", lambda e: e.bn_aggr(out=mv[:, :], in_=st6[:, :, :]), r=[st6], w=[mv])
        rsqrt_eps(rs, rs[:, :], mv[:, 1:2], [mv])
        kb.op("dve", lambda e: e.tensor_scalar(out=outf[:, :], in0=z[:, :], scalar1=mv[:, 0:1], scalar2=rs[:, 0:1],
                                               op0=ALU.subtract, op1=ALU.mult), r=[z, mv, rs], w=[outf])
        kb.op("pool", lambda e: e.tensor_tensor(out=outf[:, :], in0=outf[:, :], in1=gt[:, g_off:g_off + D], op=ALU.mult),
              r=[gt], w=[outf])
        kb.op("dve", lambda e: e.tensor_tensor(out=outf[:, :], in0=outf[:, :], in1=bt[:, b_off:b_off + D], op=ALU.add),
              r=[bt], w=[outf])
        kb.op("act", lambda e: e.activation(out=outb[:, :], in_=outf[:, :], func=AF.Copy), r=[outf], w=[outb])

    def transpose_to_xt(outb, xts, XT_T, tok0):
        for c in range(8):
            kb.op("pe", lambda e, c=c: e.transpose(out=PSB[:, c * 128:(c + 1) * 128], in_=outb[:, c * 128:(c + 1) * 128],
                                                   identity=identb[:]), r=[outb, identb], w=[PSB])
        kb.op("act", lambda e: e.activation(out=xts[:, :], in_=PSB[:, :], func=AF.Copy), r=[PSB], w=[xts])
        kb.dma_later(XT_T.h.rearrange("(c p) s -> p c s", p=128)[:, :, tok0:tok0 + 128],
                     xts.h.rearrange("p (c t) -> p c t", c=8), r=[xts], w=[XT_T])

    with ExitStack() as es:
        std_psum(es)
        rin = sb(es, "rin", [128, 2 * D], F32)
        kb.dma(rin[:], rep_in[:, :], w=[rin])
        tmp = {"st6": sb(es, "st6", [128, 2, 6], F32), "mv": sb(es, "mv", [128, 2], F32), "rs": sb(es, "rs", [128, 1], F32)}
        zs = [sb(es, "pz%d" % i, [128, D], F32) for i in range(2)]
        ofs = [sb(es, "pof%d" % i, [128, D], F32) for i in range(2)]
        obs = [sb(es, "pob%d" % i, [128, D], BF16) for i in range(2)]
        xtss = [sb(es, "pxt%d" % i, [128, D], BF16) for i in range(2)]
        it = 0
        for j, S in enumerate(jobs):
            for t in range(S // 128):
                z, of, ob, xts = zs[it % 2], ofs[it % 2], obs[it % 2], xtss[it % 2]
                it += 1
                kb.dma(z[:], xin[j][t * 128:(t + 1) * 128, :], w=[z])
                kb.flush()
                layer_norm_tile(z, rin, rin, 0, D, of, ob, tmp)
                kb.dma_later(SC[j]["XN"][0].h[t * 128:(t + 1) * 128, :], of[:], r=[of], w=[SC[j]["XN"][0]])
                transpose_to_xt(ob, xts, SC[j]["XT"], t * 128)
        kb.barrier()

    for l in range(L):
        last = l == L - 1
        with ExitStack() as es:
            std_psum(es)
            stg = [sb(es, "stg%d" % i, [128, 2048], F32) for i in range(2)]
            win = sb(es, "win", [128, 8, IN_W], BF16)
            load_w_bf16(es, win, lambda c, n0, n1: win[:, c, n0:n1], w_in[l], 8, IN_W, stg)
            wkr_sw = sb(es, "wkrsw", [128, 8, 96], BF16)
            kb.op("dve", lambda e: e.tensor_copy(out=wkr_sw[:, :, 0:64], in_=win[:, :, 320:384]), r=[win], w=[wkr_sw])
            kb.op("dve", lambda e: e.tensor_copy(out=wkr_sw[:, :, 64:80], in_=win[:, :, 400:416]), r=[win], w=[wkr_sw])
            kb.op("dve", lambda e: e.tensor_copy(out=wkr_sw[:, :, 80:96], in_=win[:, :, 384:400]), r=[win], w=[wkr_sw])
            qgt = sb(es, "qgt", [128, 2], F32)
            kvgt = sb(es, "kvgt", [128, 1], F32)
            kb.dma(qgt[:], qg[l], w=[qgt])
            kb.dma(kvgt[:], kvg[l], w=[kvgt])
            xTs = [sb(es, "xT%d" % i, [128, 8, 512], BF16) for i in range(2)]
            cst = [sb(es, "cs%d" % i, [96, 512], F32) for i in range(2)]
            snt = [sb(es, "sn%d" % i, [96, 512], F32) for i in range(2)]
            sq = sb(es, "sq", [128, 512], BF16)
            rstd = sb(es, "rstd", [128, 512], F32)
            ev = [sb(es, "ev%d" % i, [128, 512], BF16) for i in range(2)]
            evf = [sb(es, "evf%d" % i, [128, 512], F32) for i in range(2)]
            kr1 = sb(es, "kr1", [96, 512], F32)
            kr2 = sb(es, "kr2", [96, 512], F32)
            krb = sb(es, "krb", [96, 512], BF16)
            tmb = [sb(es, "tmb%d" % i, [128, 512], BF16) for i in range(2)]
            tmf = [sb(es, "tmf%d" % i, [128, 256], F32) for i in range(2)]
            evc = 0
            for j, S in enumerate(jobs):
                sc = SC[j]
                for blk in range(S // 512):
                    t0 = blk * 512
                    xT = xTs[blk % 2]
                    kb.dma(xT[:], sc["XT"].h.rearrange("(c p) s -> p c s", p=128)[:, :, t0:t0 + 512], r=[sc["XT"]], w=[xT])
                    cs, sn = cst[blk % 2], snt[blk % 2]
                    kb.dma(cs[64:96, :], cos2[:, t0:t0 + 512], w=[cs])
                    kb.dma(sn[64:96, :], sin2[:, t0:t0 + 512], w=[sn])

                    def fm(col0, m, bank, lhs=None):
                        for c in range(8):
                            lt = (win[:, c, col0:col0 + m] if lhs is None else lhs[:, c, 0:m])
                            kb.op("pe", lambda e, c=c, lt=lt: e.matmul(PS[bank][0:m, :], lhsT=lt, rhs=xT[:, c, :],
                                                                        start=(c == 0), stop=(c == 7)),
                                  r=[win if lhs is None else lhs, xT], w=[PS[bank]])

                    for (col0, nchunk, gtile, dst, key) in ((0, 2, qgt, "CQT", "q"), (256, 1, kvgt, "CKVT", "kv")):
                        for cc in range(nchunk):
                            fm(col0 + cc * 128, 128, cc)
                        for cc in range(nchunk):
                            kb.op("act", lambda e, cc=cc: e.activation(out=sq[:, :], in_=PS[cc][:, :], func=AF.Square),
                                  r=[PS[cc]], w=[sq])
                            kb.op("pe", lambda e, cc=cc: e.matmul(PS[2][:, :], lhsT=onesb[:, :], rhs=sq[:, :],
                                                                  start=(cc == 0), stop=(cc == nchunk - 1)),
                                  r=[onesb, sq], w=[PS[2]])
                        nfeat = 128.0 * nchunk
                        rsqrt_eps(rstd, rstd[:, :], PS[2][:, :], [PS[2]], scale=1.0 / nfeat)
                        for cc in range(nchunk):
                            o = ev[evc % 2]
                            evc += 1
                            kb.op("dve", lambda e, cc=cc, o=o, gtile=gtile: e.scalar_tensor_tensor(
                                out=o[:, :], in0=PS[cc][:, :], scalar=gtile[:, cc:cc + 1], in1=rstd[:, :],
                                op0=ALU.mult, op1=ALU.mult), r=[PS[cc], gtile, rstd], w=[o])
                            kb.dma(sc[dst].h[cc * 128:(cc + 1) * 128, t0:t0 + 512], o[:, :], r=[o], w=[sc[dst]])
                    fm(320, 96, 3)
                    fm(0, 96, 4, lhs=wkr_sw)
                    kb.op("dve", lambda e: e.tensor_tensor(out=kr1[64:96, :], in0=PS[3][64:96, :], in1=cs[64:96, :], op=ALU.mult),
                          r=[PS[3], cs], w=[kr1])
                    kb.op("dve", lambda e: e.tensor_tensor(out=kr2[64:96, :], in0=PS[4][64:96, :], in1=sn[64:96, :], op=ALU.mult),
                          r=[PS[4], sn], w=[kr2])
                    kb.op("pool", lambda e: e.tensor_tensor(out=krb[64:96, :], in0=kr1[64:96, :], in1=kr2[64:96, :], op=ALU.add),
                          r=[kr1, kr2], w=[krb])
                    kb.dma(sc["KRT"].h[:, t0:t0 + 512], krb[64:96, :], r=[krb], w=[sc["KRT"]])
                    bi = 0
                    for (col0, dst, kind) in ((416, "POOLT", "f"), (544, "POOLT", "f"), (672, "QMT", "b"), (800, "QMT", "b"),
                                              (928, "KMT", "k"), (1056, "KMT", "k")):
                        bank = 5 + (bi % 2)
                        bi += 1
                        fm(col0, 128, bank)
                        r0 = ((col0 - 416) % 256) if dst == "POOLT" else ((col0 - 672) % 256)
                        if kind == "f":
                            o = evf[evc % 2]
                            evc += 1
                            kb.op("act", lambda e, o=o, bank=bank: e.activation(out=o[:, :], in_=PS[bank][:, :], func=AF.Copy),
                                  r=[PS[bank]], w=[o])
                        else:
                            o = ev[evc % 2]
                            evc += 1
                            scl = 0.125 if kind == "k" else 1.0
                            kb.op("act", lambda e, o=o, bank=bank, scl=scl: e.activation(out=o[:, :], in_=PS[bank][:, :],
                                                                                           func=AF.Copy, scale=scl),
                                  r=[PS[bank]], w=[o])
                        kb.dma(sc[dst].h[r0:r0 + 128, t0:t0 + 512], o[:, :], r=[o], w=[sc[dst]])
                    for ty in range(4):
                        fm(1696 + 4 * ty, 4, 3 + (ty % 2))
                        o = evf[evc % 2]
                        evc += 1
                        bank = 3 + (ty % 2)
                        kb.op("act", lambda e, o=o, bank=bank: e.activation(out=o[0:4, :], in_=PS[bank][0:4, :], func=AF.Copy),
                              r=[PS[bank]], w=[o])
                        kb.dma(sc["G4"].h[ty, :, t0:t0 + 512], o[0:4, :], r=[o], w=[sc["G4"]])
                    for tt in range(4):
                        tk0 = t0 + tt * 128
                        for c in range(8):
                            kb.op("pe", lambda e, c=c, tt=tt: e.matmul(PS[0][:, :], lhsT=xT[:, c, tt * 128:(tt + 1) * 128],
                                                                        rhs=win[:, c, 928:1440], start=(c == 0), stop=(c == 7)),
                                  r=[xT, win], w=[PS[0]])
                        for c in range(8):
                            kb.op("pe", lambda e, c=c, tt=tt: e.matmul(PS[1][:, 0:256], lhsT=xT[:, c, tt * 128:(tt + 1) * 128],
                                                                        rhs=win[:, c, 1440:1696], start=(c == 0), stop=(c == 7)),
                                  r=[xT, win], w=[PS[1]])
                        ob = tmb[tt % 2]
                        of = tmf[tt % 2]
                        kb.op("act", lambda e, ob=ob: e.activation(out=ob[:, 0:256], in_=PS[0][:, 0:256], func=AF.Copy, scale=0.125),
                              r=[PS[0]], w=[ob])
                        kb.op("dve", lambda e, ob=ob: e.tensor_copy(out=ob[:, 256:512], in_=PS[0][:, 256:512]), r=[PS[0]], w=[ob])
                        kb.op("act", lambda e, of=of: e.activation(out=of[:, :], in_=PS[1][:, 0:256], func=AF.Copy), r=[PS[1]], w=[of])
                        kb.dma(sc["KM"].h[tk0:tk0 + 128, :], ob[:, 0:256], r=[ob], w=[sc["KM"]])
                        kb.dma(sc["VM"].h[tk0:tk0 + 128, :], ob[:, 256:512], r=[ob], w=[sc["VM"]])
                        kb.dma(sc["OM"].h[tk0:tk0 + 128, :], of[:, :], r=[of], w=[sc["OM"]])
            kb.barrier()

        for j, S in enumerate(jobs):
            sc = SC[j]
            NKC = S // 128
            NQB = S // 512
            with ExitStack() as es:
                uniq[0] += 1
                STT = [T(es.enter_context(nc.psum_tensor("st%d_%d" % (i_, uniq[0]), [128, 1024], F32))) for i_ in range(3)]
                OTT = T(es.enter_context(nc.psum_tensor("ot_%d" % uniq[0], [128, 512], F32)))
                MSC = T(es.enter_context(nc.psum_tensor("msc_%d" % uniq[0], [128, 512], F32)))
                stg = [sb(es, "stg%d" % i, [128, 2048], F32) for i in range(2)]
                wq = sb(es, "wq", [128, 2, 768], BF16)
                wqs = sb(es, "wqs", [128, 2, 768], BF16)
                wkv = sb(es, "wkv", [128, 1, 1024], BF16)
                load_w_bf16(es, wq, lambda c, n0, n1: wq[:, c, n0:n1], w_uq[l], 2, 768, stg)
                load_w_bf16(es, wkv, lambda c, n0, n1: wkv[:, c, n0:n1], w_ukv[l], 1, 1024, stg)
                wq4 = wq.h.rearrange("p c (h d) -> p c h d", h=8)
                wqs4 = wqs.h.rearrange("p c (h d) -> p c h d", h=8)
                kb.op("dve", lambda e: e.tensor_copy(out=wqs[:, :, :], in_=wq[:, :, :]), r=[wq], w=[wqs])
                kb.op("dve", lambda e: e.tensor_copy(out=wqs4[:, :, :, 64:80], in_=wq4[:, :, :, 80:96]), r=[wq], w=[wqs])
                kb.op("dve", lambda e: e.tensor_copy(out=wqs4[:, :, :, 80:96], in_=wq4[:, :, :, 64:80]), r=[wq], w=[wqs])
                ckv = sb(es, "ckv", [128, S], BF16)
                KT = sb(es, "KT", [96, S], BF16)
                VA = sb(es, "VA", [128, NKC, 65], BF16)
                kb.dma(ckv[:], sc["CKVT"].h[:, :], r=[sc["CKVT"]], w=[ckv])
                kb.dma(KT[64:96, :], sc["KRT"].h[:, :], r=[sc["KRT"]], w=[KT])
                kb.op("dve", lambda e: e.memset(VA[:, :, 64:65], 1.0), w=[VA])
                cqs = [sb(es, "cq%d" % i, [128, 2, 512], BF16) for i in range(2)]
                cst = [sb(es, "acs%d" % i, [96, 512], F32) for i in range(2)]
                snt = [sb(es, "asn%d" % i, [96, 512], F32) for i in range(2)]
                QTs = [sb(es, "QT%d" % i, [96, 512], BF16) for i in range(2)]
                q1 = sb(es, "q1", [96, 512], F32)
                q2 = sb(es, "q2", [96, 512], F32)
                PTs = [sb(es, "PT%d" % i, [128, 1024], BF16) for i in range(3)]
                den = sb(es, "den", [65, 512], F32)
                bcs = sb(es, "bcs", [64, 512], F32)
                ots = [sb(es, "ot%d" % i, [64, 512], BF16) for i in range(2)]
                scale = 96.0 ** -0.5
                qbc = 0
                otcs = [sb(es, "otc%d" % i, [65, 512], F32) for i in range(2)]
                qt_ready = {}

                def build_q(h, qb, slot):
                    q0 = qb * 512
                    cq, cs, sn, QT = cqs[slot], cst[slot], snt[slot], QTs[slot]
                    kb.dma(cq[:], sc["CQT"].h.rearrange("(c p) s -> p c s", p=128)[:, :, q0:q0 + 512], r=[sc["CQT"]], w=[cq])
                    kb.dma(cs[64:96, :], cos2[:, q0:q0 + 512], w=[cs])
                    kb.dma(sn[64:96, :], sin2[:, q0:q0 + 512], w=[sn])
                    for c in range(2):
                        kb.op("pe", lambda e, c=c: e.matmul(MSC[0:96, :], lhsT=wq[:, c, h * 96:(h + 1) * 96], rhs=cq[:, c, :],
                                                            start=(c == 0), stop=(c == 1)), r=[wq, cq], w=[MSC])
                    kb.op("dve", lambda e: e.tensor_copy(out=QT[0:64, :], in_=MSC[0:64, :]), r=[MSC], w=[QT])
                    kb.op("dve", lambda e: e.tensor_tensor(out=q1[64:96, :], in0=MSC[64:96, :], in1=cs[64:96, :], op=ALU.mult),
                          r=[MSC, cs], w=[q1])
                    for c in range(2):
                        kb.op("pe", lambda e, c=c: e.matmul(MSC[0:96, :], lhsT=wqs[:, c, h * 96:(h + 1) * 96], rhs=cq[:, c, :],
                                                            start=(c == 0), stop=(c == 1)), r=[wqs, cq], w=[MSC])
                    kb.op("dve", lambda e: e.tensor_tensor(out=q2[64:96, :], in0=MSC[64:96, :], in1=sn[64:96, :], op=ALU.mult),
                          r=[MSC, sn], w=[q2])
                    kb.op("pool", lambda e: e.tensor_tensor(out=QT[64:96, :], in0=q1[64:96, :], in1=q2[64:96, :], op=ALU.add),
                          r=[q1, q2], w=[QT])
                    return QT
                for h in range(8):
                    for blk in range(NQB):
                        bk = MSC if blk % 2 == 0 else OTT
                        kb.op("pe", lambda e, blk=blk, bk=bk: e.matmul(bk[0:64, :], lhsT=wkv[:, 0, h * 128:h * 128 + 64],
                                                                        rhs=ckv[:, blk * 512:(blk + 1) * 512], start=True, stop=True),
                              r=[wkv, ckv], w=[bk])
                        kb.op("dve", lambda e, blk=blk, bk=bk: e.tensor_copy(out=KT[0:64, blk * 512:(blk + 1) * 512], in_=bk[0:64, :]),
                              r=[bk], w=[KT])
                    for g in range(NKC // 8):
                        bk = MSC if g % 2 == 0 else OTT
                        for i in range(8):
                            kc = g * 8 + i
                            kb.op("pe", lambda e, kc=kc, i=i, bk=bk: e.matmul(bk[:, i * 64:(i + 1) * 64], lhsT=ckv[:, kc * 128:(kc + 1) * 128],
                                                                               rhs=wkv[:, 0, h * 128 + 64:h * 128 + 128], start=True, stop=True),
                                  r=[wkv, ckv], w=[bk])
                        if g % 2 == 0:
                            kb.op("act", lambda e, g=g, bk=bk: e.activation(out=VA[:, g * 8:(g + 1) * 8, 0:64],
                                                                            in_=bk.h.rearrange("p (i d) -> p i d", i=8), func=AF.Copy),
                                  r=[bk], w=[VA])
                        else:
                            kb.op("dve", lambda e, g=g, bk=bk: e.tensor_copy(out=VA[:, g * 8:(g + 1) * 8, 0:64],
                                                                             in_=bk.h.rearrange("p (i d) -> p i d", i=8)),
                                  r=[bk], w=[VA])
                    for qb in range(NQB):
                        q0 = qb * 512
                        OT = OTT
                        ot = ots[qbc % 2]
                        otc = otcs[qbc % 2]
                        if (h, qb) in qt_ready:
                            QT = qt_ready.pop((h, qb))
                        else:
                            QT = build_q(h, qb, qbc % 2)
                        nxt = (h, qb + 1) if qb + 1 < NQB else ((h + 1, 0) if h + 1 < 8 else None)
                        nslot = (qbc + 1) % 2
                        qbc += 1
                        npair = NKC // 2

                        def mm1(p):
                            st = STT[p % 3]
                            for i in range(2):
                                kc = 2 * p + i
                                kb.op("pe", lambda e, kc=kc, i=i: e.matmul(st[:, i * 512:(i + 1) * 512], lhsT=KT[0:96, kc * 128:(kc + 1) * 128],
                                                                            rhs=QT[0:96, :], start=True, stop=True),
                                      r=[KT, QT], w=[st])

                        def ex(p):
                            st = STT[p % 3]
                            pt = PTs[p % 3]
                            kb.op("act", lambda e: e.activation(out=pt[:, :], in_=st[:, :], func=AF.Exp, scale=scale),
                                  r=[st], w=[pt])

                        def mm2(p):
                            pt = PTs[p % 3]
                            for i in range(2):
                                kc = 2 * p + i
                                kb.op("pe", lambda e, kc=kc, i=i: e.matmul(OT[0:65, :], lhsT=VA[:, kc, :], rhs=pt[:, i * 512:(i + 1) * 512],
                                                                            start=(kc == 0), stop=(kc == NKC - 1)),
                                      r=[VA, pt], w=[OT])

                        mm1(0)
                        if npair > 1:
                            mm1(1)
                        if nxt is not None:
                            qt_ready[nxt] = build_q(nxt[0], nxt[1], nslot)
                        for p in range(npair):
                            ex(p)
                            if p + 2 < npair:
                                mm1(p + 2)
                            mm2(p)
                        kb.op("dve", lambda e: e.tensor_copy(out=otc[0:65, :], in_=OT[0:65, :]), r=[OT], w=[otc])
                        kb.op("dve", lambda e: e.reciprocal(out=den[64:65, :], in_=otc[64:65, :]), r=[otc], w=[den])
                        kb.op("pe", lambda e: e.matmul(MSC[0:64, :], lhsT=onesf[64:65, 0:64], rhs=den[64:65, :], start=True, stop=True),
                              r=[onesf, den], w=[MSC])
                        kb.op("dve", lambda e: e.tensor_tensor(out=ot[:, :], in0=otc[0:64, :], in1=MSC[0:64, :], op=ALU.mult),
                              r=[otc, MSC], w=[ot])
                        kb.dma(sc["MIXT"].h[h * 64:(h + 1) * 64, q0:q0 + 512], ot[:, :], r=[ot], w=[sc["MIXT"]])
                kb.barrier()

        with ExitStack() as es:
            std_psum(es)
            stg = [sb(es, "stg%d" % i, [128, 2048], F32) for i in range(2)]
            wp = sb(es, "wp", [128, 2, 64], BF16)
            for g in range(4):
                st = stg[g % 2]
                p0 = (g % 2) * 64
                kb.dma(st[p0:p0 + 64, 0:64], w_pool[l, g], w=[st])
                kb.op("dve", lambda e, st=st, p0=p0, g=g: e.tensor_copy(out=wp[p0:p0 + 64, g // 2, :], in_=st[p0:p0 + 64, 0:64]),
                      r=[st], w=[wp])
            psct = sb(es, "psct", [64, 4], F32)
            kb.dma(psct[:], psc[l], w=[psct])
            PB = 2048
            xps = [sb(es, "xp%d" % i, [128, 2, PB + 16], F32) for i in range(2)]
            a2 = sb(es, "a2", [128, PB + 16], F32)
            a4 = sb(es, "a4", [128, PB + 16], F32)
            yb = [sb(es, "yb%d" % i, [128, PB], BF16) for i in range(2)]
            yf = sb(es, "yf", [128, PB], F32)
            ped = sb(es, "ped", [128, 2, 16], F32)
            po = [sb(es, "po%d" % i, [64, 512], BF16) for i in range(2)]
            WINS = (2, 4, 8, 16)
            bc = 0
            for j, S in enumerate(jobs):
                sc = SC[j]
                kb.dma(ped[:], pedge[j], w=[ped])
                pb = min(PB, S)
                for blk in range(S // pb):
                    t0 = blk * pb
                    xp = xps[bc % 2]
                    bc += 1
                    lo = max(0, t0 - 8)
                    hi = min(S, t0 + pb + 8)
                    if lo > t0 - 8:
                        kb.op("pool", lambda e: e.memset(xp[:, :, 0:8], 0.0), w=[xp])
                    if hi < t0 + pb + 8:
                        kb.op("pool", lambda e: e.memset(xp[:, :, pb + 8:pb + 16], 0.0), w=[xp])
                    kb.dma(xp[:, :, 8 + (lo - t0):8 + (hi - t0)],
                           sc["POOLT"].h.rearrange("(c p) s -> p c s", p=128)[:, :, lo:hi], r=[sc["POOLT"]], w=[xp])
                    for c in range(2):
                        n = pb + 16
                        x = xp.h[:, c, :]
                        kb.op("dve", lambda e, x=x: e.tensor_tensor(out=a2[:, 0:n - 1], in0=x[:, 0:n - 1], in1=x[:, 1:n], op=ALU.add),
                              r=[xp], w=[a2])
                        kb.op("pool", lambda e: e.tensor_tensor(out=a4[:, 0:n - 3], in0=a2[:, 0:n - 3], in1=a2[:, 2:n - 1], op=ALU.add),
                              r=[a2], w=[a4])
                        if c == 1:
                            kb.op("dve", lambda e: e.tensor_tensor(out=a2[:, 0:n - 7], in0=a4[:, 0:n - 7], in1=a4[:, 4:n - 3], op=ALU.add),
                                  r=[a4], w=[a2])
                            kb.op("pool", lambda e: e.tensor_tensor(out=a4[64:128, 0:n - 15], in0=a2[64:128, 0:n - 15],
                                                                    in1=a2[64:128, 8:n - 7], op=ALU.add), r=[a2], w=[a4])
                        for half in range(2):
                            w = WINS[2 * c + half]
                            src = a2 if half == 0 else a4
                            p0 = half * 64
                            o0 = 8 - w // 2
                            kb.op("dve", lambda e, src=src, p0=p0, w=w, o0=o0, x=x: e.scalar_tensor_tensor(
                                out=yf[p0:p0 + 64, 0:pb], in0=src[p0:p0 + 64, o0:o0 + pb], scalar=1.0 / w,
                                in1=x[p0:p0 + 64, 8:8 + pb], op0=ALU.mult, op1=ALU.subtract), r=[src, xp], w=[yf])
                            if t0 == 0:
                                kb.op("dve", lambda e, src=src, p0=p0, o0=o0, x=x, c=c: e.tensor_tensor(
                                    out=yf[p0:p0 + 64, 0:8], in0=src[p0:p0 + 64, o0:o0 + 8], in1=ped[p0:p0 + 64, c, 0:8], op=ALU.mult),
                                    r=[src, ped], w=[yf])
                                kb.op("dve", lambda e, p0=p0, x=x: e.tensor_tensor(
                                    out=yf[p0:p0 + 64, 0:8], in0=yf[p0:p0 + 64, 0:8], in1=x[p0:p0 + 64, 8:16], op=ALU.subtract),
                                    r=[xp], w=[yf])
                            if t0 + pb == S:
                                kb.op("dve", lambda e, src=src, p0=p0, o0=o0, x=x, c=c: e.tensor_tensor(
                                    out=yf[p0:p0 + 64, pb - 8:pb], in0=src[p0:p0 + 64, o0 + pb - 8:o0 + pb], in1=ped[p0:p0 + 64, c, 8:16],
                                    op=ALU.mult), r=[src, ped], w=[yf])
                                kb.op("dve", lambda e, p0=p0, x=x: e.tensor_tensor(
                                    out=yf[p0:p0 + 64, pb - 8:pb], in0=yf[p0:p0 + 64, pb - 8:pb], in1=x[p0:p0 + 64, pb:pb + 8],
                                    op=ALU.subtract), r=[xp], w=[yf])
                        ybt = yb[c]
                        kb.op("act", lambda e, ybt=ybt: e.activation(out=ybt[:, 0:pb], in_=yf[:, 0:pb], func=AF.Copy), r=[yf], w=[ybt])
                        for half in range(2):
                            g = 2 * c + half
                            p0 = half * 64
                            for sb_ in range(pb // 512):
                                bank = 5 + (sb_ % 2)
                                kb.op("pe", lambda e, p0=p0, c=c, sb_=sb_, ybt=ybt, bank=bank: e.matmul(
                                    PS[bank][0:64, :], lhsT=wp[p0:p0 + 64, c, :], rhs=ybt[p0:p0 + 64, sb_ * 512:(sb_ + 1) * 512],
                                    start=True, stop=True), r=[wp, ybt], w=[PS[bank]])
                                o = po[sb_ % 2]
                                kb.op("act", lambda e, o=o, bank=bank, g=g: e.activation(out=o[:, :], in_=PS[bank][0:64, :], func=AF.Copy,
                                                                                         scale=psct[:, g:g + 1]),
                                      r=[PS[bank], psct], w=[o])
                                kb.dma(sc["MIXT"].h[512 + g * 64:512 + (g + 1) * 64, t0 + sb_ * 512:t0 + (sb_ + 1) * 512], o[:, :],
                                       r=[o], w=[sc["MIXT"]])
            kb.barrier()

        for j, S in enumerate(jobs):
            sc = SC[j]
            SCH = min(S, 2048)
            NSC = S // SCH
            NCH = SCH // 128
            NCT = S // 128
            with ExitStack() as es:
                std_psum(es)
                gb = sb(es, "gb", [4, 4], F32)
                ngb = sb(es, "ngb", [4, 4], F32)
                kb.dma(gb[:], gbias[l], w=[gb])
                kb.op("dve", lambda e: e.tensor_scalar(out=ngb[:, :], in0=gb[:, :], scalar1=-1.0, scalar2=None, op0=ALU.mult),
                      r=[gb], w=[ngb])
                selt = sb(es, "selt", [4, 4, 128], F32)
                kb.dma(selt[:], sel_d[:, :, :], w=[selt])
                masks = [sb(es, "mkf", [128, 128], F32), sb(es, "mkb", [128, 128], F32)]
                kb.dma(masks[0][:], maskf_d[:, :], w=[masks[0]])
                kb.dma(masks[1][:], maskb_d[:, :], w=[masks[1]])
                repn = sb(es, "repn", [128, 256], F32)
                kb.dma(repn[:], rep_l[l, :, 4 * D:4 * D + 256], w=[repn])
                ones4 = sb(es, "ones4", [4, SCH], F32)
                kb.op("pool", lambda e: e.memset(ones4[:], 1.0), w=[ones4])
                gi = sb(es, "gi", [4, SCH], F32)
                gf = sb(es, "gf", [4, SCH], F32)
                t1 = sb(es, "gt1", [4, SCH], F32)
                t2 = sb(es, "gt2", [4, SCH], F32)
                Mo = sb(es, "Mo", [4, SCH], F32)
                WIo = sb(es, "WIo", [4, SCH], F32)
                A_ = sb(es, "A_", [4, SCH], F32)
                NMt = sb(es, "NMt", [4, SCH], F32)
                WS = sb(es, "WS", [4, SCH], F32)
                EM = sb(es, "EM", [4, SCH], F32)
                STK = sb(es, "STK", [128, SCH], F32)
                carNB = sb(es, "carNB", [4, 1], F32)
                carM = sb(es, "carM", [4, 1], F32)
                kb.op("pool", lambda e: e.memset(STK[:], 0.0), w=[STK])
                qT = sb(es, "mqT", [64, 4, SCH], BF16)
                kT = sb(es, "mkT", [64, 4, SCH], BF16)
                CN = [sb(es, "CN%d" % h, [64, 65], F32) for h in range(4)]
                CNb = [sb(es, "CNb%d" % h, [64, 65], BF16) for h in range(4)]
                tq = [sb(es, "tq%d" % i, [128, 16], F32) for i in range(2)]
                vas = [sb(es, "va%d" % i, [128, 4, 65], BF16) for i in range(2)]
                vms = [sb(es, "vm%d" % i, [128, 256], BF16) for i in range(2)]
                kms = [sb(es, "km%d" % i, [128, 256], BF16) for i in range(2)]
                Dm = [sb(es, "Dm%d" % i, [128, 128], F32) for i in range(4)]
                Em = [sb(es, "Em%d" % i, [128, 128], F32) for i in range(4)]
                PTm = [sb(es, "PTm%d" % i, [128, 128], BF16) for i in range(4)]
                intra = [sb(es, "intra%d" % i, [128, 65], F32) for i in range(4)]
                HN = [sb(es, "HN%d" % i, [128, 65], F32) for i in range(4)]
                dn = [sb(es, "dn%d" % i, [128, 2], F32) for i in range(4)]
                dcs = [sb(es, "dcs%d" % i, [64, 1], F32) for i in range(4)]
                KW = [sb(es, "KW%d" % i, [128, 64], BF16) for i in range(4)]
                HT = [sb(es, "HT%d" % i, [128, 256], F32) for i in range(2)]
                hfl = [sb(es, "hfl%d" % i, [128, 256], F32) for i in range(2)]
                oml = [sb(es, "oml%d" % i, [128, 256], F32) for i in range(2)]
                gst = sb(es, "gst", [128, 4, 6], F32)
                gmv = sb(es, "gmv", [128, 4, 2], F32)
                grs = sb(es, "grs", [128, 4], F32)
                yb_ = [sb(es, "myb%d" % i, [128, 256], BF16) for i in range(2)]
                yT = [sb(es, "myT%d" % i, [128, 256], BF16) for i in range(2)]
                it = 0
                un = 0
                for d in range(2):
                    kb.flush()
                    kb.op("dve", lambda e: e.memset(carNB[:], 0.0), w=[carNB])
                    kb.op("dve", lambda e: e.memset(carM[:], 0.0), w=[carM])
                    gci = 0
                    for k in range(NSC):
                        o0 = k * SCH if d == 0 else (NSC - 1 - k) * SCH
                        kb.dma(gi[:], sc["G4"].h[2 * d, :, o0:o0 + SCH], r=[sc["G4"]], w=[gi])
                        kb.dma(gf[:], sc["G4"].h[2 * d + 1, :, o0:o0 + SCH], r=[sc["G4"]], w=[gf])
                        kb.dma(qT[:], sc["QMT"].h.rearrange("(h p) s -> p h s", p=64)[:, :, o0:o0 + SCH], r=[sc["QMT"]], w=[qT])
                        kb.dma(kT[:], sc["KMT"].h.rearrange("(h p) s -> p h s", p=64)[:, :, o0:o0 + SCH], r=[sc["KMT"]], w=[kT])
                        kb.op("dve", lambda e, d=d: e.tensor_scalar(out=gi[:, :], in0=gi[:, :], scalar1=gb[:, 2 * d:2 * d + 1], scalar2=None,
                                                                    op0=ALU.add), r=[gb], w=[gi])
                        kb.op("act", lambda e, d=d: e.activation(out=t1[:, :], in_=gf[:, :], func=AF.Exp, scale=-1.0,
                                                                 bias=ngb[:, 2 * d + 1:2 * d + 2]), r=[gf, ngb], w=[t1])
                        kb.op("act", lambda e: e.activation(out=t1[:, :], in_=t1[:, :], func=AF.Ln, scale=1.0, bias=onesf[0:4, 0:1]),
                              r=[onesf], w=[t1])
                        if d == 0:
                            isrc, fsrc = gi, t1
                        else:
                            kb.op("dve", lambda e: e.tensor_copy(out=t2[:, :], in_=gi[:, ::-1]), r=[gi], w=[t2])
                            kb.op("dve", lambda e: e.tensor_copy(out=gf[:, :], in_=t1[:, ::-1]), r=[t1], w=[gf])
                            isrc, fsrc = t2, gf
                        kb.op("dve", lambda e, fsrc=fsrc: e.tensor_tensor_scan(out=NMt[:, :], data0=ones4[:, :], data1=fsrc[:, :],
                                                                               initial=carNB[:, 0:1], op0=ALU.mult, op1=ALU.add),
                              r=[ones4, fsrc, carNB], w=[NMt])
                        kb.op("dve", lambda e: e.tensor_copy(out=carNB[:, 0:1], in_=NMt[:, SCH - 1:SCH]), r=[NMt], w=[carNB])
                        kb.op("dve", lambda e, isrc=isrc: e.tensor_tensor(out=A_[:, :], in0=isrc[:, :], in1=NMt[:, :], op=ALU.add),
                              r=[isrc, NMt], w=[A_])
                        Mf = t1 if d == 0 else gi
                        kb.op("dve", lambda e, Mf=Mf: e.tensor_tensor_scan(out=Mf[:, :], data0=ones4[:, :], data1=A_[:, :],
                                                                           initial=carM[:, 0:1], op0=ALU.mult, op1=ALU.max),
                              r=[ones4, A_, carM], w=[Mf])
                        kb.op("dve", lambda e, Mf=Mf: e.tensor_tensor(out=EM[:, :], in0=NMt[:, :], in1=Mf[:, :], op=ALU.subtract),
                              r=[NMt, Mf], w=[EM])
                        kb.op("act", lambda e: e.activation(out=EM[:, :], in_=EM[:, :], func=AF.Exp), r=[], w=[EM])
                        kb.op("dve", lambda e, Mf=Mf: e.tensor_scalar(out=NMt[:, :], in0=Mf[:, :], scalar1=-1.0, scalar2=None, op0=ALU.mult),
                              r=[Mf], w=[NMt])
                        WIf = gf if d == 0 else t1
                        for c in range(NCH):
                            c0 = c * 128
                            bprev = carM[:, 0:1] if c == 0 else Mf[:, c0 - 1:c0]
                            kb.op("act", lambda e, c0=c0, bprev=bprev, WIf=WIf: e.activation(out=WIf[:, c0:c0 + 128], in_=NMt[:, c0:c0 + 128],
                                                                                             func=AF.Exp, bias=bprev),
                                  r=[NMt, Mf, carM], w=[WIf])
                            kb.op("act", lambda e, c0=c0: e.activation(out=WS[:, c0:c0 + 128], in_=A_[:, c0:c0 + 128], func=AF.Exp,
                                                                       bias=NMt[:, c0 + 127:c0 + 128]), r=[A_, NMt], w=[WS])
                        kb.op("dve", lambda e, Mf=Mf: e.tensor_copy(out=carM[:, 0:1], in_=Mf[:, SCH - 1:SCH]), r=[Mf], w=[carM])
                        if d == 0:
                            kb.op("pool", lambda e, Mf=Mf: e.tensor_copy(out=Mo[:, :], in_=Mf[:, :]), r=[Mf], w=[Mo])
                            kb.op("pool", lambda e, WIf=WIf: e.tensor_copy(out=WIo[:, :], in_=WIf[:, :]), r=[WIf], w=[WIo])
                            srcs = (A_, WIo, EM, WS)
                        else:
                            kb.op("dve", lambda e, Mf=Mf: e.tensor_copy(out=Mo[:, :], in_=Mf[:, ::-1]), r=[Mf], w=[Mo])
                            kb.op("dve", lambda e, WIf=WIf: e.tensor_copy(out=WIo[:, :], in_=WIf[:, ::-1]), r=[WIf], w=[WIo])
                            kb.op("dve", lambda e: e.tensor_copy(out=t2[:, :], in_=A_[:, ::-1]), r=[A_], w=[t2])
                            kb.op("dve", lambda e: e.tensor_copy(out=A_[:, :], in_=EM[:, ::-1]), r=[EM], w=[A_])
                            kb.op("dve", lambda e: e.tensor_copy(out=EM[:, :], in_=WS[:, ::-1]), r=[WS], w=[EM])
                            srcs = (t2, WIo, A_, EM)
                        for kk, s_ in enumerate(srcs):
                            kb.dma(STK[32 * kk:32 * kk + 4, :], s_[:, :], r=[s_], w=[STK])
                        order = range(NCH) if d == 0 else range(NCH - 1, -1, -1)
                        for c in order:
                            c0 = c * 128
                            g0 = o0 + c0
                            ci = gci
                            gci += 1
                            sl = it % 2
                            it += 1
                            tqs, va, vm, km, ht = tq[sl], vas[sl], vms[sl], kms[sl], HT[sl]
                            kb.op("pe", lambda e, c0=c0: e.transpose(out=PS[1][:, 384:512], in_=STK[:, c0:c0 + 128], identity=identf[:]),
                                  r=[STK, identf], w=[PS[1]])
                            kb.op("act", lambda e, tqs=tqs: e.activation(
                                out=tqs.h.rearrange("p (k h) -> p k h", k=4),
                                in_=PS[1].h[:, 384:512].rearrange("p (k x) -> p k x", k=4)[:, :, 0:4], func=AF.Copy), r=[PS[1]], w=[tqs])
                            kb.dma(vm[:], sc["VM"].h[g0:g0 + 128, :], r=[sc["VM"]], w=[vm])
                            kb.dma(km[:], sc["KM"].h[g0:g0 + 128, :], r=[sc["KM"]], w=[km])
                            if d == 1:
                                kb.dma(hfl[sl][:], sc["HF"].h[g0:g0 + 128, :], r=[sc["HF"]], w=[hfl[sl]])
                                kb.dma(oml[sl][:], sc["OM"].h[g0:g0 + 128, :], r=[sc["OM"]], w=[oml[sl]])
                            kb.flush()
                            kb.op("pool", lambda e, va=va, vm=vm: e.tensor_copy(out=va[:, :, 0:64], in_=vm.h.rearrange("p (h x) -> p h x", h=4)),
                                  r=[vm], w=[va])
                            kb.op("pool", lambda e, va=va: e.memset(va[:, :, 64:65], 1.0), w=[va])
                            def RG(h):
                                if h < 3:
                                    A_b, B_b = PS[2 * h], PS[2 * h + 1]
                                    return dict(A=A_b, B=B_b, st=A_b[:, 0:128], mb=A_b[:, 128:256], ia=B_b[:, 0:65], ie=B_b[:, 128:193],
                                                dc=B_b[0:64, 256:321], dec=B_b[0:64, 330:331])
                                A_b = PS[6]
                                return dict(A=A_b, B=A_b, st=A_b[:, 0:128], mb=A_b[:, 128:256], ia=A_b[:, 256:321], ie=A_b[:, 321:386],
                                            dc=A_b[0:64, 386:451], dec=A_b[0:64, 451:452])
                            R4 = [RG(h) for h in range(4)]
                            for h in range(4):
                                g = R4[h]
                                kb.op("pe", lambda e, h=h, g=g: e.matmul(g["st"], lhsT=kT[:, h, c0:c0 + 128], rhs=qT[:, h, c0:c0 + 128],
                                                                          start=True, stop=True), r=[kT, qT], w=[g["A"]])
                                kb.op("pe", lambda e, h=h, g=g: e.matmul(g["mb"], lhsT=selt[:, h, :], rhs=Mo[:, c0:c0 + 128],
                                                                          start=True, stop=True), r=[selt, Mo], w=[g["A"]])
                            for h in range(4):
                                g = R4[h]
                                kb.op("dve", lambda e, h=h, g=g: e.scalar_tensor_tensor(
                                    out=Dm[h][:, :], in0=g["mb"], scalar=tqs[:, h:h + 1], in1=masks[d][:, :],
                                    op0=ALU.subtract, op1=ALU.max), r=[g["A"], tqs, masks[d]], w=[Dm[h]])
                            for h in range(4):
                                kb.op("act", lambda e, h=h: e.activation(out=Em[h][:, :], in_=Dm[h][:, :], func=AF.Exp, scale=-1.0),
                                      r=[Dm[h]], w=[Em[h]])
                            for h in range(4):
                                g = R4[h]
                                kb.op("dve", lambda e, h=h, g=g: e.tensor_tensor(out=PTm[h][:, :], in0=g["st"], in1=Em[h][:, :], op=ALU.mult),
                                      r=[g["A"], Em[h]], w=[PTm[h]])
                            for h in range(4):
                                g = R4[h]
                                kb.op("pe", lambda e, h=h, g=g: e.matmul(g["ia"], lhsT=PTm[h][:, :], rhs=va[:, h, :], start=True, stop=True),
                                      r=[PTm[h], va], w=[g["B"]])
                            for h in range(4):
                                g = R4[h]
                                if ci == 0:
                                    kb.op("act", lambda e, h=h, g=g: e.activation(out=HN[h][:, :], in_=g["ia"], func=AF.Copy),
                                          r=[g["B"]], w=[HN[h]])
                                else:
                                    kb.op("act", lambda e, h=h, g=g: e.activation(out=intra[h][:, :], in_=g["ia"], func=AF.Copy),
                                          r=[g["B"]], w=[intra[h]])
                                    kb.op("pe", lambda e, h=h, g=g: e.matmul(g["ie"], lhsT=qT[:, h, c0:c0 + 128], rhs=CNb[h][:, :],
                                                                              start=True, stop=True), r=[qT, CNb[h]], w=[g["B"]])
                            if ci > 0:
                                for h in range(4):
                                    g = R4[h]
                                    kb.op("dve", lambda e, h=h, g=g: e.scalar_tensor_tensor(
                                        out=HN[h][:, :], in0=g["ie"], scalar=tqs[:, 4 + h:5 + h], in1=intra[h][:, :],
                                        op0=ALU.mult, op1=ALU.add), r=[g["B"], tqs, intra[h]], w=[HN[h]])
                            for h in range(4):
                                kb.op("dve", lambda e, h=h: e.scalar_tensor_tensor(
                                    out=dn[h][:, 0:1], in0=HN[h][:, 64:65], scalar=-1.0, in1=HN[h][:, 64:65],
                                    op0=ALU.mult, op1=ALU.max), r=[HN[h]], w=[dn[h]])
                            for h in range(4):
                                kb.op("dve", lambda e, h=h: e.tensor_scalar(
                                    out=dn[h][:, 0:1], in0=dn[h][:, 0:1], scalar1=tqs[:, 8 + h:9 + h], scalar2=None,
                                    op0=ALU.max), r=[tqs], w=[dn[h]])
                            for h in range(4):
                                kb.op("dve", lambda e, h=h: e.reciprocal(out=dn[h][:, 1:2], in_=dn[h][:, 0:1]), r=[], w=[dn[h]])
                            for h in range(4):
                                kb.op("dve", lambda e, h=h: e.tensor_scalar(
                                    out=ht[:, h * 64:(h + 1) * 64], in0=HN[h][:, 0:64], scalar1=dn[h][:, 1:2], scalar2=None, op0=ALU.mult),
                                    r=[HN[h], dn[h]], w=[ht])
                            if ci < NCT - 1:
                                col = c0 + 127 if d == 0 else c0
                                for h in range(4):
                                    g = R4[h]
                                    kb.op("pool", lambda e, h=h: e.tensor_scalar(
                                        out=KW[h][:, :], in0=km[:, h * 64:(h + 1) * 64], scalar1=tqs[:, 12 + h:13 + h], scalar2=None,
                                        op0=ALU.mult), r=[km, tqs], w=[KW[h]])
                                    kb.op("pe", lambda e, h=h, g=g: e.matmul(g["dc"], lhsT=KW[h][:, :], rhs=va[:, h, :], start=True, stop=True),
                                          r=[KW[h], va], w=[g["B"]])
                                    if ci > 0:
                                        kb.op("pe", lambda e, h=h, g=g: e.matmul(g["dec"], lhsT=selt[:, h, 0:64], rhs=WIo[:, col:col + 1],
                                                                                  start=True, stop=True), r=[selt, WIo], w=[g["B"]])
                                for h in range(4):
                                    g = R4[h]
                                    if ci == 0:
                                        kb.op("act", lambda e, h=h, g=g: e.activation(out=CN[h][:, :], in_=g["dc"], func=AF.Copy),
                                              r=[g["B"]], w=[CN[h]])
                                    else:
                                        kb.op("dve", lambda e, h=h, g=g: e.tensor_copy(out=dcs[h][:, 0:1], in_=g["dec"]),
                                              r=[g["B"]], w=[dcs[h]])
                                for h in range(4):
                                    g = R4[h]
                                    if ci > 0:
                                        kb.op("dve", lambda e, h=h, g=g: e.scalar_tensor_tensor(
                                            out=CN[h][:, :], in0=CN[h][:, :], scalar=dcs[h][:, 0:1], in1=g["dc"],
                                            op0=ALU.mult, op1=ALU.add), r=[g["B"], dcs[h]], w=[CN[h]])
                                for h in range(4):
                                    kb.op("act", lambda e, h=h: e.activation(out=CNb[h][:, :], in_=CN[h][:, :], func=AF.Copy),
                                          r=[CN[h]], w=[CNb[h]])
                            if d == 0:
                                kb.dma_later(sc["HF"].h[g0:g0 + 128, :], ht[:, :], r=[ht], w=[sc["HF"]])
                            else:
                                hf, om, ybt, yTt = hfl[sl], oml[sl], yb_[sl], yT[sl]
                                kb.op("dve", lambda e, hf=hf, ht=ht: e.tensor_tensor(out=hf[:, :], in0=hf[:, :], in1=ht[:, :], op=ALU.add),
                                      r=[ht], w=[hf])
                                for h in range(4):
                                    kb.op("dve", lambda e, h=h, hf=hf: e.bn_stats(out=gst[:, h, :], in_=hf[:, h * 64:(h + 1) * 64]),
                                          r=[hf], w=[gst])
                                    kb.op("dve", lambda e, h=h: e.bn_aggr(out=gmv[:, h, :], in_=gst[:, h, :]), r=[gst], w=[gmv])
                                rsqrt_eps(grs, grs[:, :], gmv[:, :, 1], [gmv])
                                for h in range(4):
                                    kb.op("dve", lambda e, h=h, hf=hf: e.tensor_scalar(
                                        out=hf[:, h * 64:(h + 1) * 64], in0=hf[:, h * 64:(h + 1) * 64], scalar1=gmv[:, h, 0:1],
                                        scalar2=grs[:, h:h + 1], op0=ALU.subtract, op1=ALU.mult), r=[gmv, grs], w=[hf])
                                kb.op("act", lambda e, om=om: e.activation(out=om[:, :], in_=om[:, :], func=AF.Sigmoid), r=[], w=[om])
                                kb.op("pool", lambda e, hf=hf: e.tensor_tensor(out=hf[:, :], in0=hf[:, :], in1=repn[:, :], op=ALU.mult),
                                      r=[repn], w=[hf])
                                kb.op("dve", lambda e, hf=hf, om=om, ybt=ybt: e.tensor_tensor(out=ybt[:, :], in0=hf[:, :], in1=om[:, :],
                                                                                             op=ALU.mult), r=[hf, om], w=[ybt])
                                for cc in range(2):
                                    kb.op("pe", lambda e, cc=cc, ybt=ybt: e.transpose(out=PSB[:, cc * 128:(cc + 1) * 128],
                                                                                      in_=ybt[:, cc * 128:(cc + 1) * 128], identity=identb[:]),
                                          r=[ybt, identb], w=[PSB])
                                kb.op("act", lambda e, yTt=yTt: e.activation(out=yTt[:, :], in_=PSB[:, 0:256], func=AF.Copy), r=[PSB], w=[yTt])
                                kb.dma_later(sc["MIXT"].h[768:1024, :].rearrange("(c p) s -> p c s", p=128)[:, :, g0:g0 + 128],
                                             yTt.h.rearrange("p (c t) -> p c t", c=2), r=[yTt], w=[sc["MIXT"]])
                kb.barrier()

        if dbg and l == 0:
            dbg_out = T(nc.dram_tensor("dbg_mixt", [D, jobs[0]], BF16, kind="ExternalOutput").ap())
            with ExitStack() as es:
                dt_ = sb(es, "dbgt", [128, 8, jobs[0]], BF16)
                kb.dma(dt_[:], SC[0]["MIXT"].h.rearrange("(c p) s -> p c s", p=128), r=[SC[0]["MIXT"]], w=[dt_])
                kb.dma(dbg_out.h.rearrange("(c p) s -> p c s", p=128), dt_[:], r=[dt_], w=[dbg_out])
                kb.barrier()
        alpha = float(8.0 ** 0.25)
        with ExitStack() as es:
            std_psum(es)
            stg = [sb(es, "stg%d" % i, [128, 2048], F32) for i in range(2)]
            wo = sb(es, "wo", [128, 8, D], BF16)
            load_w_bf16(es, wo, lambda c, n0, n1: wo[:, c, n0:n1], w_out[l], 8, D, stg)
            rl = sb(es, "rl", [128, 2 * D], F32)
            kb.dma(rl[:], rep_l[l, :, 0:2 * D], w=[rl])
            tmp = {"st6": sb(es, "st6", [128, 2, 6], F32), "mv": sb(es, "mv", [128, 2], F32), "rs": sb(es, "rs", [128, 1], F32)}
            mts = [sb(es, "mt%d" % i, [128, 8, 512], BF16) for i in range(2)]
            xres = [sb(es, "xres%d" % i, [128, D], F32) for i in range(2)]
            z = [sb(es, "z%d" % i, [128, D], F32) for i in range(2)]
            x1f = [sb(es, "x1f%d" % i, [128, D], F32) for i in range(2)]
            ob = [sb(es, "ob%d" % i, [128, D], BF16) for i in range(2)]
            xts = [sb(es, "xts%d" % i, [128, D], BF16) for i in range(2)]
            bi = 0
            for j, S in enumerate(jobs):
                sc = SC[j]
                XNi, XNo = sc["XN"][0], sc["XN"][1]
                for blk in range(S // 512):
                    t0 = blk * 512
                    mt = mts[bi % 2]
                    bi += 1
                    kb.dma(mt[:], sc["MIXT"].h.rearrange("(c p) s -> p c s", p=128)[:, :, t0:t0 + 512], r=[sc["MIXT"]], w=[mt])
                    def mm_a(tt):
                        for nb in range(2):
                            for c in range(8):
                                kb.op("pe", lambda e, c=c, nb=nb, tt=tt: e.matmul(PS[nb + 2 * (tt % 2)][:, :], lhsT=mt[:, c, tt * 128:(tt + 1) * 128],
                                                                                   rhs=wo[:, c, nb * 512:(nb + 1) * 512],
                                                                                   start=(c == 0), stop=(c == 7)), r=[mt, wo], w=[PS[nb + 2 * (tt % 2)]])
                    mm_a(0)
                    for tt in range(4):
                        tk0 = t0 + tt * 128
                        xr, zt, obt, x1, xt_ = xres[tt % 2], z[tt % 2], ob[tt % 2], x1f[tt % 2], xts[tt % 2]
                        kb.dma(xr[:], XNi.h[tk0:tk0 + 128, :], r=[XNi], w=[xr])
                        kb.flush()
                        if tt + 1 < 4:
                            mm_a(tt + 1)
                        for nb in range(2):
                            kb.op("dve", lambda e, nb=nb, xr=xr, zt=zt, tt=tt: e.scalar_tensor_tensor(
                                out=zt[:, nb * 512:(nb + 1) * 512], in0=xr[:, nb * 512:(nb + 1) * 512], scalar=alpha,
                                in1=PS[nb + 2 * (tt % 2)][:, :], op0=ALU.mult, op1=ALU.add), r=[xr, PS[nb + 2 * (tt % 2)]], w=[zt])
                        layer_norm_tile(zt, rl, rl, 0, D, x1, obt, tmp)
                        kb.dma_later(XNo.h[tk0:tk0 + 128, :], x1[:], r=[x1], w=[XNo])
                        transpose_to_xt(obt, xt_, sc["XT"], tk0)
            kb.barrier()
        with ExitStack() as es:
            std_psum(es)
            stg = [sb(es, "stg%d" % i, [128, 2048], F32) for i in range(2)]
            wg = sb(es, "wg", [128, 8, DFF], BF16)
            wu = sb(es, "wu", [128, 8, DFF], BF16)
            load_w_bf16(es, wg, lambda c, n0, n1: wg[:, c, n0:n1], w_gate[l], 8, DFF, stg)
            load_w_bf16(es, wu, lambda c, n0, n1: wu[:, c, n0:n1], w_up[l], 8, DFF, stg)
            x1Ts = [sb(es, "x1T%d" % i, [128, 8, 512], BF16) for i in range(2)]
            sg = [sb(es, "sg%d" % i, [128, 512], F32) for i in range(2)]
            hd = [sb(es, "hd%d" % i, [128, 512], BF16) for i in range(2)]
            bi = 0
            for j, S in enumerate(jobs):
                sc = SC[j]
                for blk in range(S // 512):
                    t0 = blk * 512
                    x1T = x1Ts[bi % 2]
                    bi += 1
                    kb.dma(x1T[:], sc["XT"].h.rearrange("(c p) s -> p c s", p=128)[:, :, t0:t0 + 512], r=[sc["XT"]], w=[x1T])
                    for f in range(NFC):
                        bg, bu = PS[2 * (f % 2)], PS[1 + 2 * (f % 2)]
                        for c in range(8):
                            kb.op("pe", lambda e, c=c, f=f, bg=bg: e.matmul(bg[:, :], lhsT=wg[:, c, f * 128:(f + 1) * 128], rhs=x1T[:, c, :],
                                                                            start=(c == 0), stop=(c == 7)), r=[wg, x1T], w=[bg])
                        for c in range(8):
                            kb.op("pe", lambda e, c=c, f=f, bu=bu: e.matmul(bu[:, :], lhsT=wu[:, c, f * 128:(f + 1) * 128], rhs=x1T[:, c, :],
                                                                            start=(c == 0), stop=(c == 7)), r=[wu, x1T], w=[bu])
                        sgt, hdt = sg[f % 2], hd[f % 2]
                        kb.op("act", lambda e, sgt=sgt, bg=bg: e.activation(out=sgt[:, :], in_=bg[:, :], func=AF.Silu), r=[bg], w=[sgt])
                        kb.op("dve", lambda e, hdt=hdt, sgt=sgt, bu=bu: e.tensor_tensor(out=hdt[:, :], in0=bu[:, :], in1=sgt[:, :], op=ALU.mult),
                              r=[bu, sgt], w=[hdt])
                        kb.dma(sc["HIDT"].h[f * 128:(f + 1) * 128, t0:t0 + 512], hdt[:, :], r=[hdt], w=[sc["HIDT"]])
            kb.barrier()
        with ExitStack() as es:
            std_psum(es)
            stg = [sb(es, "stg%d" % i, [128, 2048], F32) for i in range(2)]
            wd = sb(es, "wd", [128, NFC, D], BF16)
            load_w_bf16(es, wd, lambda c, n0, n1: wd[:, c, n0:n1], w_down[l], NFC, D, stg)
            rl = sb(es, "rl", [128, 2 * D], F32)
            kb.dma(rl[:], rep_l[l, :, 2 * D:4 * D], w=[rl])
            tmp = {"st6": sb(es, "st6", [128, 2, 6], F32), "mv": sb(es, "mv", [128, 2], F32), "rs": sb(es, "rs", [128, 1], F32)}
            HIDs = [sb(es, "HID%d" % i, [128, NFC, 512], BF16) for i in range(2)]
            x1f = [sb(es, "x1f%d" % i, [128, D], F32) for i in range(2)]
            z = [sb(es, "z%d" % i, [128, D], F32) for i in range(2)]
            of2 = [sb(es, "of2%d" % i, [128, D], F32) for i in range(2)]
            ob = [sb(es, "ob%d" % i, [128, D], BF16) for i in range(2)]
            xts = [sb(es, "xts%d" % i, [128, D], BF16) for i in range(2)]
            bi = 0
            for j, S in enumerate(jobs):
                sc = SC[j]
                X1, XNo = sc["XN"][1], sc["XN"][0]
                for blk in range(S // 512):
                    t0 = blk * 512
                    HID = HIDs[bi % 2]
                    bi += 1
                    kb.dma(HID[:], sc["HIDT"].h.rearrange("(f p) s -> p f s", p=128)[:, :, t0:t0 + 512], r=[sc["HIDT"]], w=[HID])
                    def mm_c(tt):
                        for nb in range(2):
                            for f in range(NFC):
                                kb.op("pe", lambda e, f=f, nb=nb, tt=tt: e.matmul(PS[nb + 2 * (tt % 2)][:, :], lhsT=HID[:, f, tt * 128:(tt + 1) * 128],
                                                                                   rhs=wd[:, f, nb * 512:(nb + 1) * 512],
                                                                                   start=(f == 0), stop=(f == NFC - 1)), r=[HID, wd], w=[PS[nb + 2 * (tt % 2)]])
                    mm_c(0)
                    for tt in range(4):
                        tk0 = t0 + tt * 128
                        zt, obt, x1, o2, xt_ = z[tt % 2], ob[tt % 2], x1f[tt % 2], of2[tt % 2], xts[tt % 2]
                        kb.dma(x1[:], X1.h[tk0:tk0 + 128, :], r=[X1], w=[x1])
                        kb.flush()
                        if tt + 1 < 4:
                            mm_c(tt + 1)
                        for nb in range(2):
                            kb.op("dve", lambda e, nb=nb, x1=x1, zt=zt, tt=tt: e.scalar_tensor_tensor(
                                out=zt[:, nb * 512:(nb + 1) * 512], in0=x1[:, nb * 512:(nb + 1) * 512], scalar=alpha,
                                in1=PS[nb + 2 * (tt % 2)][:, :], op0=ALU.mult, op1=ALU.add), r=[x1, PS[nb + 2 * (tt % 2)]], w=[zt])
                        layer_norm_tile(zt, rl, rl, 0, D, o2, obt, tmp)
                        if last:
                            kb.dma_later(yout[j].h[tk0:tk0 + 128, :], o2[:], r=[o2], w=[yout[j]])
                        else:
                            kb.dma_later(XNo.h[tk0:tk0 + 128, :], o2[:], r=[o2], w=[XNo])
                            transpose_to_xt(obt, xt_, sc["XT"], tk0)
            kb.barrier()
    kb.barrier()
    es_glob.close()
    return nc


def _tables(jobs, smax):
    half = 16
    inv = 1.0 / (10000.0 ** (np.arange(0, 32, 2, dtype=np.float32) / 32.0))
    ang = np.arange(smax, dtype=np.float32)[:, None] * inv[None, :].astype(np.float32)
    cos = np.cos(ang).astype(np.float32).T
    sin = np.sin(ang).astype(np.float32).T
    cos2 = np.concatenate([cos, cos], 0)
    sin2 = np.concatenate([-sin, sin], 0)
    pedge = np.zeros((len(jobs), 128, 2, 16), np.float32)
    for j, S in enumerate(jobs):
        for g, w in enumerate((2, 4, 8, 16)):
            c, p0 = g // 2, (g % 2) * 64
            for i in range(8):
                t = i
                cnt = min(t + w // 2, S) - max(t - w // 2, 0)
                pedge[j, p0:p0 + 64, c, i] = 1.0 / cnt
                t = S - 8 + i
                cnt = min(t + w // 2, S) - max(t - w // 2, 0)
                pedge[j, p0:p0 + 64, c, 8 + i] = 1.0 / cnt
    s_ = np.arange(128)[:, None]
    j_ = np.arange(128)[None, :]
    maskf = np.where(s_ <= j_, 0.0, BIG).astype(np.float32)
    maskb = np.where(s_ >= j_, 0.0, BIG).astype(np.float32)
    sel = np.zeros((4, 4, 128), np.float32)
    for h in range(4):
        sel[h, h, :] = 1.0
    return dict(cos2=np.ascontiguousarray(cos2), sin2=np.ascontiguousarray(sin2), pedge=pedge, maskf=maskf, maskb=maskb,
                sel=sel, ident=np.eye(128, dtype=np.float32))


def _common_inputs(inp, L, jobs, smax):
    f = lambda a: np.ascontiguousarray(np.asarray(a, dtype=np.float32))
    rep = lambda v: np.broadcast_to(f(v)[None, :], (128, f(v).shape[0]))
    m = {}
    for k in ("w_in", "w_uq", "w_ukv", "w_pool", "w_out", "w_gate", "w_up", "w_down"):
        m[k] = f(inp[k])[:L]
    m["rep_in"] = np.ascontiguousarray(np.concatenate([rep(inp["ln_in_g"]), rep(inp["ln_in_b"])], 1))
    m["rep_l"] = np.ascontiguousarray(np.stack([
        np.concatenate([rep(inp["ln1_g"][l]), rep(inp["ln1_b"][l]), rep(inp["ln2_g"][l]), rep(inp["ln2_b"][l]),
                        rep(inp["mlstm_norm_g"][l])], 1) for l in range(L)]))
    m["qg"] = np.ascontiguousarray(f(inp["q_norm_g"])[:L].reshape(L, 2, 128).transpose(0, 2, 1))
    m["kvg"] = np.ascontiguousarray(f(inp["kv_norm_g"])[:L].reshape(L, 128, 1))
    m["psc"] = np.ascontiguousarray(f(inp["pool_scale"])[:L].reshape(L, 4, 64).transpose(0, 2, 1))
    m["gbias"] = np.ascontiguousarray(f(inp["mlstm_gate_bias"])[:L].reshape(L, 4, 4).transpose(0, 2, 1))
    m.update(_tables(jobs, smax))
    return m


_CACHE = {}


def run(inp, L, job_inputs_per_core, jobs, dbg=False):
    smax = max(jobs)
    key = (L, tuple(jobs))
    if key not in _CACHE:
        _CACHE[key] = build_program(L, jobs, smax, dbg)
    nc = _CACHE[key]
    common = _common_inputs(inp, L, jobs, smax)
    in_maps = []
    for xs in job_inputs_per_core:
        m = dict(common)
        for j, x in enumerate(xs):
            m["xin%d" % j] = np.ascontiguousarray(x, dtype=np.float32)
        in_maps.append(m)
    res = run_bass_kernel_spmd(nc, in_maps, core_ids=list(range(len(in_maps))))
    if dbg:
        return [[r["yout%d" % j] for j in range(len(jobs))] + [r["dbg_mixt"]] for r in res.results]
    return [[r["yout%d" % j] for j in range(len(jobs))] for r in res.results]


def kernel(**inputs):
    xp = np.asarray(inputs["x_prompt"], dtype=np.float32)
    xs = np.asarray(inputs["x_sample"], dtype=np.float32)
    L = int(np.asarray(inputs["w_in"]).shape[0])
    jobs = [xp.shape[1], xs.shape[1], xs.shape[1]]
    per_core = [[xp[c % 2], xs[2 * c], xs[2 * c + 1]] for c in range(8)]
    outs = run(inputs, L, per_core, jobs)
    y_prompt = np.stack([outs[0][0], outs[1][0]], 0).astype(np.float32)
    y_sample = np.stack([outs[c][1 + i] for c in range(8) for i in range(2)], 0).astype(np.float32)
    return (y_prompt, y_sample)
```
